# Optimizing a Trainium2 kernel written in Bass

```python
import jax
import jax.numpy as jnp
from jax import lax
import numpy as np


D_MODEL = 2048
BATCH = 4
SEQ = 2048
DEPTH = 1

CTX_LEN = 256
GRID_W = 64
CHUNK = 64
RET_HEADS = 8
RET_DK = 256
RET_DV = 256
GLA_HEADS = 4
GLA_DK = 256
GLA_DV = 512
GLA_RANK = 16
GLA_TAU = 16.0
D_FF = ((8 * D_MODEL // 3 + 127) // 128) * 128
CONV_K = 3
ROPE_BASE = 10000.0
EPS = 1e-6
RET_QK = RET_HEADS * RET_DK
RET_V = RET_HEADS * RET_DV
GLA_QK = GLA_HEADS * GLA_DK
GLA_V = GLA_HEADS * GLA_DV
SPLIT_SIZES = (RET_QK, RET_QK, RET_V, RET_V, GLA_QK, GLA_QK, GLA_V, GLA_V, GLA_RANK, GLA_RANK, D_MODEL, D_MODEL)
N_IN = 2 * RET_QK + 2 * RET_V + 2 * GLA_QK + 2 * GLA_V + 2 * GLA_RANK + 2 * D_MODEL

kernel_name = 'hybrid_retention_gla_convffn_block'


def rmsnorm(t, g):
    tf = t.astype(jnp.float32)
    y = tf * lax.rsqrt(jnp.mean(tf * tf, axis=-1, keepdims=True) + EPS)
    return y.astype(t.dtype) * g


def split_cols(p):
    bounds = []
    s = 0
    for n in SPLIT_SIZES[:-1]:
        s += n
        bounds.append(s)
    return jnp.split(p, bounds, axis=-1)


def to_heads(t, n_heads):
    b, l, _ = t.shape
    return t.reshape(b, l, n_heads, -1).transpose(0, 2, 1, 3)


def from_heads(t):
    b, h, l, d = t.shape
    return t.transpose(0, 2, 1, 3).reshape(b, l, h * d)


def to_chunks(t):
    b, h, l, d = t.shape
    return t.reshape(b, h, l // CHUNK, CHUNK, d).transpose(2, 0, 1, 3, 4)


def from_chunks(t):
    n, b, h, c, d = t.shape
    return t.transpose(1, 2, 0, 3, 4).reshape(b, h, n * c, d)


def flip(t):
    return jnp.flip(t, axis=2)


def rope_2d(t):
    l = t.shape[2]
    pos = jnp.arange(l)
    row = (pos // GRID_W).astype(jnp.float32)
    col = (pos % GRID_W).astype(jnp.float32)
    n_freq = t.shape[-1] // 4
    inv = ROPE_BASE ** (-jnp.arange(n_freq, dtype=jnp.float32) / n_freq)
    ang = jnp.concatenate([row[:, None] * inv, col[:, None] * inv], axis=-1)
    cos = jnp.concatenate([jnp.cos(ang), jnp.cos(ang)], axis=-1).astype(t.dtype)
    sin = jnp.concatenate([jnp.sin(ang), jnp.sin(ang)], axis=-1).astype(t.dtype)
    t1, t2 = jnp.split(t, 2, axis=-1)
    return t * cos + jnp.concatenate([-t2, t1], axis=-1) * sin


def retention_scan(q, k, v, log_g, s0):
    idx = jnp.arange(CHUNK, dtype=jnp.float32)
    mask = idx[:, None] >= idx[None, :]
    dmat = jnp.exp(jnp.where(mask[None], (idx[:, None] - idx[None, :])[None] * log_g[:, None, None], -jnp.inf))
    q_dec = jnp.exp((idx + 1.0)[None, :] * log_g[:, None])[None, :, :, None]
    k_dec = jnp.exp((CHUNK - 1.0 - idx)[None, :] * log_g[:, None])[None, :, :, None]
    c_dec = jnp.exp(CHUNK * log_g)[None, :, None, None]

    def step(s, xs):
        qc, kc, vc = xs
        inter = jnp.einsum('bhcd,bhde->bhce', qc * q_dec, s)
        scores = jnp.einsum('bhnd,bhmd->bhnm', qc, kc) * dmat[None]
        intra = jnp.einsum('bhnm,bhme->bhne', scores, vc)
        s_new = s * c_dec + jnp.einsum('bhcd,bhce->bhde', kc * k_dec, vc)
        return s_new, inter + intra

    s_fin, o = lax.scan(step, s0, (to_chunks(q), to_chunks(k), to_chunks(v)))
    return from_chunks(o), s_fin


def gla_scan(q, k, v, log_a, s0):
    idx = jnp.arange(CHUNK)
    mask = (idx[:, None] >= idx[None, :])[:, :, None]

    def step(s, xs):
        qc, kc, vc, ac = xs
        b = jnp.cumsum(ac, axis=-2)
        inter = jnp.einsum('bhcd,bhde->bhce', qc * jnp.exp(b), s)
        diff = b[:, :, :, None, :] - b[:, :, None, :, :]
        decay = jnp.exp(jnp.where(mask, diff, -jnp.inf))
        scores = jnp.einsum('bhnd,bhmd,bhnmd->bhnm', qc, kc, decay)
        intra = jnp.einsum('bhnm,bhme->bhne', scores, vc)
        b_last = b[:, :, -1:, :]
        s_new = jnp.exp(b_last[:, :, 0, :])[..., None] * s + jnp.einsum('bhcd,bhce->bhde', kc * jnp.exp(b_last - b), vc)
        return s_new, inter + intra

    s_fin, o = lax.scan(step, s0, (to_chunks(q), to_chunks(k), to_chunks(v), to_chunks(log_a)))
    return from_chunks(o), s_fin


def retention_readout(o, gate, g):
    mu = jnp.mean(o, axis=-1, keepdims=True)
    var = jnp.mean(jnp.square(o - mu), axis=-1, keepdims=True)
    n = from_heads((o - mu) * lax.rsqrt(var + EPS)).astype(gate.dtype)
    return n * g * jax.nn.silu(gate)


def gla_readout(o, gate, g):
    n = from_heads(o * lax.rsqrt(jnp.mean(o * o, axis=-1, keepdims=True) + EPS)).astype(gate.dtype)
    return n * g * jax.nn.silu(gate)


def token_mixer(h, hc, w_in, decay_logit, gate_up, gate_bias, ret_g, gla_g, w_rp, w_gp, w_o, want_ctx):
    bsz = h.shape[0]
    log_gamma = jax.nn.log_sigmoid(decay_logit.astype(jnp.float32))

    def prepare(p, latent):
        rq, rk, rv, rg, gq, gk, gv, gg, gaf, gab, mr, mg = split_cols(p)
        rq = to_heads(rq, RET_HEADS)
        rk = to_heads(rk, RET_HEADS) * RET_DK ** -0.5
        if latent:
            rq = rope_2d(rq)
            rk = rope_2d(rk)
        la_f = to_heads(jax.nn.log_sigmoid((gaf @ gate_up[0] + gate_bias[0]).astype(jnp.float32)) / GLA_TAU, GLA_HEADS)
        la_b = to_heads(jax.nn.log_sigmoid((gab @ gate_up[1] + gate_bias[1]).astype(jnp.float32)) / GLA_TAU, GLA_HEADS)
        return (rq, rk, to_heads(rv, RET_HEADS), rg, to_heads(gq, GLA_HEADS),
                to_heads(gk, GLA_HEADS) * GLA_DK ** -0.5, to_heads(gv, GLA_HEADS), gg, la_f, la_b, mr, mg)

    rq_l, rk_l, rv_l, rg_l, gq_l, gk_l, gv_l, gg_l, af_l, ab_l, mr_l, mg_l = prepare(h @ w_in, True)
    rq_c, rk_c, rv_c, rg_c, gq_c, gk_c, gv_c, gg_c, af_c, ab_c, mr_c, mg_c = prepare(hc @ w_in, False)
    s0_ret = jnp.zeros((bsz, RET_HEADS, RET_DK, RET_DV), jnp.float32)
    s0_gla = jnp.zeros((bsz, GLA_HEADS, GLA_DK, GLA_DV), jnp.float32)

    ro_cf, s_rf = retention_scan(rq_c, rk_c, rv_c, log_gamma[0], s0_ret)
    ro_cb, s_rb = retention_scan(flip(rq_c), flip(rk_c), flip(rv_c), log_gamma[1], s0_ret)
    ro_l = (retention_scan(rq_l, rk_l, rv_l, log_gamma[0], s_rf)[0]
            + flip(retention_scan(flip(rq_l), flip(rk_l), flip(rv_l), log_gamma[1], s_rb)[0]))

    go_cf, s_gf = gla_scan(gq_c, gk_c, gv_c, af_c, s0_gla)
    go_cb, s_gb = gla_scan(flip(gq_c), flip(gk_c), flip(gv_c), flip(ab_c), s0_gla)
    go_l = (gla_scan(gq_l, gk_l, gv_l, af_l, s_gf)[0]
            + flip(gla_scan(flip(gq_l), flip(gk_l), flip(gv_l), flip(ab_l), s_gb)[0]))

    def merge(ro, rg, go, gg, mr, mg):
        y_r = retention_readout(ro, rg, ret_g) @ w_rp
        y_g = gla_readout(go, gg, gla_g) @ w_gp
        return (jax.nn.sigmoid(mr) * y_r + jax.nn.sigmoid(mg) * y_g) @ w_o

    y = merge(ro_l, rg_l, go_l, gg_l, mr_l, mg_l)
    yc = merge(ro_cf + flip(ro_cb), rg_c, go_cf + flip(go_cb), gg_c, mr_c, mg_c) if want_ctx else None
    return y, yc


def conv_ffn(h, w_up, conv_w, conv_b, w_down, rows, cols):
    bsz, l, _ = h.shape
    a, v = jnp.split(h @ w_up, 2, axis=-1)
    a = a.reshape(bsz, rows, cols, D_FF)
    a = lax.conv_general_dilated(a, conv_w[:, :, None, :], window_strides=(1, 1), padding='SAME',
                                 dimension_numbers=('NHWC', 'HWIO', 'NHWC'), feature_group_count=D_FF)
    a = jax.nn.gelu(a.reshape(bsz, l, D_FF) + conv_b)
    return (a * v) @ w_down


def setup_inputs(seed: int = 0) -> dict:
    key = jax.random.key(seed)
    ks = jax.random.split(key, 24)

    def nrm(k, shape, s):
        return jax.random.normal(k, shape, jnp.float32) * s

    ret_init = jnp.log(2.0 ** jnp.arange(5, 5 + RET_HEADS, dtype=jnp.float32) - 1.0)
    return {
        'x': nrm(ks[0], (BATCH, SEQ, D_MODEL), 1.0),
        'c': nrm(ks[1], (BATCH, D_MODEL), 1.0),
        'ctx': nrm(ks[2], (BATCH, CTX_LEN, D_MODEL), 1.0),
        'c_ctx': nrm(ks[3], (D_MODEL,), 1.0),
        'w_ada': nrm(ks[4], (DEPTH, D_MODEL, 6 * D_MODEL), D_MODEL ** -0.5),
        'b_ada': nrm(ks[5], (DEPTH, 6 * D_MODEL), 0.01),
        'norm1_g': 1.0 + nrm(ks[6], (DEPTH, D_MODEL), 0.02),
        'w_in': nrm(ks[7], (DEPTH, D_MODEL, N_IN), D_MODEL ** -0.5),
        'ret_decay_logit': ret_init[None, None, :] + nrm(ks[8], (DEPTH, 2, RET_HEADS), 0.01),
        'gla_gate_up': nrm(ks[9], (DEPTH, 2, GLA_RANK, GLA_QK), GLA_RANK ** -0.5),
        'gla_gate_bias': nrm(ks[10], (DEPTH, 2, GLA_QK), 0.1),
        'ret_norm_g': 1.0 + nrm(ks[11], (DEPTH, RET_V), 0.02),
        'gla_norm_g': 1.0 + nrm(ks[12], (DEPTH, GLA_V), 0.02),
        'w_ret_proj': nrm(ks[13], (DEPTH, RET_V, D_MODEL), RET_V ** -0.5),
        'w_gla_proj': nrm(ks[14], (DEPTH, GLA_V, D_MODEL), GLA_V ** -0.5),
        'w_out': nrm(ks[15], (DEPTH, D_MODEL, D_MODEL), D_MODEL ** -0.5),
        'norm2_g': 1.0 + nrm(ks[16], (DEPTH, D_MODEL), 0.02),
        'w_up': nrm(ks[17], (DEPTH, D_MODEL, 2 * D_FF), D_MODEL ** -0.5),
        'conv_w': nrm(ks[18], (DEPTH, CONV_K, CONV_K, D_FF), 1.0 / CONV_K),
        'conv_b': nrm(ks[19], (DEPTH, D_FF), 0.01),
        'w_down': nrm(ks[20], (DEPTH, D_FF, D_MODEL), D_FF ** -0.5),
        'final_g': 1.0 + nrm(ks[21], (D_MODEL,), 0.02),
    }


def reference(x, c, ctx, c_ctx, w_ada, b_ada, norm1_g, w_in, ret_decay_logit, gla_gate_up, gla_gate_bias,
              ret_norm_g, gla_norm_g, w_ret_proj, w_gla_proj, w_out, norm2_g, w_up, conv_w, conv_b, w_down, final_g):
    rows = x.shape[1] // GRID_W
    ctx_len = ctx.shape[1]
    for l in range(DEPTH):
        last = l == DEPTH - 1
        mod = (jax.nn.silu(c) @ w_ada[l] + b_ada[l])[:, None, :]
        mod_c = jax.nn.silu(c_ctx) @ w_ada[l] + b_ada[l]
        sh1, sc1, g1, sh2, sc2, g2 = jnp.split(mod, 6, axis=-1)
        csh1, csc1, cg1, csh2, csc2, cg2 = jnp.split(mod_c, 6, axis=-1)

        h = rmsnorm(x, norm1_g[l]) * (1.0 + sc1) + sh1
        hc = rmsnorm(ctx, norm1_g[l]) * (1.0 + csc1) + csh1
        y, yc = token_mixer(h, hc, w_in[l], ret_decay_logit[l], gla_gate_up[l], gla_gate_bias[l],
                            ret_norm_g[l], gla_norm_g[l], w_ret_proj[l], w_gla_proj[l], w_out[l], not last)
        x = x + g1 * y

        h = rmsnorm(x, norm2_g[l]) * (1.0 + sc2) + sh2
        x = x + g2 * conv_ffn(h, w_up[l], conv_w[l], conv_b[l], w_down[l], rows, GRID_W)

        if not last:
            ctx = ctx + cg1 * yc
            hc = rmsnorm(ctx, norm2_g[l]) * (1.0 + csc2) + csh2
            ctx = ctx + cg2 * conv_ffn(hc, w_up[l], conv_w[l], conv_b[l], w_down[l], 1, ctx_len)
    return rmsnorm(x, final_g)
```

```python
import numpy as np
from contextlib import ExitStack
import concourse.bass as bass
import concourse.mybir as mybir
from concourse.bass_utils import run_bass_kernel_spmd

F32 = mybir.dt.float32
BF16 = mybir.dt.bfloat16
AF = mybir.ActivationFunctionType
ALU = mybir.AluOpType
AX = mybir.AxisListType

D = 2048
KC = 16
L = 2048
CTX = 256
NT = 9
TOK = NT * 128
TOKA = 2304
RH, GH = 8, 4
DFF = 5504
NCC = 43
EPS = 1e-6
EPS_RO = 256.0 * EPS
STOP_AFTER = None
DBG_OF_HEAD = None
NO_INTERLEAVE = False
EXP1 = False
EXP2 = True


class _Stop(Exception):
    pass


class Region:
    __slots__ = ("name", "w", "r", "excl")

    def __init__(self, name, excl=False):
        self.name = name
        self.w = {}
        self.r = {}
        self.excl = excl


class Prog:
    ENG = ("pe", "act", "dve", "pool", "sp")

    def __init__(self):
        self.ops = {e: [] for e in self.ENG}
        self.cnt = {}
        self.known = {e: {} for e in self.ENG}

    def _need(self, eng, reads, writes, is_dma=False):
        need = {}

        def add(k, v):
            if v > need.get(k, 0):
                need[k] = v
        for rg in reads:
            for k, v in rg.w.items():
                if k == eng and eng == "pe":
                    continue
                add(k, v)
            if rg.excl:
                for k, v in rg.r.items():
                    if k != eng or is_dma:
                        add(k, v)
        for rg in writes:
            for k, v in rg.w.items():
                if k != eng or is_dma:
                    add(k, v)
            for k, v in rg.r.items():
                if k != eng or is_dma:
                    add(k, v)
        out = []
        kn = self.known[eng]
        for k, v in need.items():
            if kn.get(k, 0) < v:
                kn[k] = v
                out.append((k, v))
        return out

    def _record(self, key, val, reads, writes):
        for rg in reads:
            rg.r[key] = val
        for rg in writes:
            rg.w[key] = val

    def emit(self, eng, fn, reads=(), writes=(), inc=True, wait=True):
        w = self._need(eng, reads, writes) if wait else []
        key = None
        if inc:
            self.cnt[eng] = self.cnt.get(eng, 0) + 1
            key = eng
            self._record(eng, self.cnt[eng], reads, writes)
        self.ops[eng].append((w, fn, key, 1))

    def mm(self, out, pairs, reads, writes):
        n = len(pairs)
        for i, (l, r) in enumerate(pairs):
            def fn(t, l=l, r=r, st=(i == 0), sp=(i == n - 1)):
                return t.matmul(out, lhsT=l, rhs=r, start=st, stop=sp)
            self.emit("pe", fn, reads, writes, inc=(i == n - 1), wait=(i == 0))

    def transpose(self, out, in_, ident, reads, writes):
        self.emit("pe", lambda t: t.transpose(out, in_, ident), reads, writes)

    def dma(self, eng, chan, out, in_, reads=(), writes=()):
        w = self._need(eng, reads, writes, is_dma=True)
        self.cnt[chan] = self.cnt.get(chan, 0) + 16
        self._record(chan, self.cnt[chan], reads, writes)
        self.ops[eng].append((w, lambda e: e.dma_start(out=out, in_=in_), chan, 16))

    def barrier(self):
        for eng in self.ENG:
            kn = self.known[eng]
            waits = []
            for k, v in self.cnt.items():
                if k == eng and eng == "pe":
                    continue
                if kn.get(k, 0) < v:
                    kn[k] = v
                    waits.append((k, v))
            if waits:
                self.ops[eng].append((waits, None, None, 0))

    def replay(self, block, sems, final_waits):
        hmap = {"pe": block.tensor, "act": block.scalar, "dve": block.vector, "pool": block.gpsimd, "sp": block.sync}
        for eng in self.ENG:
            ops = self.ops[eng]
            extra = final_waits if eng == "sp" else ()

            def body(h, ops=ops, extra=extra):
                for waits, fn, key, amt in ops:
                    for k, v in waits:
                        h.wait_ge(sems[k], v)
                    if fn is None:
                        continue
                    ins = fn(h)
                    if key is not None:
                        ins.then_inc(sems[key], amt)
                for k, v in extra:
                    h.wait_ge(sems[k], v)
            hmap[eng](body)


class Arena:
    def __init__(self, ap_f32, nwords):
        self.ap = ap_f32
        self.n = nwords
        self.off = 0

    def reset(self, off=0):
        self.off = off

    def f32(self, words):
        assert self.off + words <= self.n, (self.off, words, self.n)
        v = self.ap[:, self.off:self.off + words]
        self.off += words
        return v

    def bf16(self, elems):
        assert elems % 2 == 0
        return self.f32(elems // 2).bitcast(BF16)


def _r3(ap, a, b):
    return ap.rearrange("p (a b) -> p a b", a=a, b=b)


def build_program(dbg=None):
    nc = bass.Bass("TRN2", target_bir_lowering=False)
    P = Prog()
    dt_in = {}

    def din(name, shape, dt=F32):
        dt_in[name] = nc.dram_tensor(name, list(shape), dt, kind="ExternalInput").ap()
        return dt_in[name]

    x_d = din("x", [L, D])
    ctx_d = din("ctx", [CTX, D])
    cv_d = din("cv", [128, KC * 2])
    wada_d = din("wada", [24, 128, KC * 512])
    vec_d = din("vecs", [128, 96 + 48])
    conv_d = din("convv", [128, NCC * 10])
    cst_d = din("consts", [128, 128 * 3 + 512])
    rope_d = din("rope", [128, 2 * L])
    dl_d = din("dlog", [1, 16])
    gu_d = din("gup", [17, 2 * 1024])
    rg_d = din("rng", [1, 2048])
    gg_d = din("gng", [1, 2048])
    wr_d = din("w_ret", [RH * 4, 128, KC * 256])
    wg_d = din("w_gla", [GH * 6, 128, KC * 256])
    wlr_d = din("w_lr", [128, KC * 32])
    wmg_d = din("w_mg", [16, 128, KC * 256])
    wrp_d = din("w_rp", [16, 128, KC * 128])
    wgp_d = din("w_gp", [16, 128, KC * 128])
    wo_d = din("w_o", [16, 128, KC * 128])
    wup_d = din("w_up", [NCC, 128, KC * 256])
    wdn_d = din("w_dn", [NCC, 128, D])
    out_d = nc.dram_tensor("out", [1024, D], F32, kind="ExternalOutput").ap()
    R_d = nc.dram_tensor("r_scr", [TOK, 4096], BF16).ap()
    ST_d = nc.dram_tensor("st_scr", [12 * 2, 128, 1024], F32).ap()
    dbg_d = {}
    if dbg:
        for name, shape in dbg.items():
            dbg_d[name] = nc.dram_tensor("dbg_" + name, list(shape), F32, kind="ExternalOutput").ap()

    with ExitStack() as es:
        def sb(name, shape, dt=F32):
            return es.enter_context(nc.sbuf_tensor(name, list(shape), dt))
        HT_t = sb("HT", [128, 18432], F32)
        BIG_t = sb("BIG", [128, 18432], F32)
        WB_t = sb("WB", [128, 8192], F32)
        GA_t = sb("GA", [128, 3328], F32)
        MISC_t = sb("MISC", [128, 2560], F32)
        ident_f = sb("ident_f", [128, 128])
        cmask = sb("cmask", [128, 256 + 512])
        ident_b = sb("ident_b", [128, 128], BF16)
        mf_b = sb("mf_b", [128, 128], BF16)
        mb_b = sb("mb_b", [128, 128], BF16)
        ones_b = sb("ones_b", [128, 128], BF16)
        modT = sb("modT", [128, 192])
        vecs = sb("vecs_s", [128, 144])
        A1 = sb("A1", [128, 48])
        convv = sb("convv_s", [128, NCC * 10])
        lg = sb("lg", [128, 64])
        cvs = sb("cvs", [128, 64])
        PS = [es.enter_context(nc.psum_tensor("ps%d" % i, [128, 512], F32)) for i in range(8)]
        psr = [Region("ps%d" % i, excl=True) for i in range(8)]
        PSb = [p[:, :].bitcast(BF16) for p in PS]

        HT = HT_t[:, :]
        BIG = BIG_t[:, :]
        hT = HT.bitcast(BF16)
        hTo3 = _r3(hT[:, 0:18432], KC, TOK)
        hTx3 = _r3(hT[:, 18432:36864], KC, TOK)

        def hcol(k, c0, n):
            if c0 < TOK:
                assert c0 + n <= TOK
                return hTo3[:, k, c0:c0 + n]
            return hTx3[:, k, c0 - TOK:c0 - TOK + n]
        hT_reg = [Region("hT%d" % t) for t in range(18)]
        consts_reg = Region("consts")
        mod_reg = Region("mod")

        P.dma("sp", "ld", ident_f[:, :], cst_d[:, 0:128], writes=[consts_reg])
        P.dma("sp", "ld", cmask[:, :], cst_d[:, 128:896], writes=[consts_reg])
        P.dma("sp", "ld", vecs[:, :], vec_d[:, :], writes=[consts_reg])
        P.dma("sp", "ld", convv[:, :], conv_d[:, :], writes=[consts_reg])
        P.dma("sp", "ld", cvs[:, 0:32], cv_d[:, :], writes=[consts_reg])
        P.dma("sp", "ld", lg[:, 0:16], dl_d[0:1, :].partition_broadcast(128), writes=[consts_reg])
        c2 = Region("c2")
        P.emit("dve", lambda v: v.tensor_copy(ident_b[:, :], ident_f[:, :]), [consts_reg], [c2])
        P.emit("dve", lambda v: v.tensor_copy(mf_b[:, :], cmask[:, 0:128]), [consts_reg], [c2])
        P.emit("dve", lambda v: v.tensor_copy(mb_b[:, :], cmask[:, 128:256]), [consts_reg], [c2])
        P.emit("dve", lambda v: v.memset(ones_b[:, :], 1.0), [], [c2])
        epsc = lg[:, 48:56]
        P.emit("dve", lambda v: v.memset(epsc[:, 0:1], EPS), [], [c2])
        P.emit("dve", lambda v: v.memset(epsc[:, 1:2], EPS_RO), [], [c2])
        P.emit("dve", lambda v: v.memset(epsc[:, 2:3], 1.0), [], [c2])
        iotas = _r3(cmask[:, 256:768], 4, 128)
        bada = vecs[:, 0:96]
        n1g = vecs[:, 96:112]
        n2g = vecs[:, 112:128]
        fgv = vecs[:, 128:144]
        ONE = epsc[:, 2:3]

        def rstd_act(out, in_, scale, eps_ap, reads, writes):
            P.emit("act", lambda s: s.activation(out=out, in_=in_, func=AF.Ln, scale=scale, bias=eps_ap), reads + [c2], writes)
            P.emit("act", lambda s: s.activation(out=out, in_=out, func=AF.Exp, scale=-0.5), writes, writes)

        cs_b = MISC_t[:, 0:16].bitcast(BF16)
        cs_reg = Region("cs")
        P.emit("act", lambda s: s.activation(out=cs_b, in_=cvs[:, 0:32], func=AF.Silu), [consts_reg], [cs_reg])
        cs3 = _r3(cs_b, KC, 2)
        wb_reg = [Region("wb%d" % i) for i in range(4)]
        WBb = WB_t[:, :].bitcast(BF16)
        modT3 = _r3(modT[:, :], 96, 2)
        for g in range(24):
            bi = g % 2
            wv = WBb[:, bi * 8192:(bi + 1) * 8192]
            wregs = wb_reg[bi * 2:bi * 2 + 2]
            P.dma("pool", "w", wv, wada_d[g], writes=wregs)
            w4 = wv.rearrange("p (j k c) -> p j k c", j=4, k=KC, c=128)
            pb = g % 2
            for jj in range(4):
                P.mm(PS[pb][:, jj * 2:jj * 2 + 2],
                     [(w4[:, jj, k, :], cs3[:, k, :]) for k in range(KC)],
                     wregs + [cs_reg], [psr[pb]])
            P.emit("dve", lambda v, g=g, pb=pb: v.tensor_tensor(
                out=modT3[:, g * 4:g * 4 + 4, :], in0=_r3(PS[pb][:, 0:8], 4, 2),
                in1=bada[:, g * 4:g * 4 + 4].unsqueeze(2).to_broadcast([128, 4, 2]), op=ALU.add),
                [psr[pb], consts_reg], [mod_reg])
        sh1 = modT3[:, 0:16, 0]
        sc1 = modT3[:, 16:32, 0]
        g1 = modT3[:, 32:48, 0]
        sh2 = modT3[:, 48:64, 0]
        sc2 = modT3[:, 64:80, 0]
        g2 = modT3[:, 80:96, 0]
        csh1 = modT3[:, 0:16, 1]
        csc1 = modT3[:, 16:32, 1]
        a_reg = Region("A")
        P.emit("dve", lambda v: v.scalar_tensor_tensor(out=A1[:, 0:16], in0=sc1, scalar=1.0, in1=n1g, op0=ALU.add, op1=ALU.mult),
               [mod_reg, consts_reg], [a_reg])
        P.emit("dve", lambda v: v.scalar_tensor_tensor(out=A1[:, 16:32], in0=csc1, scalar=1.0, in1=n1g, op0=ALU.add, op1=ALU.mult),
               [mod_reg, consts_reg], [a_reg])
        P.emit("dve", lambda v: v.scalar_tensor_tensor(out=A1[:, 32:48], in0=sc2, scalar=1.0, in1=n2g, op0=ALU.add, op1=ALU.mult),
               [mod_reg, consts_reg], [a_reg])
        A2 = A1[:, 32:48]

        bigA = Arena(BIG, 18432)
        xt = [bigA.f32(2048) for _ in range(2)]
        xs = [bigA.f32(2048) for _ in range(2)]
        xt_reg = [Region("xt%d" % i) for i in range(2)]
        xs_reg = [Region("xs%d" % i) for i in range(2)]
        junk = bigA.f32(2048)
        junk_reg = Region("junk")
        st_small = MISC_t[:, 64:128]
        st_reg = [Region("st%d" % i) for i in range(2)]
        for t in range(18):
            bi = t % 2
            src = x_d[t * 128:(t + 1) * 128, :] if t < 16 else ctx_d[(t - 16) * 128:(t - 15) * 128, :]
            P.dma("sp", "ld", xt[bi], src, writes=[xt_reg[bi]])
            ssq = st_small[:, bi * 4:bi * 4 + 1]
            rstd = st_small[:, bi * 4 + 1:bi * 4 + 2]
            P.emit("dve", lambda v, ssq=ssq: v.memset(ssq, 0.0), [], [st_reg[bi]])
            P.emit("act", lambda s, bi=bi, ssq=ssq: s.activation(out=junk, in_=xt[bi], func=AF.Square, accum_out=ssq),
                   [xt_reg[bi]], [junk_reg, st_reg[bi]])
            rstd_act(rstd, ssq, 1.0 / D, epsc[:, 0:1], [st_reg[bi]], [st_reg[bi]])
            P.emit("dve", lambda v, bi=bi, rstd=rstd: v.tensor_scalar(out=xs[bi], in0=xt[bi], scalar1=rstd, scalar2=None, op0=ALU.mult),
                   [xt_reg[bi], st_reg[bi]], [xs_reg[bi]])
            Acol = A1[:, 0:16] if t < 16 else A1[:, 16:32]
            Bcol = sh1 if t < 16 else csh1
            for q in range(4):
                pb = 2 + (t * 4 + q) % 4
                for kk in range(4):
                    k = q * 4 + kk
                    P.transpose(PS[pb][:, kk * 128:(kk + 1) * 128], xs[bi][:, k * 128:(k + 1) * 128], ident_f[:, :],
                                [xs_reg[bi], consts_reg], [psr[pb]])
                for kk in range(4):
                    k = q * 4 + kk
                    dst = hcol(k, t * 128, 128)
                    if kk % 2 == 0:
                        P.emit("act", lambda s, pb=pb, kk=kk, k=k, dst=dst, Acol=Acol, Bcol=Bcol: s.activation(
                            out=dst, in_=PS[pb][:, kk * 128:(kk + 1) * 128], func=AF.Identity,
                            scale=Acol[:, k:k + 1], bias=Bcol[:, k:k + 1]), [psr[pb], a_reg, mod_reg], [hT_reg[t]])
                    else:
                        P.emit("dve", lambda v, pb=pb, kk=kk, k=k, dst=dst, Acol=Acol, Bcol=Bcol: v.tensor_scalar(
                            out=dst, in0=PS[pb][:, kk * 128:(kk + 1) * 128], scalar1=Acol[:, k:k + 1], scalar2=Bcol[:, k:k + 1],
                            op0=ALU.mult, op1=ALU.add), [psr[pb], a_reg, mod_reg], [hT_reg[t]])

        def finish():
            final_waits = [("out", P.cnt.get("out", 0))]
            sem_keys = sorted(P.cnt.keys())
            sems = {k: es.enter_context(nc.semaphore("s_" + k)) for k in sem_keys}
            with nc.Block() as block:
                P.replay(block, sems, final_waits)

        def dump_f32(name, ap, n):
            P.barrier()
            stage = BIG[:, 0:n] if ap.dtype != F32 else None
            if stage is not None:
                rg = Region("dump")
                P.emit("dve", lambda v: v.tensor_copy(stage, ap), [], [rg])
                P.dma("sp", "out", dbg_d[name][:, :], stage, reads=[rg])
            else:
                P.dma("sp", "out", dbg_d[name][:, :], ap)
            P.barrier()

        if STOP_AFTER == "B":
            P.barrier()
            dq = bigA.f32(2304)
            dq_reg = Region("dq")
            for k in range(KC):
                for half, src3 in enumerate((hTo3, hTx3)):
                    P.emit("dve", lambda v, k=k, src3=src3, half=half: v.tensor_copy(dq[:, half * TOK:(half + 1) * TOK], src3[:, k, :]),
                           hT_reg, [dq_reg])
                P.dma("sp", "out", dbg_d["hT"][:, k * TOKA:(k + 1) * TOKA], dq, reads=[dq_reg])
            P.dma("sp", "out", dbg_d["modT"][:, :], modT[:, :], reads=[mod_reg])
            finish()
            return nc

        wb_i = [0]

        def load_w(src, nbf=4096, nslots=4):
            i = wb_i[0] % nslots
            wb_i[0] += 1
            v = WBb[:, i * 4096:i * 4096 + nbf]
            P.dma("pool", "w", v, src, writes=[wb_reg[i]])
            return v, wb_reg[i]
        ip_rot = [0]

        def ipb():
            b = ip_rot[0] % 3
            ip_rot[0] += 1
            return b

        def rope_evac(pb0, pb1, n, dst0, dst1, cosv, sinv, t1, t2, treg, tabreg, dreg):
            p0 = PS[pb0][:, 0:n]
            p1 = PS[pb1][:, 0:n]
            P.emit("dve", lambda v: v.tensor_tensor(out=t1, in0=p0, in1=cosv, op=ALU.mult), [psr[pb0], tabreg], [treg])
            P.emit("dve", lambda v: v.tensor_tensor(out=t2, in0=p1, in1=sinv, op=ALU.mult), [psr[pb1], tabreg], [treg])
            P.emit("dve", lambda v: v.tensor_tensor(out=dst0, in0=t1, in1=t2, op=ALU.subtract), [treg], [dreg])
            P.emit("dve", lambda v: v.tensor_tensor(out=t1, in0=p1, in1=cosv, op=ALU.mult), [psr[pb1], tabreg, treg], [treg])
            P.emit("dve", lambda v: v.tensor_tensor(out=t2, in0=p0, in1=sinv, op=ALU.mult), [psr[pb0], tabreg, treg], [treg])
            P.emit("dve", lambda v: v.tensor_tensor(out=dst1, in0=t1, in1=t2, op=ALU.add), [treg], [dreg])

        GAb = GA_t[:, :].bitcast(BF16)
        gaT3 = _r3(GAb[:, 0:4608], 2, TOKA)
        guT = GAb[:, 4608:6656]
        ga_reg = Region("gaT")
        gu_reg = Region("guT")

        def gla_decay(T, h, tile, dr, need_q, zb=4):
            E, Lb, ek, eb, enb, nb, sdec, dreg, elreg = T["E"], T["L"], T["ek"], T["eb"], T["enb"], T["nb"], T["sdec"], T["dreg"], T["elreg"]
            for c in range(2):
                col = dr * 1024 + h * 256 + c * 128
                P.mm(PS[zb][:, c * 128:(c + 1) * 128],
                     [(guT[0:17, col:col + 128], gaT3[0:17, dr, tile * 128:(tile + 1) * 128])], [gu_reg, ga_reg], [psr[zb]])
            P.emit("act", lambda s: s.activation(out=E, in_=PS[zb][:, 0:256], func=AF.Exp, scale=-1.0), [psr[zb]], [elreg])
            P.emit("act", lambda s: s.activation(out=E, in_=E, func=AF.Ln, bias=ONE), [elreg, c2], [elreg])
            E3 = _r3(E, 2, 128)
            L3 = _r3(Lb, 2, 128)
            for c in range(2):
                if dr == 0:
                    P.emit("dve", lambda v, c=c: v.tensor_tensor_scan(L3[:, c, :], E3[:, c, :], E3[:, c, :], 0.0, ALU.add, ALU.bypass),
                           [elreg], [elreg])
                else:
                    P.emit("dve", lambda v, c=c: v.tensor_tensor_scan(L3[:, c, ::-1], E3[:, c, ::-1], E3[:, c, ::-1], 0.0, ALU.add, ALU.bypass),
                           [elreg], [elreg])
            li = 127 if dr == 0 else 0
            Llast = L3[:, :, li:li + 1]
            P.emit("dve", lambda v: v.tensor_scalar(out=nb.unsqueeze(2), in0=Llast, scalar1=-1.0 / 16, scalar2=None, op0=ALU.mult), [elreg], [dreg])
            ek3 = _r3(ek, 2, 128)
            for c in range(2):
                P.emit("act", lambda s, c=c: s.activation(out=ek3[:, c, :], in_=L3[:, c, :], func=AF.Exp, scale=1.0 / 16, bias=nb[:, c:c + 1]),
                       [elreg, dreg], [dreg])
            P.emit("act", lambda s: s.activation(out=sdec.unsqueeze(2), in_=Llast, func=AF.Exp, scale=-1.0 / 16), [elreg], [dreg])
            if need_q:
                P.emit("act", lambda s: s.activation(out=eb, in_=Lb, func=AF.Exp, scale=-1.0 / 16), [elreg], [dreg])
                P.emit("act", lambda s: s.activation(out=enb, in_=Lb, func=AF.Exp, scale=1.0 / 16), [elreg], [dreg])
            return ek3, [sdec[:, 0:1], sdec[:, 1:2]]

        def kprep(T, kT_tile3, kreg, ek3, dcy_regs, cb, keng="dve"):
            khT, kh, kreg2 = T["khT"], T["kh"], T["khreg"]
            khT3 = _r3(khT, 2, 128)
            P.emit(keng, lambda e: e.tensor_tensor(out=khT3, in0=kT_tile3, in1=ek3, op=ALU.mult), [kreg] + dcy_regs, [kreg2])
            for c in range(2):
                P.transpose(PSb[cb][:, 768 + c * 128:768 + (c + 1) * 128], khT3[:, c, :], ident_b[:, :], [kreg2, c2], [psr[cb]])
            P.emit("act", lambda s: s.activation(out=kh, in_=PSb[cb][:, 768:1024], func=AF.Copy), [psr[cb]], [kreg2])

        def supdate(T, v_tile, vreg, sdecs, dcy_regs, S32, S16, sreg, dv, same_scalar):
            kh, kreg2 = T["kh"], T["khreg"]
            S3 = _r3(S32, 2, 512)
            if same_scalar and dv == 256:
                for c in range(2):
                    P.mm(PS[7][:, c * 256:(c + 1) * 256], [(kh[:, c * 128:(c + 1) * 128], v_tile)], [kreg2, vreg], [psr[7]])
                P.emit("dve", lambda v: v.scalar_tensor_tensor(
                    out=S3[:, :, 0:256], in0=S3[:, :, 0:256], scalar=sdecs[0], in1=_r3(PS[7][:, 0:512], 2, 256), op0=ALU.mult, op1=ALU.add),
                    [psr[7], sreg] + dcy_regs, [sreg])
            else:
                for c in range(2):
                    P.mm(PS[7][:, 0:dv], [(kh[:, c * 128:(c + 1) * 128], v_tile)], [kreg2, vreg], [psr[7]])
                    P.emit("dve", lambda v, c=c: v.scalar_tensor_tensor(
                        out=S3[:, c, 0:dv], in0=S3[:, c, 0:dv], scalar=sdecs[c], in1=PS[7][:, 0:dv], op0=ALU.mult, op1=ALU.add),
                        [psr[7], sreg] + dcy_regs, [sreg])
            if S16 is not None:
                S163 = _r3(S16[0], 2, 512)
                P.emit("act", lambda s: s.activation(out=S163[:, :, 0:dv], in_=S3[:, :, 0:dv], func=AF.Copy), [sreg], [S16[1]])

        lgam_reg = Region("lgam")
        P.emit("act", lambda s: s.activation(out=lg[:, 16:32], in_=lg[:, 0:16], func=AF.Exp, scale=-1.0), [consts_reg], [lgam_reg])
        P.emit("act", lambda s: s.activation(out=lg[:, 16:32], in_=lg[:, 16:32], func=AF.Ln, bias=ONE), [lgam_reg, c2], [lgam_reg])
        P.emit("dve", lambda v: v.tensor_scalar(out=lg[:, 16:32], in0=lg[:, 16:32], scalar1=-1.0, scalar2=None, op0=ALU.mult), [lgam_reg], [lgam_reg])
        P.emit("act", lambda s: s.activation(out=lg[:, 32:48], in_=lg[:, 16:32], func=AF.Exp, scale=128.0), [lgam_reg], [lgam_reg])

        def ret_tables(rtab3, rtreg, h, dirs_kinds):
            for dr, kind in dirs_kinds:
                col = 16 + dr * 8 + h
                if kind == 2:
                    io = iotas[:, 2 if dr == 0 else 3, :]
                else:
                    io = iotas[:, 0 if dr == 0 else 1, :]
                if kind == 1:
                    P.emit("dve", lambda v, col=col: v.tensor_scalar(out=lg[:, 56:57], in0=lg[:, col:col + 1], scalar1=-1.0, scalar2=None, op0=ALU.mult),
                           [lgam_reg], [rtreg])
                    P.emit("act", lambda s, dr=dr, kind=kind, io=io: s.activation(out=rtab3[:, dr * 3 + kind, :], in_=io, func=AF.Exp, scale=lg[:, 56:57]),
                           [rtreg, consts_reg], [rtreg])
                else:
                    P.emit("act", lambda s, dr=dr, kind=kind, io=io, col=col: s.activation(out=rtab3[:, dr * 3 + kind, :], in_=io, func=AF.Exp, scale=lg[:, col:col + 1]),
                           [lgam_reg, consts_reg], [rtreg])

        P.barrier()
        bigA.reset(0)
        c0_kT = [bigA.bf16(2 * TOK) for _ in range(2)]
        c0_v = [bigA.bf16(9 * 512) for _ in range(2)]
        c0k_reg = [Region("c0k%d" % i) for i in range(2)]
        c0v_reg = [Region("c0v%d" % i) for i in range(2)]
        ropeO = bigA.f32(2 * 896)
        ropeO_reg = Region("ropeO")
        P.dma("sp", "ld", ropeO[:, 0:896], rope_d[:, TOK:L], writes=[ropeO_reg])
        P.dma("sp", "ld", ropeO[:, 896:1792], rope_d[:, L + TOK:2 * L], writes=[ropeO_reg])

        def mk_temps(A, tag, shared):
            T = {}
            T["E"] = shared[0]
            T["L"] = shared[1]
            T["elreg"] = shared[2]
            T["ek"] = A.f32(256)
            T["eb"] = A.f32(256)
            T["enb"] = A.f32(256)
            T["nb"] = A.f32(2)
            T["sdec"] = A.f32(2)
            T["dreg"] = Region("dcy" + tag)
            T["khT"] = A.bf16(256)
            T["kh"] = A.bf16(256)
            T["khreg"] = Region("kh" + tag)
            return T
        sh0 = (bigA.f32(256), bigA.f32(256), Region("el0"))
        T0 = [mk_temps(bigA, "c0_%d" % i, sh0) for i in range(2)]
        c0_S32 = [bigA.f32(1024) for _ in range(2)]
        c0_Sreg = [Region("c0S%d" % i) for i in range(2)]
        c0_rtab = bigA.f32(6 * 128)
        c0_rtab3 = _r3(c0_rtab, 6, 128)
        c0_rtreg = Region("c0rt")
        c0_t1 = bigA.f32(128)
        c0_t2 = bigA.f32(128)
        c0_treg = Region("c0t")
        st_dreg = [[Region("std%d_%d" % (i, j)) for j in range(2)] for i in range(12)]

        P.emit("dve", lambda v: v.memset(GAb[0:32, 0:4608], 1.0), [], [ga_reg])
        if EXP2:
            P.barrier()
        P.dma("pool", "w", guT[0:17, :], gu_d[:, :], writes=[gu_reg])
        wl, wlreg = load_w(wlr_d[:, :], nbf=512)
        wl3 = _r3(wl, KC, 32)
        if EXP1:
            P.barrier()
        for dr in range(2):
            for blk in range(6):
                pb = ipb()
                c0 = blk * 384
                P.mm(PS[pb][0:16, 0:384], [(wl3[:, k, dr * 16:(dr + 1) * 16], hcol(k, c0, 384)) for k in range(KC)],
                     [wlreg] + hT_reg[blk * 3:blk * 3 + 3], [psr[pb]])
                P.emit("act", lambda s, pb=pb, dr=dr, c0=c0: s.activation(out=gaT3[0:16, dr, c0:c0 + 384], in_=PS[pb][0:16, 0:384], func=AF.Copy),
                       [psr[pb]], [ga_reg])

        step_i = [0]

        def c0_inproj(hd):
            is_ret = hd < 8
            h = hd if is_ret else hd - 8
            dv = 256 if is_ret else 512
            s = hd % 2
            if is_ret:
                Wk, wkreg = load_w(wr_d[h * 4 + 1])
                Wv = [load_w(wr_d[h * 4 + 2])]
            else:
                Wk, wkreg = load_w(wg_d[h * 6 + 1])
                Wv = [load_w(wg_d[h * 6 + 2]), load_w(wg_d[h * 6 + 3])]
            Wk3 = _r3(Wk, KC, 256)
            kT3 = _r3(c0_kT[s], 2, TOK)
            v3 = _r3(c0_v[s], 9, 512)
            for blk in range(3):
                c0 = TOK + blk * 384
                pbs = [ipb(), ipb()]
                for c in range(2):
                    P.mm(PS[pbs[c]][:, 0:384], [(Wk3[:, k, c * 128:(c + 1) * 128], hcol(k, c0, 384)) for k in range(KC)],
                         [wkreg] + hT_reg[9 + blk * 3:12 + blk * 3], [psr[pbs[c]]])
                for tt in range(3):
                    lt = blk * 3 + tt
                    tile = 9 + lt
                    sl = slice(tt * 128, (tt + 1) * 128)
                    if is_ret and tile < 16:
                        rc = slice(lt * 128, (lt + 1) * 128)
                        cosv = ropeO[:, 0:896][:, rc]
                        sinv = ropeO[:, 896:1792][:, rc]
                        p0 = PS[pbs[0]][:, sl]
                        p1 = PS[pbs[1]][:, sl]
                        d0 = kT3[:, 0, lt * 128:(lt + 1) * 128]
                        d1 = kT3[:, 1, lt * 128:(lt + 1) * 128]
                        P.emit("dve", lambda v, p0=p0, cosv=cosv: v.tensor_tensor(out=c0_t1, in0=p0, in1=cosv, op=ALU.mult), [psr[pbs[0]], ropeO_reg], [c0_treg])
                        P.emit("dve", lambda v, p1=p1, sinv=sinv: v.tensor_tensor(out=c0_t2, in0=p1, in1=sinv, op=ALU.mult), [psr[pbs[1]], ropeO_reg], [c0_treg])
                        P.emit("dve", lambda v, d0=d0: v.tensor_tensor(out=d0, in0=c0_t1, in1=c0_t2, op=ALU.subtract), [c0_treg], [c0k_reg[s]])
                        P.emit("dve", lambda v, p1=p1, cosv=cosv: v.tensor_tensor(out=c0_t1, in0=p1, in1=cosv, op=ALU.mult), [psr[pbs[1]], ropeO_reg, c0_treg], [c0_treg])
                        P.emit("dve", lambda v, p0=p0, sinv=sinv: v.tensor_tensor(out=c0_t2, in0=p0, in1=sinv, op=ALU.mult), [psr[pbs[0]], ropeO_reg, c0_treg], [c0_treg])
                        P.emit("dve", lambda v, d1=d1: v.tensor_tensor(out=d1, in0=c0_t1, in1=c0_t2, op=ALU.add), [c0_treg], [c0k_reg[s]])
                    else:
                        for c in range(2):
                            P.emit("act", lambda sc, c=c, sl=sl, lt=lt, pbc=pbs[c]: sc.activation(
                                out=kT3[:, c, lt * 128:(lt + 1) * 128], in_=PS[pbc][:, sl], func=AF.Copy), [psr[pbs[c]]], [c0k_reg[s]])
            yield
            for lt in range(9):
                if lt % 3 == 0 and lt > 0:
                    yield
                tile = 9 + lt
                pb = ipb()
                for i, (Wvv, wvreg) in enumerate(Wv):
                    Wv3 = _r3(Wvv, KC, 256)
                    P.mm(PS[pb][:, i * 256:(i + 1) * 256], [(hcol(k, tile * 128, 128), Wv3[:, k, :]) for k in range(KC)],
                         [wvreg, hT_reg[tile]], [psr[pb]])
                P.emit("act", lambda sc, lt=lt, pb=pb: sc.activation(out=v3[:, lt, 0:dv], in_=PS[pb][:, 0:dv], func=AF.Copy), [psr[pb]], [c0v_reg[s]])
            yield

        def c0_scan(hd, filler):
            is_ret = hd < 8
            h = hd if is_ret else hd - 8
            dv = 256 if is_ret else 512
            s = hd % 2
            kT3 = _r3(c0_kT[s], 2, TOK)
            v3 = _r3(c0_v[s], 9, 512)
            if is_ret:
                ret_tables(c0_rtab3, c0_rtreg, h, [(0, 2), (1, 2)])
            for dr in range(2):
                P.emit("dve", lambda v, S=c0_S32[dr]: v.memset(S, 0.0), [], [c0_Sreg[dr]])
            orders = [[16, 17], [17, 16, 15, 14, 13, 12, 11, 10, 9]]
            for i in range(9):
                live = [dr for dr in range(2) if i < len(orders[dr])]
                info = {}
                for dr in live:
                    tile = orders[dr][i]
                    lt = tile - 9
                    T = T0[dr]
                    if is_ret:
                        ek3 = c0_rtab3[:, dr * 3 + 2, :].unsqueeze(1).to_broadcast([128, 2, 128])
                        col = 32 + dr * 8 + h
                        sdecs = [lg[:, col:col + 1], lg[:, col:col + 1]]
                        dregs = [c0_rtreg, lgam_reg]
                    else:
                        ek3, sdecs = gla_decay(T, h, tile, dr, False, zb=[3, 5][dr])
                        dregs = [T["dreg"]]
                    kprep(T, kT3[:, :, lt * 128:(lt + 1) * 128], c0k_reg[s], ek3, dregs, [3, 5][dr])
                    info[dr] = (lt, sdecs, dregs)
                for dr in live:
                    lt, sdecs, dregs = info[dr]
                    supdate(T0[dr], v3[:, lt, 0:dv], c0v_reg[s], sdecs, dregs, c0_S32[dr], None, c0_Sreg[dr], dv, is_ret)
                filler()
            for dr in range(2):
                P.dma("sp", "st", ST_d[hd * 2 + dr][:, :], c0_S32[dr], reads=[c0_Sreg[dr]], writes=[st_dreg[hd][dr]])

        gens0 = [None]

        def filler0():
            g = gens0[0]
            if g is None:
                return
            try:
                next(g)
            except StopIteration:
                gens0[0] = None

        def drain0():
            while gens0[0] is not None:
                filler0()
        gens0[0] = c0_inproj(0)
        drain0()
        for hd_ in range(12):
            gens0[0] = c0_inproj(hd_ + 1) if hd_ + 1 < 12 else None
            c0_scan(hd_, filler0)
            drain0()

        if STOP_AFTER == "C0":
            P.barrier()
            for i in range(24):
                P.dma("sp", "out", dbg_d["st"][i * 128:(i + 1) * 128, :], ST_d[i][:, :])
            rgd = Region("dmpg")
            P.emit("dve", lambda v: v.tensor_copy(BIG[0:32, 0:4608], GAb[0:32, 0:4608]), [], [rgd])
            P.dma("sp", "out", dbg_d["ga"][:, :], BIG[0:32, 0:4608], reads=[rgd])
            finish()
            return nc

        P.barrier()
        bigA.reset(0)
        sets = []
        for s in range(2):
            st = {"qT": bigA.bf16(2 * TOK), "kT": bigA.bf16(2 * TOK), "v": bigA.bf16(9 * 512), "gs": bigA.bf16(9 * 512),
                  "qreg": Region("q%d" % s), "kreg": Region("k%d" % s), "vreg": Region("v%d" % s), "greg": Region("g%d" % s)}
            sets.append(st)
        o_f = bigA.f32(9 * 512)
        o_f3 = _r3(o_f, 9, 512)
        of_reg = Region("o_f")
        cA = Arena(HT[:, 9216:18432], 9216)
        mA = Arena(MISC_t[:, 128:2560], 2432)
        ropeN = cA.f32(2 * TOK)
        ropeN_reg = Region("ropeN")
        P.dma("sp", "ld", ropeN[:, 0:TOK], rope_d[:, 0:TOK], writes=[ropeN_reg])
        P.dma("sp", "ld", ropeN[:, TOK:2 * TOK], rope_d[:, L:L + TOK], writes=[ropeN_reg])
        shC = (cA.f32(256), cA.f32(256), Region("elC"))
        TC = [mk_temps(cA, "c%d" % i, shC) for i in range(2)]
        for i in range(2):
            TC[i]["qt"] = cA.bf16(256)
            TC[i]["kt"] = cA.bf16(256)
            TC[i]["PT"] = cA.bf16(128)
            TC[i]["qreg"] = Region("qt%d" % i)
            TC[i]["preg"] = Region("pt%d" % i)
        rtab = cA.f32(6 * 128)
        rtab3 = _r3(rtab, 6, 128)
        rtreg = Region("rt")
        S32 = [WB_t[:, 6144 + i * 1024:6144 + (i + 1) * 1024] for i in range(2)]
        Sreg = [Region("S32_%d" % i) for i in range(2)]
        S16 = [(cA.bf16(1024), Region("S16_%d" % i)) for i in range(2)]
        rp_t1 = cA.f32(384)
        rp_t2 = cA.f32(384)
        rp_reg = Region("rp")
        otot = mA.f32(512)
        ntmp = mA.f32(512)
        ro_reg = Region("ro")
        rb = [mA.bf16(512) for _ in range(2)]
        rb_reg = [Region("rb%d" % i) for i in range(2)]
        stats = mA.f32(16)
        silt = mA.f32(512)
        sil_reg = Region("sil")
        grow = [cA.f32(512) for _ in range(2)]
        grow_reg = [Region("grow%d" % i) for i in range(2)]
        R_reg = Region("R_d")

        def in_proj(hd, s):
            is_ret = hd < 8
            h = hd if is_ret else hd - 8
            dv = 256 if is_ret else 512
            st = sets[s]
            qT3 = _r3(st["qT"], 2, TOK)
            kT3 = _r3(st["kT"], 2, TOK)
            v3 = _r3(st["v"], 9, 512)
            gs3 = _r3(st["gs"], 9, 512)
            base = wr_d if is_ret else wg_d
            nsub = 4 if is_ret else 6
            gsrc = (rg_d if is_ret else gg_d)[0:1, h * dv:(h + 1) * dv]
            P.dma("sp", "ld", grow[s][:, 0:dv], gsrc.partition_broadcast(128), writes=[grow_reg[s]])
            for qi, (dst3, dreg) in enumerate(((qT3, st["qreg"]), (kT3, st["kreg"]))):
                W, wreg = load_w(base[h * nsub + qi], nslots=3)
                W3 = _r3(W, KC, 256)
                for blk in range(3):
                    c0 = blk * 384
                    pbs = [ipb(), ipb()]
                    for c in range(2):
                        P.mm(PS[pbs[c]][:, 0:384], [(W3[:, k, c * 128:(c + 1) * 128], hcol(k, c0, 384)) for k in range(KC)],
                             [wreg] + hT_reg[blk * 3:blk * 3 + 3], [psr[pbs[c]]])
                    if is_ret:
                        rope_evac(pbs[0], pbs[1], 384, dst3[:, 0, c0:c0 + 384], dst3[:, 1, c0:c0 + 384],
                                  ropeN[:, c0:c0 + 384], ropeN[:, TOK + c0:TOK + c0 + 384], rp_t1, rp_t2, rp_reg, ropeN_reg, dreg)
                    else:
                        for c in range(2):
                            P.emit("act", lambda sc, c=c, c0=c0, pbc=pbs[c], dst3=dst3: sc.activation(
                                out=dst3[:, c, c0:c0 + 384], in_=PS[pbc][:, 0:384], func=AF.Copy), [psr[pbs[c]]], [dreg])
                    yield
            vsubs = [2] if is_ret else [2, 3]
            Wv = [load_w(base[h * nsub + i], nslots=3) for i in vsubs]
            for tile in range(9):
                pb = ipb()
                for i, (Wvv, wvreg) in enumerate(Wv):
                    Wv3 = _r3(Wvv, KC, 256)
                    P.mm(PS[pb][:, i * 256:(i + 1) * 256], [(hcol(k, tile * 128, 128), Wv3[:, k, :]) for k in range(KC)],
                         [wvreg, hT_reg[tile]], [psr[pb]])
                P.emit("act", lambda sc, tile=tile, pb=pb: sc.activation(out=v3[:, tile, 0:dv], in_=PS[pb][:, 0:dv], func=AF.Copy),
                       [psr[pb]], [st["vreg"]])
                if tile % 3 == 2:
                    yield
            gsubs = [3] if is_ret else [4, 5]
            Wg = [load_w(base[h * nsub + i], nslots=3) for i in gsubs]
            for tile in range(9):
                pb = ipb()
                for i, (Wgg, wgreg) in enumerate(Wg):
                    Wg3 = _r3(Wgg, KC, 256)
                    P.mm(PS[pb][:, i * 256:(i + 1) * 256], [(hcol(k, tile * 128, 128), Wg3[:, k, :]) for k in range(KC)],
                         [wgreg, hT_reg[tile]], [psr[pb]])
                P.emit("act", lambda sc, pb=pb: sc.activation(out=silt[:, 0:dv], in_=PS[pb][:, 0:dv], func=AF.Silu), [psr[pb]], [sil_reg])
                P.emit("pool", lambda e, tile=tile: e.tensor_tensor(out=gs3[:, tile, 0:dv], in0=silt[:, 0:dv], in1=grow[s][:, 0:dv], op=ALU.mult),
                       [sil_reg, grow_reg[s]], [st["greg"]])
            yield

        def scan(hd, s, filler):
            is_ret = hd < 8
            h = hd if is_ret else hd - 8
            dv = 256 if is_ret else 512
            st = sets[s]
            qT3 = _r3(st["qT"], 2, TOK)
            kT3 = _r3(st["kT"], 2, TOK)
            v3 = _r3(st["v"], 9, 512)
            gs3 = _r3(st["gs"], 9, 512)
            for dr in range(2):
                P.dma("sp", "st", S32[dr], ST_d[hd * 2 + dr][:, :], reads=[st_dreg[hd][dr]], writes=[Sreg[dr]])
                S3 = _r3(S32[dr], 2, 512)
                S163 = _r3(S16[dr][0], 2, 512)
                P.emit("act", lambda sc, S3=S3, S163=S163: sc.activation(out=S163[:, :, 0:dv], in_=S3[:, :, 0:dv], func=AF.Copy),
                       [Sreg[dr]], [S16[dr][1]])
            if is_ret:
                ret_tables(rtab3, rtreg, h, [(0, 0), (0, 1), (0, 2), (1, 0), (1, 1), (1, 2)])
            ZB = [3, 5]
            OB = [4, 6]
            for t in range(9):
                info = {}
                for dr in range(2):
                    tile = t if dr == 0 else 8 - t
                    mask = mf_b if dr == 0 else mb_b
                    T = TC[dr]
                    tsl = slice(tile * 128, (tile + 1) * 128)
                    if is_ret:
                        eb3 = rtab3[:, dr * 3 + 0, :].unsqueeze(1).to_broadcast([128, 2, 128])
                        enb3 = rtab3[:, dr * 3 + 1, :].unsqueeze(1).to_broadcast([128, 2, 128])
                        ek3 = rtab3[:, dr * 3 + 2, :].unsqueeze(1).to_broadcast([128, 2, 128])
                        col = 32 + dr * 8 + h
                        sdecs = [lg[:, col:col + 1], lg[:, col:col + 1]]
                        dregs = [rtreg, lgam_reg]
                    else:
                        ek3, sdecs = gla_decay(T, h, tile, dr, True, zb=ZB[dr])
                        eb3 = _r3(T["eb"], 2, 128)
                        enb3 = _r3(T["enb"], 2, 128)
                        dregs = [T["dreg"]]
                    qt3 = _r3(T["qt"], 2, 128)
                    kt3 = _r3(T["kt"], 2, 128)
                    P.emit("dve", lambda v, qt3=qt3, eb3=eb3, tsl=tsl: v.tensor_tensor(out=qt3, in0=qT3[:, :, tsl], in1=eb3, op=ALU.mult),
                           [st["qreg"]] + dregs, [T["qreg"]])
                    P.emit("dve", lambda v, kt3=kt3, enb3=enb3, tsl=tsl: v.tensor_tensor(out=kt3, in0=kT3[:, :, tsl], in1=enb3, op=ALU.mult),
                           [st["kreg"]] + dregs, [T["qreg"]])
                    zb = ZB[dr]
                    P.mm(PS[zb][:, 256:384], [(kt3[:, c, :], qt3[:, c, :]) for c in range(2)], [T["qreg"]], [psr[zb]])
                    PT = T["PT"]
                    P.emit("dve", lambda v, PT=PT, mask=mask, zb=zb: v.tensor_tensor(out=PT, in0=PS[zb][:, 256:384], in1=mask[:, :], op=ALU.mult),
                           [psr[zb], c2], [T["preg"]])
                    if t < 8:
                        kprep(T, kT3[:, :, tsl], st["kreg"], ek3, dregs, zb, keng="pool")
                    info[dr] = (tile, sdecs, dregs, qt3, PT)
                for dr in range(2):
                    tile, sdecs, dregs, qt3, PT = info[dr]
                    T = TC[dr]
                    ob = OB[dr]
                    S163 = _r3(S16[dr][0], 2, 512)
                    P.mm(PS[ob][:, 0:dv], [(PT, v3[:, tile, 0:dv])] + [(qt3[:, c, :], S163[:, c, 0:dv]) for c in range(2)],
                         [T["preg"], T["qreg"], st["vreg"], S16[dr][1]], [psr[ob]])
                    do_readout = (t > 4) if dr == 0 else (t >= 4)
                    if not do_readout:
                        P.emit("act", lambda sc, tile=tile, ob=ob: sc.activation(out=o_f3[:, tile, 0:dv], in_=PS[ob][:, 0:dv], func=AF.Copy),
                               [psr[ob]], [of_reg])
                    else:
                        readout(ob, is_ret, h, dv, tile, gs3, st)
                    if t < 8:
                        supdate(T, v3[:, tile, 0:dv], st["vreg"], sdecs, dregs, S32[dr], S16[dr], Sreg[dr], dv, is_ret)
                filler()
                if t % 2 == 1:
                    filler()
            if hd == DBG_OF_HEAD:
                P.barrier()
                P.dma("sp", "out", dbg_d["of"][:, :], o_f)
                finish()
                raise _Stop()

        ro_i = [0]

        def readout(ob, is_ret, h, dv, tile, gs3, st):
            P.emit("dve", lambda v: v.tensor_tensor(out=otot[:, 0:dv], in0=o_f3[:, tile, 0:dv], in1=PS[ob][:, 0:dv], op=ALU.add),
                   [psr[ob], of_reg], [ro_reg])
            mean = stats[:, 0:1]
            var = stats[:, 1:2]
            rs = stats[:, 2:3]
            nmr = stats[:, 3:4]
            i = ro_i[0] % 2
            ro_i[0] += 1
            if is_ret:
                P.emit("dve", lambda v: v.bn_stats(stats[:, 8:14], otot[:, 0:dv]), [ro_reg], [ro_reg])
                P.emit("dve", lambda v: v.bn_aggr(stats[:, 0:2], stats[:, 8:14]), [ro_reg], [ro_reg])
                rstd_act(rs, var, 1.0, epsc[:, 1:2], [ro_reg], [ro_reg])
                P.emit("dve", lambda v: v.scalar_tensor_tensor(out=nmr, in0=mean, scalar=-1.0, in1=rs, op0=ALU.mult, op1=ALU.mult), [ro_reg], [ro_reg])
                P.emit("act", lambda sc: sc.activation(out=ntmp[:, 0:dv], in_=otot[:, 0:dv], func=AF.Identity, scale=rs, bias=nmr), [ro_reg], [ro_reg])
                P.emit("dve", lambda v, i=i: v.tensor_tensor(out=rb[i][:, 0:dv], in0=ntmp[:, 0:dv], in1=gs3[:, tile, 0:dv], op=ALU.mult),
                       [ro_reg, st["greg"]], [rb_reg[i]])
                col0 = h * 256
            else:
                P.emit("dve", lambda v: v.memset(var, 0.0), [ro_reg], [ro_reg])
                P.emit("act", lambda sc: sc.activation(out=ntmp[:, 0:dv], in_=otot[:, 0:dv], func=AF.Square, accum_out=var), [ro_reg], [ro_reg])
                rstd_act(rs, var, 1.0 / dv, epsc[:, 1:2], [ro_reg], [ro_reg])
                P.emit("dve", lambda v, i=i: v.scalar_tensor_tensor(out=rb[i][:, 0:dv], in0=otot[:, 0:dv], scalar=rs, in1=gs3[:, tile, 0:dv],
                                                                   op0=ALU.mult, op1=ALU.mult), [ro_reg, st["greg"]], [rb_reg[i]])
                col0 = 2048 + h * 512
            P.dma("sp", "rst", R_d[tile * 128:(tile + 1) * 128, col0:col0 + dv], rb[i][:, 0:dv], reads=[rb_reg[i]], writes=[R_reg])

        gens = [None]

        def filler(n=1):
            g = gens[0]
            if g is None:
                return
            for _ in range(n):
                try:
                    next(g)
                except StopIteration:
                    gens[0] = None
                    return

        def drain():
            while gens[0] is not None:
                filler()
        NHEADS = 12
        gens[0] = in_proj(0, 0)
        drain()
        try:
            for hd in range(NHEADS):
                gens[0] = in_proj(hd + 1, (hd + 1) % 2) if hd + 1 < NHEADS else None
                if NO_INTERLEAVE:
                    drain()
                scan(hd, hd % 2, filler)
                drain()
        except _Stop:
            return nc

        if STOP_AFTER == "C":
            P.barrier()
            rgd = Region("dmpR")
            stg = BIG.bitcast(BF16)[:, 0:4096]
            for tile in range(9):
                P.dma("sp", "ld", stg, R_d[tile * 128:(tile + 1) * 128, :], writes=[rgd])
                P.emit("dve", lambda v: v.tensor_copy(HT[:, 0:4096], stg), [rgd], [rgd])
                P.dma("sp", "out", dbg_d["R"][tile * 128:(tile + 1) * 128, :], HT[:, 0:4096], reads=[rgd], writes=[rgd])
            finish()
            return nc

        P.barrier()
        rT3 = _r3(BIG.bitcast(BF16), 32, TOK)
        rT_reg = Region("rT")
        hxb = HT[:, 9216:18432]
        rtile = [hxb[:, i * 2048:(i + 1) * 2048].bitcast(BF16) for i in range(2)]
        rtile_reg = [Region("rtile%d" % i) for i in range(2)]
        for tile in range(9):
            i = tile % 2
            P.dma("sp", "ld", rtile[i], R_d[tile * 128:(tile + 1) * 128, :], reads=[R_reg], writes=[rtile_reg[i]])
            for q in range(4):
                pb = (tile * 4 + q) % 8
                for j in range(8):
                    kc = q * 8 + j
                    P.transpose(PSb[pb][:, j * 128:(j + 1) * 128], rtile[i][:, kc * 128:(kc + 1) * 128], ident_b[:, :],
                                [rtile_reg[i], c2], [psr[pb]])
                dst = rT3[:, q * 8:(q + 1) * 8, tile * 128:(tile + 1) * 128]
                srcv = _r3(PSb[pb][:, 0:1024], 8, 128)
                if q % 2 == 0:
                    P.emit("act", lambda sc, dst=dst, srcv=srcv: sc.activation(out=dst, in_=srcv, func=AF.Copy), [psr[pb]], [rT_reg])
                else:
                    P.emit("dve", lambda v, dst=dst, srcv=srcv: v.tensor_copy(dst, srcv), [psr[pb]], [rT_reg])
        P.barrier()
        mT3 = _r3(hxb.bitcast(BF16), KC, TOK)
        mT_reg = Region("mT")
        gA = Arena(GA_t[:, :], 3328)
        dt1 = [gA.f32(384) for _ in range(2)]
        dt2 = [gA.f32(384) for _ in range(2)]
        dt_reg = [Region("dt%d" % i) for i in range(2)]

        def load2(srcs):
            i = wb_i[0] % 4
            wb_i[0] += 1
            off = 0
            views = []
            for src, n in srcs:
                v = WBb[:, i * 4096 + off:i * 4096 + off + n]
                P.dma("pool", "w", v, src, writes=[wb_reg[i]])
                views.append(v)
                off += n
            return views, wb_reg[i]

        def d1_block(j, blk, wrp3, wgp3, wmg3, regA, regB):
            c0 = blk * 384
            par = (j * 3 + blk) % 2
            pbs = [par * 4 + i for i in range(4)]
            P.mm(PS[pbs[0]][:, 0:384], [(wrp3[:, k, :], rT3[:, k, c0:c0 + 384]) for k in range(KC)], [regA, rT_reg], [psr[pbs[0]]])
            P.mm(PS[pbs[1]][:, 0:384], [(wgp3[:, k, :], rT3[:, 16 + k, c0:c0 + 384]) for k in range(KC)], [regA, rT_reg], [psr[pbs[1]]])
            P.mm(PS[pbs[2]][:, 0:384], [(wmg3[:, k, 0:128], hTo3[:, k, c0:c0 + 384]) for k in range(KC)],
                 [regB] + hT_reg[blk * 3:blk * 3 + 3], [psr[pbs[2]]])
            P.mm(PS[pbs[3]][:, 0:384], [(wmg3[:, k, 128:256], hTo3[:, k, c0:c0 + 384]) for k in range(KC)],
                 [regB] + hT_reg[blk * 3:blk * 3 + 3], [psr[pbs[3]]])
            t1, t2, treg = dt1[par], dt2[par], dt_reg[par]
            P.emit("act", lambda sc: sc.activation(out=t1, in_=PS[pbs[2]][:, 0:384], func=AF.Sigmoid), [psr[pbs[2]]], [treg])
            P.emit("act", lambda sc: sc.activation(out=t2, in_=PS[pbs[3]][:, 0:384], func=AF.Sigmoid), [psr[pbs[3]]], [treg])
            P.emit("dve", lambda v: v.tensor_tensor(out=t1, in0=PS[pbs[0]][:, 0:384], in1=t1, op=ALU.mult), [psr[pbs[0]], treg], [treg])
            P.emit("dve", lambda v: v.tensor_tensor(out=t2, in0=PS[pbs[1]][:, 0:384], in1=t2, op=ALU.mult), [psr[pbs[1]], treg], [treg])
            P.emit("pool", lambda e: e.tensor_tensor(out=mT3[:, j, c0:c0 + 384], in0=t1, in1=t2, op=ALU.add), [treg], [mT_reg])

        for j in range(16):
            (wrp, wgp), regA = load2([(wrp_d[j], 2048), (wgp_d[j], 2048)])
            (wmg,), regB = load2([(wmg_d[j], 4096)])
            for blk in range(3):
                d1_block(j, blk, _r3(wrp, KC, 128), _r3(wgp, KC, 128), _r3(wmg, KC, 256), regA, regB)

        if STOP_AFTER == "D1":
            P.barrier()
            for k in range(KC):
                rgd = Region("dmp")
                P.emit("dve", lambda v, k=k: v.tensor_copy(BIG[:, 0:TOK], mT3[:, k, :]), [], [rgd])
                P.dma("sp", "out", dbg_d["mT"][:, k * TOK:(k + 1) * TOK], BIG[:, 0:TOK], reads=[rgd])
                P.barrier()
            finish()
            return nc

        P.barrier()
        x1T3 = _r3(BIG, KC, TOK)
        x1_reg = [Region("x1_%d" % k) for k in range(KC)]
        gA.reset(0)
        xtile = gA.f32(2048)
        xtile_reg = Region("xtile")
        for tile in range(9):
            P.dma("sp", "ld", xtile, x_d[tile * 128:(tile + 1) * 128, :], writes=[xtile_reg])
            for q in range(4):
                pb = (tile * 4 + q) % 8
                for kk in range(4):
                    k = q * 4 + kk
                    P.transpose(PS[pb][:, kk * 128:(kk + 1) * 128], xtile[:, k * 128:(k + 1) * 128], ident_f[:, :],
                                [xtile_reg, consts_reg], [psr[pb]])
                dst = x1T3[:, q * 4:(q + 1) * 4, tile * 128:(tile + 1) * 128]
                srcv = _r3(PS[pb][:, 0:512], 4, 128)
                if q % 2 == 0:
                    P.emit("act", lambda sc, dst=dst, srcv=srcv: sc.activation(out=dst, in_=srcv, func=AF.Copy), [psr[pb]], x1_reg[q * 4:(q + 1) * 4])
                else:
                    P.emit("dve", lambda v, dst=dst, srcv=srcv: v.tensor_copy(dst, srcv), [psr[pb]], x1_reg[q * 4:(q + 1) * 4])

        def d2_block(j, blk, wo3, rg):
            c0 = blk * 384
            pb = (j * 3 + blk) % 8
            P.mm(PS[pb][:, 0:384], [(wo3[:, k, :], mT3[:, k, c0:c0 + 384]) for k in range(KC)], [rg, mT_reg], [psr[pb]])
            P.emit("dve", lambda v: v.scalar_tensor_tensor(out=x1T3[:, j, c0:c0 + 384], in0=PS[pb][:, 0:384], scalar=g1[:, j:j + 1],
                                                          in1=x1T3[:, j, c0:c0 + 384], op0=ALU.mult, op1=ALU.add),
                   [psr[pb], x1_reg[j], mod_reg], [x1_reg[j]])
        for j in range(16):
            (wo,), rg = load2([(wo_d[j], 2048)])
            for blk in range(3):
                d2_block(j, blk, _r3(wo, KC, 128), rg)

        P.barrier()
        h2T3 = _r3(hxb.bitcast(BF16), KC, TOK)
        h2_reg = Region("h2T")
        hA = Arena(HT[:, 0:9216], 9216)
        sqb = hA.bf16(16 * 512)
        sq_reg = Region("sq")
        rbc = hA.f32(512)
        rbc_reg = Region("rbc")
        tmpk = [hA.f32(512) for _ in range(2)]
        tmpk_reg = [Region("tmpk%d" % i) for i in range(2)]

        def ssq_bcast(c0, n, pb):
            sq3 = _r3(sqb[:, 0:16 * n], KC, n)
            P.emit("act", lambda sc: sc.activation(out=sq3, in_=x1T3[:, :, c0:c0 + n], func=AF.Square), x1_reg, [sq_reg])
            P.mm(PS[pb][:, 0:n], [(ones_b[:, :], sq3[:, k, :]) for k in range(KC)], [sq_reg, c2], [psr[pb]])
            rstd_act(rbc[:, 0:n], PS[pb][:, 0:n], 1.0 / D, epsc[:, 0:1], [psr[pb]], [rbc_reg])

        def d3_chunk(k, c0, n):
            i = k % 2
            P.emit("dve", lambda v: v.scalar_tensor_tensor(out=tmpk[i][:, 0:n], in0=x1T3[:, k, c0:c0 + n], scalar=A2[:, k:k + 1], in1=rbc[:, 0:n],
                                                          op0=ALU.mult, op1=ALU.mult), [x1_reg[k], a_reg, rbc_reg], [tmpk_reg[i]])
            P.emit("act", lambda sc: sc.activation(out=h2T3[:, k, c0:c0 + n], in_=tmpk[i][:, 0:n], func=AF.Identity, bias=sh2[:, k:k + 1]),
                   [tmpk_reg[i], mod_reg], [h2_reg])
        for blk in range(3):
            ssq_bcast(blk * 384, 384, blk)
            for k in range(KC):
                d3_chunk(k, blk * 384, 384)

        if STOP_AFTER == "D":
            P.barrier()
            for k in range(KC):
                P.dma("sp", "out", dbg_d["x1T"][:, k * TOK:(k + 1) * TOK], x1T3[:, k, :])
            for k in range(KC):
                rgd = Region("dmp")
                P.emit("dve", lambda v, k=k: v.tensor_copy(hA.ap[:, 0:TOK], h2T3[:, k, :]), [], [rgd])
                P.dma("sp", "out", dbg_d["h2T"][:, k * TOK:(k + 1) * TOK], hA.ap[:, 0:TOK], reads=[rgd])
                P.barrier()
            finish()
            return nc

        P.barrier()
        hA.reset(0)
        aTp = [hA.f32(20 * 66) for _ in range(2)]
        aTp3 = [_r3(a, 20, 66) for a in aTp]
        aT_reg = [Region("aT%d" % i) for i in range(2)]
        acc = [hA.f32(1024) for _ in range(2)]
        acc_reg = [Region("acc%d" % i) for i in range(2)]
        gact = [hA.f32(1024) for _ in range(2)]
        gact_reg = [Region("gact%d" % i) for i in range(2)]
        gvT = [hA.bf16(4 * 1024), GA_t[:, 0:2048].bitcast(BF16)]
        gv_reg = [Region("gv%d" % i) for i in range(2)]
        for i in range(2):
            P.emit("dve", lambda v, i=i: v.memset(aTp[i], 0.0), [], [aT_reg[i]])
        wdnv = WBb[:, 8192:16384]
        wdn3 = _r3(wdnv, 4, 2048)
        CW = convv[:, 0:NCC * 9]
        CB = convv[:, NCC * 9:NCC * 10]

        def ffn_chunk(cc):
            g, ci = divmod(cc, 4)
            par = cc % 2
            if ci == 0:
                for i in range(min(4, NCC - g * 4)):
                    P.dma("pool", "w", wdnv[:, i * 2048:(i + 1) * 2048], wdn_d[g * 4 + i], writes=[wb_reg[2], wb_reg[3]])
            slot = cc % 2
            wu = WBb[:, slot * 4096:(slot + 1) * 4096]
            P.dma("pool", "w", wu, wup_d[cc], writes=[wb_reg[slot]])
            wu3 = _r3(wu, KC, 256)
            a3 = aTp3[par]
            for blk in range(3):
                pb = blk
                c0 = blk * 384
                P.mm(PS[pb][:, 0:384], [(wu3[:, k, 0:128], h2T3[:, k, c0:c0 + 384]) for k in range(KC)], [wb_reg[slot], h2_reg], [psr[pb]])
                P.emit("act", lambda sc, pb=pb, blk=blk: sc.activation(out=a3[:, 1 + blk * 6:7 + blk * 6, 1:65], in_=_r3(PS[pb][:, 0:384], 6, 64), func=AF.Copy),
                       [psr[pb]], [aT_reg[par]])
            acc3 = _r3(acc[par], 16, 64)
            for tap in range(9):
                i, jx = divmod(tap, 3)
                view = a3[:, i:i + 16, jx:jx + 64]
                w = CW[:, cc * 9 + tap:cc * 9 + tap + 1]
                if tap == 0:
                    P.emit("dve", lambda v, view=view, w=w: v.tensor_scalar(out=acc3, in0=view, scalar1=w, scalar2=None, op0=ALU.mult),
                           [aT_reg[par], consts_reg], [acc_reg[par]])
                else:
                    P.emit("dve", lambda v, view=view, w=w: v.scalar_tensor_tensor(out=acc3, in0=view, scalar=w, in1=acc3, op0=ALU.mult, op1=ALU.add),
                           [aT_reg[par], consts_reg, acc_reg[par]], [acc_reg[par]])
            P.emit("act", lambda sc: sc.activation(out=gact[par], in_=acc[par], func=AF.Gelu_apprx_tanh, bias=CB[:, cc:cc + 1]),
                   [acc_reg[par], consts_reg], [gact_reg[par]])
            gv3 = _r3(gvT[g % 2], 4, 1024)
            for tb in range(2):
                pb = 3 + tb
                P.mm(PS[pb][:, 0:512], [(wu3[:, k, 128:256], h2T3[:, k, tb * 512:(tb + 1) * 512]) for k in range(KC)],
                     [wb_reg[slot], h2_reg], [psr[pb]])
                P.emit("dve", lambda v, pb=pb, tb=tb: v.tensor_tensor(out=gv3[:, ci, tb * 512:(tb + 1) * 512], in0=PS[pb][:, 0:512],
                                                                    in1=gact[par][:, tb * 512:(tb + 1) * 512], op=ALU.mult),
                       [psr[pb], gact_reg[par]], [gv_reg[g % 2]])
            if ci == 3 or cc == NCC - 1:
                ncg = ci + 1
                for j in range(16):
                    for tb in range(2):
                        pb = 5 + (j * 2 + tb) % 3
                        P.mm(PS[pb][:, 0:512], [(wdn3[:, i, j * 128:(j + 1) * 128], gv3[:, i, tb * 512:(tb + 1) * 512]) for i in range(ncg)],
                             [wb_reg[2], wb_reg[3], gv_reg[g % 2]], [psr[pb]])
                        P.emit("dve", lambda v, pb=pb, j=j, tb=tb: v.scalar_tensor_tensor(
                            out=x1T3[:, j, tb * 512:(tb + 1) * 512], in0=PS[pb][:, 0:512], scalar=g2[:, j:j + 1],
                            in1=x1T3[:, j, tb * 512:(tb + 1) * 512], op0=ALU.mult, op1=ALU.add), [psr[pb], x1_reg[j], mod_reg], [x1_reg[j]])
        for cc in range(NCC):
            ffn_chunk(cc)

        P.barrier()
        hA.reset(0)
        sqb = hA.bf16(16 * 512)
        rbc = hA.f32(512)
        otile = [GA_t[:, 0:2048], hA.f32(2048)]
        otile_reg = [Region("otile%d" % i) for i in range(2)]
        oT3 = _r3(hxb[:, 0:8192], KC, 512)
        oT_reg = Region("oT")

        def fin_block(blk):
            c0 = blk * 512
            ssq_bcast(c0, 512, blk)
            for k in range(KC):
                P.emit("dve", lambda v, k=k: v.scalar_tensor_tensor(out=oT3[:, k, :], in0=x1T3[:, k, c0:c0 + 512], scalar=fgv[:, k:k + 1], in1=rbc[:, 0:512],
                                                                   op0=ALU.mult, op1=ALU.mult), [x1_reg[k], consts_reg, rbc_reg], [oT_reg])
            for tt in range(4):
                tile = blk * 4 + tt
                par = tile % 2
                for q in range(4):
                    pb = 2 + (tile * 4 + q) % 6
                    for kk in range(4):
                        P.transpose(PS[pb][:, kk * 128:(kk + 1) * 128], oT3[:, q * 4 + kk, tt * 128:(tt + 1) * 128], ident_f[:, :],
                                    [oT_reg, consts_reg], [psr[pb]])
                    dst = otile[par][:, q * 512:(q + 1) * 512]
                    if q % 2 == 0:
                        P.emit("act", lambda sc, dst=dst, pb=pb: sc.activation(out=dst, in_=PS[pb][:, 0:512], func=AF.Copy), [psr[pb]], [otile_reg[par]])
                    else:
                        P.emit("dve", lambda v, dst=dst, pb=pb: v.tensor_copy(dst, PS[pb][:, 0:512]), [psr[pb]], [otile_reg[par]])
                P.dma("sp", "out", out_d[tile * 128:(tile + 1) * 128, :], otile[par], reads=[otile_reg[par]])
        for blk in range(2):
            fin_block(blk)
        finish()
    return nc


def _tile_k(w):
    n = w.shape[1]
    return np.ascontiguousarray(w.reshape(KC, 128, n).transpose(1, 0, 2)).reshape(128, KC * n)


def _fm(v, nchunk):
    return np.ascontiguousarray(v.reshape(nchunk, 128).T)


def prep_shared(inp, s):
    w_in = inp["w_in"][0]
    o = {}
    RQ, RK, RV, RG = 0, 2048, 4096, 6144
    GQ, GK, GV, GGT = 8192, 9216, 10240, 12288
    GAF, GAB, MR, MG = 14336, 14352, 14368, 16416
    wr = np.empty((RH * 4, 128, KC * 256), np.float32)
    for h in range(RH):
        for i, base in enumerate((RQ, RK, RV, RG)):
            wr[h * 4 + i] = _tile_k(w_in[:, base + h * 256: base + (h + 1) * 256])
    o["w_ret"] = wr
    wg = np.empty((GH * 6, 128, KC * 256), np.float32)
    for h in range(GH):
        wg[h * 6 + 0] = _tile_k(w_in[:, GQ + h * 256: GQ + (h + 1) * 256])
        wg[h * 6 + 1] = _tile_k(w_in[:, GK + h * 256: GK + (h + 1) * 256])
        wg[h * 6 + 2] = _tile_k(w_in[:, GV + h * 512: GV + h * 512 + 256])
        wg[h * 6 + 3] = _tile_k(w_in[:, GV + h * 512 + 256: GV + (h + 1) * 512])
        wg[h * 6 + 4] = _tile_k(w_in[:, GGT + h * 512: GGT + h * 512 + 256])
        wg[h * 6 + 5] = _tile_k(w_in[:, GGT + h * 512 + 256: GGT + (h + 1) * 512])
    o["w_gla"] = wg
    lr = (w_in[:, GAF:GAF + 16], w_in[:, GAB:GAB + 16])
    if s == 1:
        lr = lr[::-1]
    o["w_lr"] = _tile_k(np.concatenate(lr, axis=1))
    wmg = np.empty((16, 128, KC * 256), np.float32)
    for j in range(16):
        wmg[j] = _tile_k(np.concatenate([w_in[:, MR + j * 128: MR + (j + 1) * 128], w_in[:, MG + j * 128: MG + (j + 1) * 128]], axis=1))
    o["w_mg"] = wmg
    for name, key in (("w_rp", "w_ret_proj"), ("w_gp", "w_gla_proj"), ("w_o", "w_out")):
        w = inp[key][0]
        o[name] = np.stack([_tile_k(w[:, j * 128:(j + 1) * 128]) for j in range(16)])
    w_up = inp["w_up"][0]
    o["w_up"] = np.stack([_tile_k(np.concatenate([w_up[:, cc * 128:(cc + 1) * 128], w_up[:, DFF + cc * 128: DFF + (cc + 1) * 128]], axis=1))
                          for cc in range(NCC)])
    o["w_dn"] = np.ascontiguousarray(inp["w_down"][0].reshape(NCC, 128, D))
    w_ada = inp["w_ada"][0]
    wa = np.empty((24, 128, KC * 512), np.float32)
    for g in range(24):
        blk = w_ada[:, g * 512:(g + 1) * 512].reshape(KC, 128, 4, 128).transpose(1, 2, 0, 3)
        wa[g] = np.ascontiguousarray(blk).reshape(128, KC * 512)
    o["wada"] = wa
    o["vecs"] = np.concatenate([_fm(inp["b_ada"][0], 96), _fm(inp["norm1_g"][0], 16), _fm(inp["norm2_g"][0], 16),
                                _fm(inp["final_g"], 16)], axis=1).astype(np.float32)
    cw = inp["conv_w"][0]
    if s == 1:
        cw = cw[::-1, ::-1]
    cwt = np.ascontiguousarray(cw.reshape(9, NCC, 128).transpose(2, 1, 0)).reshape(128, NCC * 9)
    o["convv"] = np.concatenate([cwt, _fm(inp["conv_b"][0], NCC)], axis=1).astype(np.float32)
    dl = inp["ret_decay_logit"][0]
    gu = inp["gla_gate_up"][0]
    gb = inp["gla_gate_bias"][0]
    order = (0, 1) if s == 0 else (1, 0)
    o["dlog"] = np.concatenate([dl[order[0]], dl[order[1]]])[None, :].astype(np.float32)
    o["gup"] = np.concatenate([np.concatenate([gu[d], gb[d][None, :]], axis=0) for d in order], axis=1).astype(np.float32)
    o["rng"] = inp["ret_norm_g"][0][None, :].astype(np.float32)
    o["gng"] = inp["gla_norm_g"][0][None, :].astype(np.float32)
    ident = np.eye(128, dtype=np.float32)
    idx = np.arange(128)
    mf = (idx[None, :] >= idx[:, None]).astype(np.float32)
    mb = (idx[None, :] <= idx[:, None]).astype(np.float32)
    io = np.stack([idx + 1.0, 128.0 - idx, 127.0 - idx, idx * 1.0]).astype(np.float32)
    iot = np.broadcast_to(io.reshape(1, 512), (128, 512))
    o["consts"] = np.concatenate([ident, mf, mb, iot], axis=1).astype(np.float32)
    pos = np.arange(L) if s == 0 else np.arange(L - 1, -1, -1)
    row = (pos // 64).astype(np.float32)
    col = (pos % 64).astype(np.float32)
    inv = (10000.0 ** (-np.arange(64, dtype=np.float32) / 64)).astype(np.float32)
    ang = np.concatenate([row[:, None] * inv, col[:, None] * inv], axis=-1)
    o["rope"] = np.concatenate([np.cos(ang).T, np.sin(ang).T], axis=1).astype(np.float32)
    return o


def make_in_maps(inp):
    shared = [prep_shared(inp, 0), prep_shared(inp, 1)]
    maps = []
    for core in range(8):
        b, s = core // 2, core % 2
        m = dict(shared[s])
        xb = inp["x"][b]
        cb = inp["ctx"][b]
        m["x"] = np.ascontiguousarray(xb if s == 0 else xb[::-1])
        m["ctx"] = np.ascontiguousarray(cb if s == 0 else cb[::-1])
        cvv = np.stack([inp["c"][b].reshape(KC, 128).T, inp["c_ctx"].reshape(KC, 128).T], axis=2)
        m["cv"] = np.ascontiguousarray(cvv).reshape(128, KC * 2).astype(np.float32)
        maps.append(m)
    return maps


_NC_CACHE = {}


def kernel(**inputs):
    inp = {k: np.asarray(v) for k, v in inputs.items()}
    if "nc" not in _NC_CACHE:
        _NC_CACHE["nc"] = build_program()
    nc = _NC_CACHE["nc"]
    maps = make_in_maps(inp)
    res = run_bass_kernel_spmd(nc, maps, core_ids=list(range(8)))
    out = np.empty((4, L, D), np.float32)
    for core in range(8):
        b, s = core // 2, core % 2
        o = res.results[core]["out"]
        if s == 0:
            out[b, 0:1024] = o
        else:
            out[b, 1024:2048] = o[::-1]
    return out
```

```python
import numpy as np
from contextlib import ExitStack
import concourse.bass as bass
import concourse.mybir as mybir
from concourse.bass_utils import run_bass_kernel_spmd

F32 = mybir.dt.float32
BF16 = mybir.dt.bfloat16
AF = mybir.ActivationFunctionType
ALU = mybir.AluOpType
AX = mybir.AxisListType

D = 2048
KC = 16
L = 2048
CTX = 256
NT = 9
TOK = NT * 128
TOKA = 2304
RH, GH = 8, 4
DFF = 5504
NCC = 43
EPS = 1e-6
EPS_RO = 256.0 * EPS
STOP_AFTER = None
DBG_OF_HEAD = None
NO_INTERLEAVE = False
EXP1 = False
EXP2 = True


class _Stop(Exception):
    pass


class Region:
    __slots__ = ("name", "w", "r", "excl")

    def __init__(self, name, excl=False):
        self.name = name
        self.w = {}
        self.r = {}
        self.excl = excl


class Prog:
    ENG = ("pe", "act", "dve", "pool", "sp")

    def __init__(self):
        self.ops = {e: [] for e in self.ENG}
        self.cnt = {}
        self.known = {e: {} for e in self.ENG}

    def _need(self, eng, reads, writes, is_dma=False):
        need = {}

        def add(k, v):
            if v > need.get(k, 0):
                need[k] = v
        for rg in reads:
            for k, v in rg.w.items():
                if k == eng and eng == "pe":
                    continue
                add(k, v)
            if rg.excl:
                for k, v in rg.r.items():
                    if k != eng or is_dma:
                        add(k, v)
        for rg in writes:
            for k, v in rg.w.items():
                if k != eng or is_dma:
                    add(k, v)
            for k, v in rg.r.items():
                if k != eng or is_dma:
                    add(k, v)
        out = []
        kn = self.known[eng]
        for k, v in need.items():
            if kn.get(k, 0) < v:
                kn[k] = v
                out.append((k, v))
        return out

    def _record(self, key, val, reads, writes):
        for rg in reads:
            rg.r[key] = val
        for rg in writes:
            rg.w[key] = val

    def emit(self, eng, fn, reads=(), writes=(), inc=True, wait=True):
        w = self._need(eng, reads, writes) if wait else []
        key = None
        if inc:
            self.cnt[eng] = self.cnt.get(eng, 0) + 1
            key = eng
            self._record(eng, self.cnt[eng], reads, writes)
        self.ops[eng].append((w, fn, key, 1))

    def mm(self, out, pairs, reads, writes):
        n = len(pairs)
        for i, (l, r) in enumerate(pairs):
            def fn(t, l=l, r=r, st=(i == 0), sp=(i == n - 1)):
                return t.matmul(out, lhsT=l, rhs=r, start=st, stop=sp)
            self.emit("pe", fn, reads, writes, inc=(i == n - 1), wait=(i == 0))

    def transpose(self, out, in_, ident, reads, writes):
        self.emit("pe", lambda t: t.transpose(out, in_, ident), reads, writes)

    def dma(self, eng, chan, out, in_, reads=(), writes=()):
        w = self._need(eng, reads, writes, is_dma=True)
        self.cnt[chan] = self.cnt.get(chan, 0) + 16
        self._record(chan, self.cnt[chan], reads, writes)
        self.ops[eng].append((w, lambda e: e.dma_start(out=out, in_=in_), chan, 16))

    def barrier(self):
        for eng in self.ENG:
            kn = self.known[eng]
            waits = []
            for k, v in self.cnt.items():
                if k == eng and eng == "pe":
                    continue
                if kn.get(k, 0) < v:
                    kn[k] = v
                    waits.append((k, v))
            if waits:
                self.ops[eng].append((waits, None, None, 0))

    def replay(self, block, sems, final_waits):
        hmap = {"pe": block.tensor, "act": block.scalar, "dve": block.vector, "pool": block.gpsimd, "sp": block.sync}
        for eng in self.ENG:
            ops = self.ops[eng]
            extra = final_waits if eng == "sp" else ()

            def body(h, ops=ops, extra=extra):
                for waits, fn, key, amt in ops:
                    for k, v in waits:
                        h.wait_ge(sems[k], v)
                    if fn is None:
                        continue
                    ins = fn(h)
                    if key is not None:
                        ins.then_inc(sems[key], amt)
                for k, v in extra:
                    h.wait_ge(sems[k], v)
            hmap[eng](body)


class Arena:
    def __init__(self, ap_f32, nwords):
        self.ap = ap_f32
        self.n = nwords
        self.off = 0

    def reset(self, off=0):
        self.off = off

    def f32(self, words):
        assert self.off + words <= self.n, (self.off, words, self.n)
        v = self.ap[:, self.off:self.off + words]
        self.off += words
        return v

    def bf16(self, elems):
        assert elems % 2 == 0
        return self.f32(elems // 2).bitcast(BF16)


def _r3(ap, a, b):
    return ap.rearrange("p (a b) -> p a b", a=a, b=b)


def build_program(dbg=None):
    nc = bass.Bass("TRN2", target_bir_lowering=False)
    P = Prog()
    dt_in = {}

    def din(name, shape, dt=F32):
        dt_in[name] = nc.dram_tensor(name, list(shape), dt, kind="ExternalInput").ap()
        return dt_in[name]

    x_d = din("x", [L, D])
    ctx_d = din("ctx", [CTX, D])
    cv_d = din("cv", [128, KC * 2])
    wada_d = din("wada", [24, 128, KC * 512])
    vec_d = din("vecs", [128, 96 + 48])
    conv_d = din("convv", [128, NCC * 10])
    cst_d = din("consts", [128, 128 * 3 + 512])
    rope_d = din("rope", [128, 2 * L])
    dl_d = din("dlog", [1, 16])
    gu_d = din("gup", [17, 2 * 1024])
    rg_d = din("rng", [1, 2048])
    gg_d = din("gng", [1, 2048])
    wr_d = din("w_ret", [RH * 4, 128, KC * 256])
    wg_d = din("w_gla", [GH * 6, 128, KC * 256])
    wlr_d = din("w_lr", [128, KC * 32])
    wmg_d = din("w_mg", [16, 128, KC * 256])
    wrp_d = din("w_rp", [16, 128, KC * 128])
    wgp_d = din("w_gp", [16, 128, KC * 128])
    wo_d = din("w_o", [16, 128, KC * 128])
    wup_d = din("w_up", [NCC, 128, KC * 256])
    wdn_d = din("w_dn", [NCC, 128, D])
    out_d = nc.dram_tensor("out", [1024, D], F32, kind="ExternalOutput").ap()
    R_d = nc.dram_tensor("r_scr", [TOK, 4096], BF16).ap()
    ST_d = nc.dram_tensor("st_scr", [12 * 2, 128, 1024], F32).ap()
    dbg_d = {}
    if dbg:
        for name, shape in dbg.items():
            dbg_d[name] = nc.dram_tensor("dbg_" + name, list(shape), F32, kind="ExternalOutput").ap()

    with ExitStack() as es:
        def sb(name, shape, dt=F32):
            return es.enter_context(nc.sbuf_tensor(name, list(shape), dt))
        HT_t = sb("HT", [128, 18432], F32)
        BIG_t = sb("BIG", [128, 18432], F32)
        WB_t = sb("WB", [128, 8192], F32)
        GA_t = sb("GA", [128, 3328], F32)
        MISC_t = sb("MISC", [128, 2560], F32)
        ident_f = sb("ident_f", [128, 128])
        cmask = sb("cmask", [128, 256 + 512])
        ident_b = sb("ident_b", [128, 128], BF16)
        mf_b = sb("mf_b", [128, 128], BF16)
        mb_b = sb("mb_b", [128, 128], BF16)
        ones_b = sb("ones_b", [128, 128], BF16)
        modT = sb("modT", [128, 192])
        vecs = sb("vecs_s", [128, 144])
        A1 = sb("A1", [128, 48])
        convv = sb("convv_s", [128, NCC * 10])
        lg = sb("lg", [128, 64])
        cvs = sb("cvs", [128, 64])
        PS = [es.enter_context(nc.psum_tensor("ps%d" % i, [128, 512], F32)) for i in range(8)]
        psr = [Region("ps%d" % i, excl=True) for i in range(8)]
        PSb = [p[:, :].bitcast(BF16) for p in PS]

        HT = HT_t[:, :]
        BIG = BIG_t[:, :]
        hT = HT.bitcast(BF16)
        hTo3 = _r3(hT[:, 0:18432], KC, TOK)
        hTx3 = _r3(hT[:, 18432:36864], KC, TOK)

        def hcol(k, c0, n):
            if c0 < TOK:
                assert c0 + n <= TOK
                return hTo3[:, k, c0:c0 + n]
            return hTx3[:, k, c0 - TOK:c0 - TOK + n]
        hT_reg = [Region("hT%d" % t) for t in range(18)]
        consts_reg = Region("consts")
        mod_reg = Region("mod")

        P.dma("sp", "ld", ident_f[:, :], cst_d[:, 0:128], writes=[consts_reg])
        P.dma("sp", "ld", cmask[:, :], cst_d[:, 128:896], writes=[consts_reg])
        P.dma("sp", "ld", vecs[:, :], vec_d[:, :], writes=[consts_reg])
        P.dma("sp", "ld", convv[:, :], conv_d[:, :], writes=[consts_reg])
        P.dma("sp", "ld", cvs[:, 0:32], cv_d[:, :], writes=[consts_reg])
        P.dma("sp", "ld", lg[:, 0:16], dl_d[0:1, :].partition_broadcast(128), writes=[consts_reg])
        c2 = Region("c2")
        P.emit("dve", lambda v: v.tensor_copy(ident_b[:, :], ident_f[:, :]), [consts_reg], [c2])
        P.emit("dve", lambda v: v.tensor_copy(mf_b[:, :], cmask[:, 0:128]), [consts_reg], [c2])
        P.emit("dve", lambda v: v.tensor_copy(mb_b[:, :], cmask[:, 128:256]), [consts_reg], [c2])
        P.emit("dve", lambda v: v.memset(ones_b[:, :], 1.0), [], [c2])
        epsc = lg[:, 48:56]
        P.emit("dve", lambda v: v.memset(epsc[:, 0:1], EPS), [], [c2])
        P.emit("dve", lambda v: v.memset(epsc[:, 1:2], EPS_RO), [], [c2])
        P.emit("dve", lambda v: v.memset(epsc[:, 2:3], 1.0), [], [c2])
        iotas = _r3(cmask[:, 256:768], 4, 128)
        bada = vecs[:, 0:96]
        n1g = vecs[:, 96:112]
        n2g = vecs[:, 112:128]
        fgv = vecs[:, 128:144]
        ONE = epsc[:, 2:3]

        def rstd_act(out, in_, scale, eps_ap, reads, writes):
            P.emit("act", lambda s: s.activation(out=out, in_=in_, func=AF.Ln, scale=scale, bias=eps_ap), reads + [c2], writes)
            P.emit("act", lambda s: s.activation(out=out, in_=out, func=AF.Exp, scale=-0.5), writes, writes)

        cs_b = MISC_t[:, 0:16].bitcast(BF16)
        cs_reg = Region("cs")
        P.emit("act", lambda s: s.activation(out=cs_b, in_=cvs[:, 0:32], func=AF.Silu), [consts_reg], [cs_reg])
        cs3 = _r3(cs_b, KC, 2)
        wb_reg = [Region("wb%d" % i) for i in range(4)]
        WBb = WB_t[:, :].bitcast(BF16)
        modT3 = _r3(modT[:, :], 96, 2)
        mod_reg2 = Region("mod2")

        def ada_group(g):
            bi = g % 2
            wv = WBb[:, bi * 8192:(bi + 1) * 8192]
            wregs = wb_reg[bi * 2:bi * 2 + 2]
            P.dma("pool", "w", wv, wada_d[g], writes=wregs)
            w4 = wv.rearrange("p (j k c) -> p j k c", j=4, k=KC, c=128)
            pb = g % 2
            for jj in range(4):
                P.mm(PS[pb][:, jj * 2:jj * 2 + 2],
                     [(w4[:, jj, k, :], cs3[:, k, :]) for k in range(KC)],
                     wregs + [cs_reg], [psr[pb]])
            P.emit("dve", lambda v: v.tensor_tensor(
                out=modT3[:, g * 4:g * 4 + 4, :], in0=_r3(PS[pb][:, 0:8], 4, 2),
                in1=bada[:, g * 4:g * 4 + 4].unsqueeze(2).to_broadcast([128, 4, 2]), op=ALU.add),
                [psr[pb], consts_reg], [mod_reg if g < 8 else mod_reg2])
        for g in range(8):
            ada_group(g)
        sh1 = modT3[:, 0:16, 0]
        sc1 = modT3[:, 16:32, 0]
        g1 = modT3[:, 32:48, 0]
        sh2 = modT3[:, 48:64, 0]
        sc2 = modT3[:, 64:80, 0]
        g2 = modT3[:, 80:96, 0]
        csh1 = modT3[:, 0:16, 1]
        csc1 = modT3[:, 16:32, 1]
        a_reg = Region("A")
        P.emit("dve", lambda v: v.scalar_tensor_tensor(out=A1[:, 0:16], in0=sc1, scalar=1.0, in1=n1g, op0=ALU.add, op1=ALU.mult),
               [mod_reg, consts_reg], [a_reg])
        P.emit("dve", lambda v: v.scalar_tensor_tensor(out=A1[:, 16:32], in0=csc1, scalar=1.0, in1=n1g, op0=ALU.add, op1=ALU.mult),
               [mod_reg, consts_reg], [a_reg])
        A2 = A1[:, 32:48]
        a2_reg = Region("A2")

        bigA = Arena(BIG, 18432)
        xt = [bigA.f32(2048) for _ in range(2)]
        xs = [bigA.f32(2048) for _ in range(2)]
        xt_reg = [Region("xt%d" % i) for i in range(2)]
        xs_reg = [Region("xs%d" % i) for i in range(2)]
        junk = bigA.f32(2048)
        junk_reg = Region("junk")
        st_small = MISC_t[:, 64:128]
        st_reg = [Region("st%d" % i) for i in range(2)]
        for t in range(18):
            bi = t % 2
            src = x_d[t * 128:(t + 1) * 128, :] if t < 16 else ctx_d[(t - 16) * 128:(t - 15) * 128, :]
            P.dma("sp", "ld", xt[bi], src, writes=[xt_reg[bi]])
            ssq = st_small[:, bi * 4:bi * 4 + 1]
            rstd = st_small[:, bi * 4 + 1:bi * 4 + 2]
            P.emit("dve", lambda v, ssq=ssq: v.memset(ssq, 0.0), [], [st_reg[bi]])
            P.emit("act", lambda s, bi=bi, ssq=ssq: s.activation(out=junk, in_=xt[bi], func=AF.Square, accum_out=ssq),
                   [xt_reg[bi]], [junk_reg, st_reg[bi]])
            rstd_act(rstd, ssq, 1.0 / D, epsc[:, 0:1], [st_reg[bi]], [st_reg[bi]])
            P.emit("dve", lambda v, bi=bi, rstd=rstd: v.tensor_scalar(out=xs[bi], in0=xt[bi], scalar1=rstd, scalar2=None, op0=ALU.mult),
                   [xt_reg[bi], st_reg[bi]], [xs_reg[bi]])
            Acol = A1[:, 0:16] if t < 16 else A1[:, 16:32]
            Bcol = sh1 if t < 16 else csh1
            for q in range(4):
                pb = 2 + (t * 4 + q) % 4
                for kk in range(4):
                    k = q * 4 + kk
                    P.transpose(PS[pb][:, kk * 128:(kk + 1) * 128], xs[bi][:, k * 128:(k + 1) * 128], ident_f[:, :],
                                [xs_reg[bi], consts_reg], [psr[pb]])
                for kk in range(4):
                    k = q * 4 + kk
                    dst = hcol(k, t * 128, 128)
                    if kk % 2 == 0:
                        P.emit("act", lambda s, pb=pb, kk=kk, k=k, dst=dst, Acol=Acol, Bcol=Bcol: s.activation(
                            out=dst, in_=PS[pb][:, kk * 128:(kk + 1) * 128], func=AF.Identity,
                            scale=Acol[:, k:k + 1], bias=Bcol[:, k:k + 1]), [psr[pb], a_reg, mod_reg], [hT_reg[t]])
                    else:
                        P.emit("dve", lambda v, pb=pb, kk=kk, k=k, dst=dst, Acol=Acol, Bcol=Bcol: v.tensor_scalar(
                            out=dst, in0=PS[pb][:, kk * 128:(kk + 1) * 128], scalar1=Acol[:, k:k + 1], scalar2=Bcol[:, k:k + 1],
                            op0=ALU.mult, op1=ALU.add), [psr[pb], a_reg, mod_reg], [hT_reg[t]])
            if t < 16:
                ada_group(8 + t)
        P.emit("dve", lambda v: v.scalar_tensor_tensor(out=A2, in0=sc2, scalar=1.0, in1=n2g, op0=ALU.add, op1=ALU.mult),
               [mod_reg2, consts_reg], [a2_reg])

        def finish():
            final_waits = [("out", P.cnt.get("out", 0))]
            sem_keys = sorted(P.cnt.keys())
            sems = {k: es.enter_context(nc.semaphore("s_" + k)) for k in sem_keys}
            with nc.Block() as block:
                P.replay(block, sems, final_waits)

        def dump_f32(name, ap, n):
            P.barrier()
            stage = BIG[:, 0:n] if ap.dtype != F32 else None
            if stage is not None:
                rg = Region("dump")
                P.emit("dve", lambda v: v.tensor_copy(stage, ap), [], [rg])
                P.dma("sp", "out", dbg_d[name][:, :], stage, reads=[rg])
            else:
                P.dma("sp", "out", dbg_d[name][:, :], ap)
            P.barrier()

        if STOP_AFTER == "B":
            P.barrier()
            dq = bigA.f32(2304)
            dq_reg = Region("dq")
            for k in range(KC):
                for half, src3 in enumerate((hTo3, hTx3)):
                    P.emit("dve", lambda v, k=k, src3=src3, half=half: v.tensor_copy(dq[:, half * TOK:(half + 1) * TOK], src3[:, k, :]),
                           hT_reg, [dq_reg])
                P.dma("sp", "out", dbg_d["hT"][:, k * TOKA:(k + 1) * TOKA], dq, reads=[dq_reg])
            P.dma("sp", "out", dbg_d["modT"][:, :], modT[:, :], reads=[mod_reg])
            finish()
            return nc

        wb_i = [0]

        def load_w(src, nbf=4096, nslots=4):
            i = wb_i[0] % nslots
            wb_i[0] += 1
            v = WBb[:, i * 4096:i * 4096 + nbf]
            P.dma("pool", "w", v, src, writes=[wb_reg[i]])
            return v, wb_reg[i]
        ip_rot = [0]

        def ipb():
            b = ip_rot[0] % 3
            ip_rot[0] += 1
            return b

        def rope_evac(pb0, pb1, n, dst0, dst1, cosv, sinv, t1, t2, treg, tabreg, dreg):
            p0 = PS[pb0][:, 0:n]
            p1 = PS[pb1][:, 0:n]
            P.emit("dve", lambda v: v.tensor_tensor(out=t1, in0=p0, in1=cosv, op=ALU.mult), [psr[pb0], tabreg], [treg])
            P.emit("dve", lambda v: v.tensor_tensor(out=t2, in0=p1, in1=sinv, op=ALU.mult), [psr[pb1], tabreg], [treg])
            P.emit("dve", lambda v: v.tensor_tensor(out=dst0, in0=t1, in1=t2, op=ALU.subtract), [treg], [dreg])
            P.emit("dve", lambda v: v.tensor_tensor(out=t1, in0=p1, in1=cosv, op=ALU.mult), [psr[pb1], tabreg, treg], [treg])
            P.emit("dve", lambda v: v.tensor_tensor(out=t2, in0=p0, in1=sinv, op=ALU.mult), [psr[pb0], tabreg, treg], [treg])
            P.emit("dve", lambda v: v.tensor_tensor(out=dst1, in0=t1, in1=t2, op=ALU.add), [treg], [dreg])

        GAb = GA_t[:, :].bitcast(BF16)
        gaT3 = _r3(GAb[:, 0:4608], 2, TOKA)
        guT = GAb[:, 4608:6656]
        ga_reg = Region("gaT")
        gu_reg = Region("guT")

        def gla_z(h, tile, dr, zb):
            for c in range(2):
                col = dr * 1024 + h * 256 + c * 128
                P.mm(PS[zb][:, c * 128:(c + 1) * 128],
                     [(guT[0:17, col:col + 128], gaT3[0:17, dr, tile * 128:(tile + 1) * 128])], [gu_reg, ga_reg], [psr[zb]])

        def gla_rest(T, dr, need_q, zb):
            E, Lb, ek, eb, enb, nb, sdec, dreg, elreg, nreg = T["E"], T["L"], T["ek"], T["eb"], T["enb"], T["nb"], T["sdec"], T["dreg"], T["elreg"], T["nreg"]
            P.emit("act", lambda s: s.activation(out=E, in_=PS[zb][:, 0:256], func=AF.Exp, scale=-1.0), [psr[zb]], [elreg])
            P.emit("act", lambda s: s.activation(out=E, in_=E, func=AF.Ln, bias=ONE), [elreg, c2], [elreg])
            E3 = _r3(E, 2, 128)
            L3 = _r3(Lb, 2, 128)
            for c in range(2):
                if dr == 0:
                    P.emit("dve", lambda v, c=c: v.tensor_tensor_scan(L3[:, c, :], E3[:, c, :], E3[:, c, :], 0.0, ALU.add, ALU.bypass),
                           [elreg], [elreg])
                else:
                    P.emit("dve", lambda v, c=c: v.tensor_tensor_scan(L3[:, c, ::-1], E3[:, c, ::-1], E3[:, c, ::-1], 0.0, ALU.add, ALU.bypass),
                           [elreg], [elreg])
            li = 127 if dr == 0 else 0
            Llast = L3[:, :, li:li + 1]
            P.emit("dve", lambda v: v.tensor_scalar(out=nb.unsqueeze(2), in0=Llast, scalar1=-1.0 / 16, scalar2=None, op0=ALU.mult), [elreg], [nreg])
            ek3 = _r3(ek, 2, 128)
            for c in range(2):
                P.emit("act", lambda s, c=c: s.activation(out=ek3[:, c, :], in_=L3[:, c, :], func=AF.Exp, scale=1.0 / 16, bias=nb[:, c:c + 1]),
                       [elreg, nreg], [dreg])
            P.emit("act", lambda s: s.activation(out=sdec.unsqueeze(2), in_=Llast, func=AF.Exp, scale=-1.0 / 16), [elreg], [nreg])
            if need_q:
                P.emit("act", lambda s: s.activation(out=eb, in_=Lb, func=AF.Exp, scale=-1.0 / 16), [elreg], [dreg])
                P.emit("act", lambda s: s.activation(out=enb, in_=Lb, func=AF.Exp, scale=1.0 / 16), [elreg], [dreg])
            return ek3, [sdec[:, 0:1], sdec[:, 1:2]]

        def k_mult(T, kT_tile3, kreg, ek3, dcy_regs, keng="dve"):
            khT3 = _r3(T["khT"], 2, 128)
            P.emit(keng, lambda e: e.tensor_tensor(out=khT3, in0=kT_tile3, in1=ek3, op=ALU.mult), [kreg] + dcy_regs, [T["khreg"]])

        def k_tr(T, cb):
            khT3 = _r3(T["khT"], 2, 128)
            for c in range(2):
                P.transpose(PSb[cb][:, 768 + c * 128:768 + (c + 1) * 128], khT3[:, c, :], ident_b[:, :], [T["khreg"], c2], [psr[cb]])

        def k_copy(T, cb):
            kh = T["kh"]
            P.emit("act", lambda s: s.activation(out=kh, in_=PSb[cb][:, 768:1024], func=AF.Copy), [psr[cb]], [T["khreg"]])

        def supdate(T, v_tile, vreg, sdecs, dcy_regs, S32, S16, sreg, dv, same_scalar):
            kh, kreg2 = T["kh"], T["khreg"]
            S3 = _r3(S32, 2, 512)
            if same_scalar and dv == 256:
                for c in range(2):
                    P.mm(PS[7][:, c * 256:(c + 1) * 256], [(kh[:, c * 128:(c + 1) * 128], v_tile)], [kreg2, vreg], [psr[7]])
                P.emit("dve", lambda v: v.scalar_tensor_tensor(
                    out=S3[:, :, 0:256], in0=S3[:, :, 0:256], scalar=sdecs[0], in1=_r3(PS[7][:, 0:512], 2, 256), op0=ALU.mult, op1=ALU.add),
                    [psr[7], sreg] + dcy_regs, [sreg])
            else:
                for c in range(2):
                    P.mm(PS[7][:, 0:dv], [(kh[:, c * 128:(c + 1) * 128], v_tile)], [kreg2, vreg], [psr[7]])
                    P.emit("dve", lambda v, c=c: v.scalar_tensor_tensor(
                        out=S3[:, c, 0:dv], in0=S3[:, c, 0:dv], scalar=sdecs[c], in1=PS[7][:, 0:dv], op0=ALU.mult, op1=ALU.add),
                        [psr[7], sreg] + dcy_regs, [sreg])
            if S16 is not None:
                S163 = _r3(S16[0], 2, 512)
                P.emit("act", lambda s: s.activation(out=S163[:, :, 0:dv], in_=S3[:, :, 0:dv], func=AF.Copy), [sreg], [S16[1]])

        lgam_reg = Region("lgam")
        P.emit("act", lambda s: s.activation(out=lg[:, 16:32], in_=lg[:, 0:16], func=AF.Exp, scale=-1.0), [consts_reg], [lgam_reg])
        P.emit("act", lambda s: s.activation(out=lg[:, 16:32], in_=lg[:, 16:32], func=AF.Ln, bias=ONE), [lgam_reg, c2], [lgam_reg])
        P.emit("dve", lambda v: v.tensor_scalar(out=lg[:, 16:32], in0=lg[:, 16:32], scalar1=-1.0, scalar2=None, op0=ALU.mult), [lgam_reg], [lgam_reg])
        P.emit("act", lambda s: s.activation(out=lg[:, 32:48], in_=lg[:, 16:32], func=AF.Exp, scale=128.0), [lgam_reg], [lgam_reg])

        def ret_tables(rtab3, rtreg, h, dirs_kinds):
            for dr, kind in dirs_kinds:
                col = 16 + dr * 8 + h
                if kind == 2:
                    io = iotas[:, 2 if dr == 0 else 3, :]
                else:
                    io = iotas[:, 0 if dr == 0 else 1, :]
                if kind == 1:
                    P.emit("dve", lambda v, col=col: v.tensor_scalar(out=lg[:, 56:57], in0=lg[:, col:col + 1], scalar1=-1.0, scalar2=None, op0=ALU.mult),
                           [lgam_reg], [rtreg])
                    P.emit("act", lambda s, dr=dr, kind=kind, io=io: s.activation(out=rtab3[:, dr * 3 + kind, :], in_=io, func=AF.Exp, scale=lg[:, 56:57]),
                           [rtreg, consts_reg], [rtreg])
                else:
                    P.emit("act", lambda s, dr=dr, kind=kind, io=io, col=col: s.activation(out=rtab3[:, dr * 3 + kind, :], in_=io, func=AF.Exp, scale=lg[:, col:col + 1]),
                           [lgam_reg, consts_reg], [rtreg])

        P.barrier()
        bigA.reset(0)
        c0_kT = [bigA.bf16(2 * TOK) for _ in range(2)]
        c0_v = [bigA.bf16(9 * 512) for _ in range(2)]
        c0k_reg = [Region("c0k%d" % i) for i in range(2)]
        c0v_reg = [Region("c0v%d" % i) for i in range(2)]
        ropeO = bigA.f32(2 * 896)
        ropeO_reg = Region("ropeO")
        P.dma("sp", "ld", ropeO[:, 0:896], rope_d[:, TOK:L], writes=[ropeO_reg])
        P.dma("sp", "ld", ropeO[:, 896:1792], rope_d[:, L + TOK:2 * L], writes=[ropeO_reg])

        def mk_big(A, tag, shared):
            T = {}
            T["E"] = shared[0]
            T["L"] = shared[1]
            T["elreg"] = shared[2]
            T["ek"] = A.f32(256)
            T["eb"] = A.f32(256)
            T["enb"] = A.f32(256)
            T["dreg"] = Region("dcy" + tag)
            return T

        def mk_small(A, A4, tag):
            T = {}
            T["nb"] = A4.f32(2)
            T["sdec"] = A4.f32(2)
            T["nreg"] = Region("nb" + tag)
            T["khT"] = A.bf16(256)
            T["kh"] = A.bf16(256)
            T["khreg"] = Region("kh" + tag)
            T["qt"] = A.bf16(256)
            T["kt"] = A.bf16(256)
            T["PT"] = A.bf16(128)
            T["qreg"] = Region("qt" + tag)
            T["preg"] = Region("pt" + tag)
            return T

        def mk_temps2(Abig, Asmall, A4, tag, shared):
            out = []
            for dr in range(2):
                big = mk_big(Abig, "%s_%d" % (tag, dr), shared)
                row = []
                for par in range(2):
                    Asm = Asmall[dr * 2 + par] if isinstance(Asmall, list) else Asmall
                    T = dict(big)
                    T.update(mk_small(Asm, A4, "%s_%d_%d" % (tag, dr, par)))
                    row.append(T)
                out.append(row)
            return out
        sh0 = (bigA.f32(256), bigA.f32(256), Region("el0"))
        T0 = mk_temps2(bigA, bigA, bigA, "c0", sh0)
        c0_S32 = [bigA.f32(1024) for _ in range(2)]
        c0_Sreg = [Region("c0S%d" % i) for i in range(2)]
        c0_rtab = bigA.f32(6 * 128)
        c0_rtab3 = _r3(c0_rtab, 6, 128)
        c0_rtreg = Region("c0rt")
        c0_t1 = bigA.f32(128)
        c0_t2 = bigA.f32(128)
        c0_treg = Region("c0t")
        st_dreg = [[Region("std%d_%d" % (i, j)) for j in range(2)] for i in range(12)]

        P.emit("dve", lambda v: v.memset(GAb[0:32, 0:4608], 1.0), [], [ga_reg])
        if EXP2:
            P.barrier()
        P.dma("pool", "w", guT[0:17, :], gu_d[:, :], writes=[gu_reg])
        wl, wlreg = load_w(wlr_d[:, :], nbf=512)
        wl3 = _r3(wl, KC, 32)
        if EXP1:
            P.barrier()
        for dr in range(2):
            for blk in range(6):
                pb = ipb()
                c0 = blk * 384
                P.mm(PS[pb][0:16, 0:384], [(wl3[:, k, dr * 16:(dr + 1) * 16], hcol(k, c0, 384)) for k in range(KC)],
                     [wlreg] + hT_reg[blk * 3:blk * 3 + 3], [psr[pb]])
                P.emit("act", lambda s, pb=pb, dr=dr, c0=c0: s.activation(out=gaT3[0:16, dr, c0:c0 + 384], in_=PS[pb][0:16, 0:384], func=AF.Copy),
                       [psr[pb]], [ga_reg])

        step_i = [0]

        def c0_inproj(hd):
            is_ret = hd < 8
            h = hd if is_ret else hd - 8
            dv = 256 if is_ret else 512
            s = hd % 2
            if is_ret:
                Wk, wkreg = load_w(wr_d[h * 4 + 1])
                Wv = [load_w(wr_d[h * 4 + 2])]
            else:
                Wk, wkreg = load_w(wg_d[h * 6 + 1])
                Wv = [load_w(wg_d[h * 6 + 2]), load_w(wg_d[h * 6 + 3])]
            Wk3 = _r3(Wk, KC, 256)
            kT3 = _r3(c0_kT[s], 2, TOK)
            v3 = _r3(c0_v[s], 9, 512)
            for blk in range(3):
                if blk > 0:
                    yield
                c0 = TOK + blk * 384
                pbs = [ipb(), ipb()]
                for c in range(2):
                    P.mm(PS[pbs[c]][:, 0:384], [(Wk3[:, k, c * 128:(c + 1) * 128], hcol(k, c0, 384)) for k in range(KC)],
                         [wkreg] + hT_reg[9 + blk * 3:12 + blk * 3], [psr[pbs[c]]])
                for tt in range(3):
                    lt = blk * 3 + tt
                    tile = 9 + lt
                    sl = slice(tt * 128, (tt + 1) * 128)
                    if is_ret and tile < 16:
                        rc = slice(lt * 128, (lt + 1) * 128)
                        cosv = ropeO[:, 0:896][:, rc]
                        sinv = ropeO[:, 896:1792][:, rc]
                        p0 = PS[pbs[0]][:, sl]
                        p1 = PS[pbs[1]][:, sl]
                        d0 = kT3[:, 0, lt * 128:(lt + 1) * 128]
                        d1 = kT3[:, 1, lt * 128:(lt + 1) * 128]
                        P.emit("dve", lambda v, p0=p0, cosv=cosv: v.tensor_tensor(out=c0_t1, in0=p0, in1=cosv, op=ALU.mult), [psr[pbs[0]], ropeO_reg], [c0_treg])
                        P.emit("dve", lambda v, p1=p1, sinv=sinv: v.tensor_tensor(out=c0_t2, in0=p1, in1=sinv, op=ALU.mult), [psr[pbs[1]], ropeO_reg], [c0_treg])
                        P.emit("dve", lambda v, d0=d0: v.tensor_tensor(out=d0, in0=c0_t1, in1=c0_t2, op=ALU.subtract), [c0_treg], [c0k_reg[s]])
                        P.emit("dve", lambda v, p1=p1, cosv=cosv: v.tensor_tensor(out=c0_t1, in0=p1, in1=cosv, op=ALU.mult), [psr[pbs[1]], ropeO_reg, c0_treg], [c0_treg])
                        P.emit("dve", lambda v, p0=p0, sinv=sinv: v.tensor_tensor(out=c0_t2, in0=p0, in1=sinv, op=ALU.mult), [psr[pbs[0]], ropeO_reg, c0_treg], [c0_treg])
                        P.emit("dve", lambda v, d1=d1: v.tensor_tensor(out=d1, in0=c0_t1, in1=c0_t2, op=ALU.add), [c0_treg], [c0k_reg[s]])
                    else:
                        for c in range(2):
                            P.emit("act", lambda sc, c=c, sl=sl, lt=lt, pbc=pbs[c]: sc.activation(
                                out=kT3[:, c, lt * 128:(lt + 1) * 128], in_=PS[pbc][:, sl], func=AF.Copy), [psr[pbs[c]]], [c0k_reg[s]])
            for lt in range(9):
                yield
                tile = 9 + lt
                pb = ipb()
                for i, (Wvv, wvreg) in enumerate(Wv):
                    Wv3 = _r3(Wvv, KC, 256)
                    P.mm(PS[pb][:, i * 256:(i + 1) * 256], [(hcol(k, tile * 128, 128), Wv3[:, k, :]) for k in range(KC)],
                         [wvreg, hT_reg[tile]], [psr[pb]])
                P.emit("act", lambda sc, lt=lt, pb=pb: sc.activation(out=v3[:, lt, 0:dv], in_=PS[pb][:, 0:dv], func=AF.Copy), [psr[pb]], [c0v_reg[s]])
            yield

        def c0_scan(hd, filler):
            is_ret = hd < 8
            h = hd if is_ret else hd - 8
            dv = 256 if is_ret else 512
            s = hd % 2
            kT3 = _r3(c0_kT[s], 2, TOK)
            v3 = _r3(c0_v[s], 9, 512)
            if is_ret:
                ret_tables(c0_rtab3, c0_rtreg, h, [(0, 2), (1, 2)])
            for dr in range(2):
                P.emit("dve", lambda v, S=c0_S32[dr]: v.memset(S, 0.0), [], [c0_Sreg[dr]])
            orders = [[16, 17], [17, 16, 15, 14, 13, 12, 11, 10, 9]]
            ZB = [3, 5]
            info = {}

            def live(i):
                return [dr for dr in range(2) if 0 <= i < len(orders[dr])]

            def stA(i):
                if not is_ret:
                    for dr in live(i):
                        gla_z(h, orders[dr][i], dr, ZB[dr])

            def stB(i):
                for dr in live(i):
                    tile = orders[dr][i]
                    lt = tile - 9
                    T = T0[dr][i % 2]
                    if is_ret:
                        ek3 = c0_rtab3[:, dr * 3 + 2, :].unsqueeze(1).to_broadcast([128, 2, 128])
                        col = 32 + dr * 8 + h
                        sdecs = [lg[:, col:col + 1], lg[:, col:col + 1]]
                        dregs = [c0_rtreg, lgam_reg]
                    else:
                        ek3, sdecs = gla_rest(T, dr, False, ZB[dr])
                        dregs = [T["dreg"], T["nreg"]]
                    k_mult(T, kT3[:, :, lt * 128:(lt + 1) * 128], c0k_reg[s], ek3, dregs)
                    info[(i, dr)] = (lt, sdecs, dregs)

            def stC(i):
                for dr in live(i):
                    k_tr(T0[dr][i % 2], ZB[dr])

            def stD(i):
                for dr in live(i):
                    k_copy(T0[dr][i % 2], ZB[dr])

            def stE(i):
                for dr in live(i):
                    lt, sdecs, dregs = info[(i, dr)]
                    supdate(T0[dr][i % 2], v3[:, lt, 0:dv], c0v_reg[s], sdecs, dregs, c0_S32[dr], None, c0_Sreg[dr], dv, is_ret)
            stA(0)
            stB(0)
            for i in range(9):
                stA(i + 1)
                stB(i + 1)
                stC(i)
                filler()
                stD(i)
                stE(i)
                filler()
            for dr in range(2):
                P.dma("sp", "st", ST_d[hd * 2 + dr][:, :], c0_S32[dr], reads=[c0_Sreg[dr]], writes=[st_dreg[hd][dr]])

        gens0 = [None]

        def filler0():
            g = gens0[0]
            if g is None:
                return
            try:
                next(g)
            except StopIteration:
                gens0[0] = None

        def drain0():
            while gens0[0] is not None:
                filler0()
        gens0[0] = c0_inproj(0)
        drain0()
        for hd_ in range(12):
            gens0[0] = c0_inproj(hd_ + 1) if hd_ + 1 < 12 else None
            c0_scan(hd_, filler0)
            drain0()

        if STOP_AFTER == "C0":
            P.barrier()
            for i in range(24):
                P.dma("sp", "out", dbg_d["st"][i * 128:(i + 1) * 128, :], ST_d[i][:, :])
            rgd = Region("dmpg")
            P.emit("dve", lambda v: v.tensor_copy(BIG[0:32, 0:4608], GAb[0:32, 0:4608]), [], [rgd])
            P.dma("sp", "out", dbg_d["ga"][:, :], BIG[0:32, 0:4608], reads=[rgd])
            finish()
            return nc

        P.barrier()
        bigA.reset(0)
        sets = []
        for s in range(2):
            st = {"qT": bigA.bf16(2 * TOK), "kT": bigA.bf16(2 * TOK), "v": bigA.bf16(9 * 512), "gs": bigA.bf16(9 * 512),
                  "qreg": Region("q%d" % s), "kreg": Region("k%d" % s), "vreg": Region("v%d" % s), "greg": Region("g%d" % s)}
            sets.append(st)
        o_f = bigA.f32(9 * 512)
        o_f3 = _r3(o_f, 9, 512)
        of_reg = Region("o_f")
        cA = Arena(HT[:, 9216:18432], 9216)
        mA = Arena(MISC_t[:, 128:2560], 2432)
        ropeN = cA.f32(2 * TOK)
        ropeN_reg = Region("ropeN")
        P.dma("sp", "ld", ropeN[:, 0:TOK], rope_d[:, 0:TOK], writes=[ropeN_reg])
        P.dma("sp", "ld", ropeN[:, TOK:2 * TOK], rope_d[:, L:L + TOK], writes=[ropeN_reg])
        shC = (cA.f32(256), cA.f32(256), Region("elC"))
        gaHole = [Arena(GA_t[:, 576:1152], 576), Arena(GA_t[:, 1728:2304], 576)]
        TC = mk_temps2(cA, [cA, gaHole[0], cA, gaHole[1]], mA, "c", shC)
        rtab = cA.f32(6 * 128)
        rtab3 = _r3(rtab, 6, 128)
        rtreg = Region("rt")
        S32 = [WB_t[:, 6144 + i * 1024:6144 + (i + 1) * 1024] for i in range(2)]
        Sreg = [Region("S32_%d" % i) for i in range(2)]
        S16 = [(cA.bf16(1024), Region("S16_%d" % i)) for i in range(2)]
        rp_t1 = cA.f32(384)
        rp_t2 = cA.f32(384)
        rp_reg = Region("rp")
        otot = mA.f32(512)
        ntmp = mA.f32(512)
        ro_reg = Region("ro")
        rb = [mA.bf16(512) for _ in range(2)]
        rb_reg = [Region("rb%d" % i) for i in range(2)]
        stats = mA.f32(16)
        silt = mA.f32(512)
        sil_reg = Region("sil")
        grow = [cA.f32(512) for _ in range(2)]
        grow_reg = [Region("grow%d" % i) for i in range(2)]
        R_reg = Region("R_d")

        def in_proj(hd, s):
            is_ret = hd < 8
            h = hd if is_ret else hd - 8
            dv = 256 if is_ret else 512
            st = sets[s]
            qT3 = _r3(st["qT"], 2, TOK)
            kT3 = _r3(st["kT"], 2, TOK)
            v3 = _r3(st["v"], 9, 512)
            gs3 = _r3(st["gs"], 9, 512)
            base = wr_d if is_ret else wg_d
            nsub = 4 if is_ret else 6
            gsrc = (rg_d if is_ret else gg_d)[0:1, h * dv:(h + 1) * dv]
            P.dma("sp", "ld", grow[s][:, 0:dv], gsrc.partition_broadcast(128), writes=[grow_reg[s]])
            for qi, (dst3, dreg) in enumerate(((qT3, st["qreg"]), (kT3, st["kreg"]))):
                W, wreg = load_w(base[h * nsub + qi], nslots=3)
                W3 = _r3(W, KC, 256)
                for blk in range(3):
                    c0 = blk * 384
                    pbs = [ipb(), ipb()]
                    for c in range(2):
                        P.mm(PS[pbs[c]][:, 0:384], [(W3[:, k, c * 128:(c + 1) * 128], hcol(k, c0, 384)) for k in range(KC)],
                             [wreg] + hT_reg[blk * 3:blk * 3 + 3], [psr[pbs[c]]])
                    if is_ret:
                        rope_evac(pbs[0], pbs[1], 384, dst3[:, 0, c0:c0 + 384], dst3[:, 1, c0:c0 + 384],
                                  ropeN[:, c0:c0 + 384], ropeN[:, TOK + c0:TOK + c0 + 384], rp_t1, rp_t2, rp_reg, ropeN_reg, dreg)
                    else:
                        for c in range(2):
                            P.emit("act", lambda sc, c=c, c0=c0, pbc=pbs[c], dst3=dst3: sc.activation(
                                out=dst3[:, c, c0:c0 + 384], in_=PS[pbc][:, 0:384], func=AF.Copy), [psr[pbs[c]]], [dreg])
                    yield
            vsubs = [2] if is_ret else [2, 3]
            Wv = [load_w(base[h * nsub + i], nslots=3) for i in vsubs]
            for tile in range(9):
                pb = ipb()
                for i, (Wvv, wvreg) in enumerate(Wv):
                    Wv3 = _r3(Wvv, KC, 256)
                    P.mm(PS[pb][:, i * 256:(i + 1) * 256], [(hcol(k, tile * 128, 128), Wv3[:, k, :]) for k in range(KC)],
                         [wvreg, hT_reg[tile]], [psr[pb]])
                P.emit("act", lambda sc, tile=tile, pb=pb: sc.activation(out=v3[:, tile, 0:dv], in_=PS[pb][:, 0:dv], func=AF.Copy),
                       [psr[pb]], [st["vreg"]])
                yield
            gsubs = [3] if is_ret else [4, 5]
            Wg = [load_w(base[h * nsub + i], nslots=3) for i in gsubs]
            for tile in range(9):
                pb = ipb()
                for i, (Wgg, wgreg) in enumerate(Wg):
                    Wg3 = _r3(Wgg, KC, 256)
                    P.mm(PS[pb][:, i * 256:(i + 1) * 256], [(hcol(k, tile * 128, 128), Wg3[:, k, :]) for k in range(KC)],
                         [wgreg, hT_reg[tile]], [psr[pb]])
                P.emit("act", lambda sc, pb=pb: sc.activation(out=silt[:, 0:dv], in_=PS[pb][:, 0:dv], func=AF.Silu), [psr[pb]], [sil_reg])
                P.emit("pool", lambda e, tile=tile: e.tensor_tensor(out=gs3[:, tile, 0:dv], in0=silt[:, 0:dv], in1=grow[s][:, 0:dv], op=ALU.mult),
                       [sil_reg, grow_reg[s]], [st["greg"]])
                yield

        def scan(hd, s, filler):
            is_ret = hd < 8
            h = hd if is_ret else hd - 8
            dv = 256 if is_ret else 512
            st = sets[s]
            qT3 = _r3(st["qT"], 2, TOK)
            kT3 = _r3(st["kT"], 2, TOK)
            v3 = _r3(st["v"], 9, 512)
            gs3 = _r3(st["gs"], 9, 512)
            for dr in range(2):
                P.dma("sp", "st", S32[dr], ST_d[hd * 2 + dr][:, :], reads=[st_dreg[hd][dr]], writes=[Sreg[dr]])
                S3 = _r3(S32[dr], 2, 512)
                S163 = _r3(S16[dr][0], 2, 512)
                P.emit("act", lambda sc, S3=S3, S163=S163: sc.activation(out=S163[:, :, 0:dv], in_=S3[:, :, 0:dv], func=AF.Copy),
                       [Sreg[dr]], [S16[dr][1]])
            if is_ret:
                ret_tables(rtab3, rtreg, h, [(0, 0), (0, 1), (0, 2), (1, 0), (1, 1), (1, 2)])
            ZB = [3, 5]
            OB = [4, 6]
            info = {}

            def tile_of(t, dr):
                return t if dr == 0 else 8 - t

            def stA(t):
                if t < 9 and not is_ret:
                    for dr in range(2):
                        gla_z(h, tile_of(t, dr), dr, ZB[dr])

            def stB(t):
                if t >= 9:
                    return
                for dr in range(2):
                    tile = tile_of(t, dr)
                    T = TC[dr][t % 2]
                    tsl = slice(tile * 128, (tile + 1) * 128)
                    if is_ret:
                        eb3 = rtab3[:, dr * 3 + 0, :].unsqueeze(1).to_broadcast([128, 2, 128])
                        enb3 = rtab3[:, dr * 3 + 1, :].unsqueeze(1).to_broadcast([128, 2, 128])
                        ek3 = rtab3[:, dr * 3 + 2, :].unsqueeze(1).to_broadcast([128, 2, 128])
                        col = 32 + dr * 8 + h
                        sdecs = [lg[:, col:col + 1], lg[:, col:col + 1]]
                        dregs = [rtreg, lgam_reg]
                    else:
                        ek3, sdecs = gla_rest(T, dr, True, ZB[dr])
                        eb3 = _r3(T["eb"], 2, 128)
                        enb3 = _r3(T["enb"], 2, 128)
                        dregs = [T["dreg"], T["nreg"]]
                    qt3 = _r3(T["qt"], 2, 128)
                    kt3 = _r3(T["kt"], 2, 128)
                    P.emit("dve", lambda v, qt3=qt3, eb3=eb3, tsl=tsl: v.tensor_tensor(out=qt3, in0=qT3[:, :, tsl], in1=eb3, op=ALU.mult),
                           [st["qreg"]] + dregs, [T["qreg"]])
                    P.emit("dve", lambda v, kt3=kt3, enb3=enb3, tsl=tsl: v.tensor_tensor(out=kt3, in0=kT3[:, :, tsl], in1=enb3, op=ALU.mult),
                           [st["kreg"]] + dregs, [T["qreg"]])
                    if t < 8:
                        k_mult(T, kT3[:, :, tsl], st["kreg"], ek3, dregs, keng="pool")
                    info[(t, dr)] = (tile, sdecs, dregs, qt3, kt3)

            def stC(t):
                for dr in range(2):
                    tile, sdecs, dregs, qt3, kt3 = info[(t, dr)]
                    T = TC[dr][t % 2]
                    zb = ZB[dr]
                    P.mm(PS[zb][:, 256:384], [(kt3[:, c, :], qt3[:, c, :]) for c in range(2)], [T["qreg"]], [psr[zb]])
                    if t < 8:
                        k_tr(T, zb)

            def stD(t):
                for dr in range(2):
                    T = TC[dr][t % 2]
                    zb = ZB[dr]
                    mask = mf_b if dr == 0 else mb_b
                    PT = T["PT"]
                    P.emit("dve", lambda v, PT=PT, mask=mask, zb=zb: v.tensor_tensor(out=PT, in0=PS[zb][:, 256:384], in1=mask[:, :], op=ALU.mult),
                           [psr[zb], c2], [T["preg"]])
                    if t < 8:
                        k_copy(T, zb)

            def stE(t):
                for dr in range(2):
                    tile, sdecs, dregs, qt3, kt3 = info[(t, dr)]
                    T = TC[dr][t % 2]
                    PT = T["PT"]
                    ob = OB[dr]
                    S163 = _r3(S16[dr][0], 2, 512)
                    P.mm(PS[ob][:, 0:dv], [(PT, v3[:, tile, 0:dv])] + [(qt3[:, c, :], S163[:, c, 0:dv]) for c in range(2)],
                         [T["preg"], T["qreg"], st["vreg"], S16[dr][1]], [psr[ob]])
                    do_readout = (t > 4) if dr == 0 else (t >= 4)
                    if not do_readout:
                        P.emit("act", lambda sc, tile=tile, ob=ob: sc.activation(out=o_f3[:, tile, 0:dv], in_=PS[ob][:, 0:dv], func=AF.Copy),
                               [psr[ob]], [of_reg])
                    else:
                        readout(ob, is_ret, h, dv, tile, gs3, st)
                    if t < 8:
                        supdate(T, v3[:, tile, 0:dv], st["vreg"], sdecs, dregs, S32[dr], S16[dr], Sreg[dr], dv, is_ret)
            stA(0)
            stB(0)
            for t in range(9):
                stA(t + 1)
                stB(t + 1)
                stC(t)
                filler()
                stD(t)
                filler()
                stE(t)
                filler()
            if hd == DBG_OF_HEAD:
                P.barrier()
                P.dma("sp", "out", dbg_d["of"][:, :], o_f)
                finish()
                raise _Stop()

        ro_i = [0]

        def readout(ob, is_ret, h, dv, tile, gs3, st):
            P.emit("dve", lambda v: v.tensor_tensor(out=otot[:, 0:dv], in0=o_f3[:, tile, 0:dv], in1=PS[ob][:, 0:dv], op=ALU.add),
                   [psr[ob], of_reg], [ro_reg])
            mean = stats[:, 0:1]
            var = stats[:, 1:2]
            rs = stats[:, 2:3]
            nmr = stats[:, 3:4]
            i = ro_i[0] % 2
            ro_i[0] += 1
            if is_ret:
                P.emit("dve", lambda v: v.bn_stats(stats[:, 8:14], otot[:, 0:dv]), [ro_reg], [ro_reg])
                P.emit("dve", lambda v: v.bn_aggr(stats[:, 0:2], stats[:, 8:14]), [ro_reg], [ro_reg])
                rstd_act(rs, var, 1.0, epsc[:, 1:2], [ro_reg], [ro_reg])
                P.emit("dve", lambda v: v.scalar_tensor_tensor(out=nmr, in0=mean, scalar=-1.0, in1=rs, op0=ALU.mult, op1=ALU.mult), [ro_reg], [ro_reg])
                P.emit("act", lambda sc: sc.activation(out=ntmp[:, 0:dv], in_=otot[:, 0:dv], func=AF.Identity, scale=rs, bias=nmr), [ro_reg], [ro_reg])
                P.emit("dve", lambda v, i=i: v.tensor_tensor(out=rb[i][:, 0:dv], in0=ntmp[:, 0:dv], in1=gs3[:, tile, 0:dv], op=ALU.mult),
                       [ro_reg, st["greg"]], [rb_reg[i]])
                col0 = h * 256
            else:
                P.emit("dve", lambda v: v.memset(var, 0.0), [ro_reg], [ro_reg])
                P.emit("act", lambda sc: sc.activation(out=ntmp[:, 0:dv], in_=otot[:, 0:dv], func=AF.Square, accum_out=var), [ro_reg], [ro_reg])
                rstd_act(rs, var, 1.0 / dv, epsc[:, 1:2], [ro_reg], [ro_reg])
                P.emit("dve", lambda v, i=i: v.scalar_tensor_tensor(out=rb[i][:, 0:dv], in0=otot[:, 0:dv], scalar=rs, in1=gs3[:, tile, 0:dv],
                                                                   op0=ALU.mult, op1=ALU.mult), [ro_reg, st["greg"]], [rb_reg[i]])
                col0 = 2048 + h * 512
            P.dma("sp", "rst", R_d[tile * 128:(tile + 1) * 128, col0:col0 + dv], rb[i][:, 0:dv], reads=[rb_reg[i]], writes=[R_reg])

        gens = [None]

        def filler(n=1):
            g = gens[0]
            if g is None:
                return
            for _ in range(n):
                try:
                    next(g)
                except StopIteration:
                    gens[0] = None
                    return

        def drain():
            while gens[0] is not None:
                filler()
        NHEADS = 12
        gens[0] = in_proj(0, 0)
        drain()
        try:
            for hd in range(NHEADS):
                gens[0] = in_proj(hd + 1, (hd + 1) % 2) if hd + 1 < NHEADS else None
                if NO_INTERLEAVE:
                    drain()
                scan(hd, hd % 2, filler)
                drain()
        except _Stop:
            return nc

        if STOP_AFTER == "C":
            P.barrier()
            rgd = Region("dmpR")
            stg = BIG.bitcast(BF16)[:, 0:4096]
            for tile in range(9):
                P.dma("sp", "ld", stg, R_d[tile * 128:(tile + 1) * 128, :], writes=[rgd])
                P.emit("dve", lambda v: v.tensor_copy(HT[:, 0:4096], stg), [rgd], [rgd])
                P.dma("sp", "out", dbg_d["R"][tile * 128:(tile + 1) * 128, :], HT[:, 0:4096], reads=[rgd], writes=[rgd])
            finish()
            return nc

        P.barrier()
        rT3 = _r3(BIG.bitcast(BF16), 32, TOK)
        rT_reg = Region("rT")
        hxb = HT[:, 9216:18432]
        rtile = [hxb[:, i * 2048:(i + 1) * 2048].bitcast(BF16) for i in range(2)]
        rtile_reg = [Region("rtile%d" % i) for i in range(2)]
        for tile in range(9):
            i = tile % 2
            P.dma("sp", "ld", rtile[i], R_d[tile * 128:(tile + 1) * 128, :], reads=[R_reg], writes=[rtile_reg[i]])
            for q in range(4):
                pb = (tile * 4 + q) % 8
                for j in range(8):
                    kc = q * 8 + j
                    P.transpose(PSb[pb][:, j * 128:(j + 1) * 128], rtile[i][:, kc * 128:(kc + 1) * 128], ident_b[:, :],
                                [rtile_reg[i], c2], [psr[pb]])
                dst = rT3[:, q * 8:(q + 1) * 8, tile * 128:(tile + 1) * 128]
                srcv = _r3(PSb[pb][:, 0:1024], 8, 128)
                if q % 2 == 0:
                    P.emit("act", lambda sc, dst=dst, srcv=srcv: sc.activation(out=dst, in_=srcv, func=AF.Copy), [psr[pb]], [rT_reg])
                else:
                    P.emit("dve", lambda v, dst=dst, srcv=srcv: v.tensor_copy(dst, srcv), [psr[pb]], [rT_reg])
        P.barrier()
        mT3 = _r3(hxb.bitcast(BF16), KC, TOK)
        mT_reg = Region("mT")
        gA = Arena(GA_t[:, :], 3328)
        dt1 = [gA.f32(384) for _ in range(2)]
        dt2 = [gA.f32(384) for _ in range(2)]
        dt_reg = [Region("dt%d" % i) for i in range(2)]

        def load2(srcs):
            i = wb_i[0] % 4
            wb_i[0] += 1
            off = 0
            views = []
            for src, n in srcs:
                v = WBb[:, i * 4096 + off:i * 4096 + off + n]
                P.dma("pool", "w", v, src, writes=[wb_reg[i]])
                views.append(v)
                off += n
            return views, wb_reg[i]

        def d1_block(j, blk, wrp3, wgp3, wmg3, regA, regB):
            c0 = blk * 384
            par = (j * 3 + blk) % 2
            pbs = [par * 4 + i for i in range(4)]
            P.mm(PS[pbs[0]][:, 0:384], [(wrp3[:, k, :], rT3[:, k, c0:c0 + 384]) for k in range(KC)], [regA, rT_reg], [psr[pbs[0]]])
            P.mm(PS[pbs[1]][:, 0:384], [(wgp3[:, k, :], rT3[:, 16 + k, c0:c0 + 384]) for k in range(KC)], [regA, rT_reg], [psr[pbs[1]]])
            P.mm(PS[pbs[2]][:, 0:384], [(wmg3[:, k, 0:128], hTo3[:, k, c0:c0 + 384]) for k in range(KC)],
                 [regB] + hT_reg[blk * 3:blk * 3 + 3], [psr[pbs[2]]])
            P.mm(PS[pbs[3]][:, 0:384], [(wmg3[:, k, 128:256], hTo3[:, k, c0:c0 + 384]) for k in range(KC)],
                 [regB] + hT_reg[blk * 3:blk * 3 + 3], [psr[pbs[3]]])
            t1, t2, treg = dt1[par], dt2[par], dt_reg[par]
            P.emit("act", lambda sc: sc.activation(out=t1, in_=PS[pbs[2]][:, 0:384], func=AF.Sigmoid), [psr[pbs[2]]], [treg])
            P.emit("act", lambda sc: sc.activation(out=t2, in_=PS[pbs[3]][:, 0:384], func=AF.Sigmoid), [psr[pbs[3]]], [treg])
            P.emit("dve", lambda v: v.tensor_tensor(out=t1, in0=PS[pbs[0]][:, 0:384], in1=t1, op=ALU.mult), [psr[pbs[0]], treg], [treg])
            P.emit("dve", lambda v: v.tensor_tensor(out=t2, in0=PS[pbs[1]][:, 0:384], in1=t2, op=ALU.mult), [psr[pbs[1]], treg], [treg])
            P.emit("pool", lambda e: e.tensor_tensor(out=mT3[:, j, c0:c0 + 384], in0=t1, in1=t2, op=ALU.add), [treg], [mT_reg])

        for j in range(16):
            (wrp, wgp), regA = load2([(wrp_d[j], 2048), (wgp_d[j], 2048)])
            (wmg,), regB = load2([(wmg_d[j], 4096)])
            for blk in range(3):
                d1_block(j, blk, _r3(wrp, KC, 128), _r3(wgp, KC, 128), _r3(wmg, KC, 256), regA, regB)

        if STOP_AFTER == "D1":
            P.barrier()
            for k in range(KC):
                rgd = Region("dmp")
                P.emit("dve", lambda v, k=k: v.tensor_copy(BIG[:, 0:TOK], mT3[:, k, :]), [], [rgd])
                P.dma("sp", "out", dbg_d["mT"][:, k * TOK:(k + 1) * TOK], BIG[:, 0:TOK], reads=[rgd])
                P.barrier()
            finish()
            return nc

        P.barrier()
        x1T3 = _r3(BIG, KC, TOK)
        x1_reg = [Region("x1_%d" % k) for k in range(KC)]
        gA.reset(0)
        xtile = gA.f32(2048)
        xtile_reg = Region("xtile")
        for tile in range(9):
            P.dma("sp", "ld", xtile, x_d[tile * 128:(tile + 1) * 128, :], writes=[xtile_reg])
            for q in range(4):
                pb = (tile * 4 + q) % 8
                for kk in range(4):
                    k = q * 4 + kk
                    P.transpose(PS[pb][:, kk * 128:(kk + 1) * 128], xtile[:, k * 128:(k + 1) * 128], ident_f[:, :],
                                [xtile_reg, consts_reg], [psr[pb]])
                dst = x1T3[:, q * 4:(q + 1) * 4, tile * 128:(tile + 1) * 128]
                srcv = _r3(PS[pb][:, 0:512], 4, 128)
                if q % 2 == 0:
                    P.emit("act", lambda sc, dst=dst, srcv=srcv: sc.activation(out=dst, in_=srcv, func=AF.Copy), [psr[pb]], x1_reg[q * 4:(q + 1) * 4])
                else:
                    P.emit("dve", lambda v, dst=dst, srcv=srcv: v.tensor_copy(dst, srcv), [psr[pb]], x1_reg[q * 4:(q + 1) * 4])

        def d2_block(j, blk, wo3, rg):
            c0 = blk * 384
            pb = (j * 3 + blk) % 8
            P.mm(PS[pb][:, 0:384], [(wo3[:, k, :], mT3[:, k, c0:c0 + 384]) for k in range(KC)], [rg, mT_reg], [psr[pb]])
            P.emit("dve", lambda v: v.scalar_tensor_tensor(out=x1T3[:, j, c0:c0 + 384], in0=PS[pb][:, 0:384], scalar=g1[:, j:j + 1],
                                                          in1=x1T3[:, j, c0:c0 + 384], op0=ALU.mult, op1=ALU.add),
                   [psr[pb], x1_reg[j], mod_reg2], [x1_reg[j]])
        for j in range(16):
            (wo,), rg = load2([(wo_d[j], 2048)])
            for blk in range(3):
                d2_block(j, blk, _r3(wo, KC, 128), rg)

        P.barrier()
        h2T3 = _r3(hxb.bitcast(BF16), KC, TOK)
        h2_reg = Region("h2T")
        hA = Arena(HT[:, 0:9216], 9216)
        sqb = hA.bf16(16 * 512)
        sq_reg = Region("sq")
        rbc = hA.f32(512)
        rbc_reg = Region("rbc")
        tmpk = [hA.f32(512) for _ in range(2)]
        tmpk_reg = [Region("tmpk%d" % i) for i in range(2)]

        def ssq_bcast(c0, n, pb):
            sq3 = _r3(sqb[:, 0:16 * n], KC, n)
            P.emit("act", lambda sc: sc.activation(out=sq3, in_=x1T3[:, :, c0:c0 + n], func=AF.Square), x1_reg, [sq_reg])
            P.mm(PS[pb][:, 0:n], [(ones_b[:, :], sq3[:, k, :]) for k in range(KC)], [sq_reg, c2], [psr[pb]])
            rstd_act(rbc[:, 0:n], PS[pb][:, 0:n], 1.0 / D, epsc[:, 0:1], [psr[pb]], [rbc_reg])

        def d3_chunk(k, c0, n):
            i = k % 2
            P.emit("dve", lambda v: v.scalar_tensor_tensor(out=tmpk[i][:, 0:n], in0=x1T3[:, k, c0:c0 + n], scalar=A2[:, k:k + 1], in1=rbc[:, 0:n],
                                                          op0=ALU.mult, op1=ALU.mult), [x1_reg[k], a2_reg, rbc_reg], [tmpk_reg[i]])
            P.emit("act", lambda sc: sc.activation(out=h2T3[:, k, c0:c0 + n], in_=tmpk[i][:, 0:n], func=AF.Identity, bias=sh2[:, k:k + 1]),
                   [tmpk_reg[i], mod_reg2], [h2_reg])
        for blk in range(3):
            ssq_bcast(blk * 384, 384, blk)
            for k in range(KC):
                d3_chunk(k, blk * 384, 384)

        if STOP_AFTER == "D":
            P.barrier()
            for k in range(KC):
                P.dma("sp", "out", dbg_d["x1T"][:, k * TOK:(k + 1) * TOK], x1T3[:, k, :])
            for k in range(KC):
                rgd = Region("dmp")
                P.emit("dve", lambda v, k=k: v.tensor_copy(hA.ap[:, 0:TOK], h2T3[:, k, :]), [], [rgd])
                P.dma("sp", "out", dbg_d["h2T"][:, k * TOK:(k + 1) * TOK], hA.ap[:, 0:TOK], reads=[rgd])
                P.barrier()
            finish()
            return nc

        P.barrier()
        hA.reset(0)
        aTp = [hA.f32(20 * 66) for _ in range(2)]
        aTp3 = [_r3(a, 20, 66) for a in aTp]
        aT_reg = [Region("aT%d" % i) for i in range(2)]
        acc = [hA.f32(1024) for _ in range(2)]
        acc_reg = [Region("acc%d" % i) for i in range(2)]
        gact = [hA.f32(1024) for _ in range(2)]
        gact_reg = [Region("gact%d" % i) for i in range(2)]
        gvT = [hA.bf16(4 * 1024), GA_t[:, 0:2048].bitcast(BF16)]
        gv_reg = [Region("gv%d" % i) for i in range(2)]
        for i in range(2):
            P.emit("dve", lambda v, i=i: v.memset(aTp[i], 0.0), [], [aT_reg[i]])
        wdnv = WBb[:, 8192:16384]
        wdn3 = _r3(wdnv, 4, 2048)
        CW = convv[:, 0:NCC * 9]
        CB = convv[:, NCC * 9:NCC * 10]

        wu_of = {}

        def ffn_a(cc):
            g, ci = divmod(cc, 4)
            par = cc % 2
            slot = cc % 2
            wu = WBb[:, slot * 4096:(slot + 1) * 4096]
            P.dma("pool", "w", wu, wup_d[cc], writes=[wb_reg[slot]])
            wu3 = _r3(wu, KC, 256)
            a3 = aTp3[par]
            for blk in range(3):
                pb = blk
                c0 = blk * 384
                P.mm(PS[pb][:, 0:384], [(wu3[:, k, 0:128], h2T3[:, k, c0:c0 + 384]) for k in range(KC)], [wb_reg[slot], h2_reg], [psr[pb]])
                P.emit("act", lambda sc, pb=pb, blk=blk: sc.activation(out=a3[:, 1 + blk * 6:7 + blk * 6, 1:65], in_=_r3(PS[pb][:, 0:384], 6, 64), func=AF.Copy),
                       [psr[pb]], [aT_reg[par]])
            acc3 = _r3(acc[par], 16, 64)
            for tap in range(9):
                i, jx = divmod(tap, 3)
                view = a3[:, i:i + 16, jx:jx + 64]
                w = CW[:, cc * 9 + tap:cc * 9 + tap + 1]
                if tap == 0:
                    P.emit("dve", lambda v, view=view, w=w: v.tensor_scalar(out=acc3, in0=view, scalar1=w, scalar2=None, op0=ALU.mult),
                           [aT_reg[par], consts_reg], [acc_reg[par]])
                else:
                    P.emit("dve", lambda v, view=view, w=w: v.scalar_tensor_tensor(out=acc3, in0=view, scalar=w, in1=acc3, op0=ALU.mult, op1=ALU.add),
                           [aT_reg[par], consts_reg, acc_reg[par]], [acc_reg[par]])
            P.emit("act", lambda sc: sc.activation(out=gact[par], in_=acc[par], func=AF.Gelu_apprx_tanh, bias=CB[:, cc:cc + 1]),
                   [acc_reg[par], consts_reg], [gact_reg[par]])
            wu_of[cc] = (wu3, slot)

        def ffn_v(cc):
            g, ci = divmod(cc, 4)
            par = cc % 2
            wu3, slot = wu_of[cc]
            if ci == 0:
                for i in range(min(4, NCC - g * 4)):
                    P.dma("pool", "w", wdnv[:, i * 2048:(i + 1) * 2048], wdn_d[g * 4 + i], writes=[wb_reg[2], wb_reg[3]])
            gv3 = _r3(gvT[g % 2], 4, 1024)
            for tb in range(2):
                pb = 3 + tb
                P.mm(PS[pb][:, 0:512], [(wu3[:, k, 128:256], h2T3[:, k, tb * 512:(tb + 1) * 512]) for k in range(KC)],
                     [wb_reg[slot], h2_reg], [psr[pb]])
                P.emit("dve", lambda v, pb=pb, tb=tb: v.tensor_tensor(out=gv3[:, ci, tb * 512:(tb + 1) * 512], in0=PS[pb][:, 0:512],
                                                                    in1=gact[par][:, tb * 512:(tb + 1) * 512], op=ALU.mult),
                       [psr[pb], gact_reg[par]], [gv_reg[g % 2]])
            if ci == 3 or cc == NCC - 1:
                ncg = ci + 1
                for j in range(16):
                    for tb in range(2):
                        pb = 5 + (j * 2 + tb) % 3
                        P.mm(PS[pb][:, 0:512], [(wdn3[:, i, j * 128:(j + 1) * 128], gv3[:, i, tb * 512:(tb + 1) * 512]) for i in range(ncg)],
                             [wb_reg[2], wb_reg[3], gv_reg[g % 2]], [psr[pb]])
                        P.emit("dve", lambda v, pb=pb, j=j, tb=tb: v.scalar_tensor_tensor(
                            out=x1T3[:, j, tb * 512:(tb + 1) * 512], in0=PS[pb][:, 0:512], scalar=g2[:, j:j + 1],
                            in1=x1T3[:, j, tb * 512:(tb + 1) * 512], op0=ALU.mult, op1=ALU.add), [psr[pb], x1_reg[j], mod_reg2], [x1_reg[j]])
        ffn_a(0)
        for cc in range(NCC):
            if cc + 1 < NCC:
                ffn_a(cc + 1)
            ffn_v(cc)

        P.barrier()
        hA.reset(0)
        sqb = hA.bf16(16 * 512)
        rbc = hA.f32(512)
        otile = [GA_t[:, 0:2048], hA.f32(2048)]
        otile_reg = [Region("otile%d" % i) for i in range(2)]
        oT3 = _r3(hxb[:, 0:8192], KC, 512)
        oT_reg = Region("oT")

        def fin_block(blk):
            c0 = blk * 512
            ssq_bcast(c0, 512, blk)
            for k in range(KC):
                P.emit("dve", lambda v, k=k: v.scalar_tensor_tensor(out=oT3[:, k, :], in0=x1T3[:, k, c0:c0 + 512], scalar=fgv[:, k:k + 1], in1=rbc[:, 0:512],
                                                                   op0=ALU.mult, op1=ALU.mult), [x1_reg[k], consts_reg, rbc_reg], [oT_reg])
            for tt in range(4):
                tile = blk * 4 + tt
                par = tile % 2
                for q in range(4):
                    pb = 2 + (tile * 4 + q) % 6
                    for kk in range(4):
                        P.transpose(PS[pb][:, kk * 128:(kk + 1) * 128], oT3[:, q * 4 + kk, tt * 128:(tt + 1) * 128], ident_f[:, :],
                                    [oT_reg, consts_reg], [psr[pb]])
                    dst = otile[par][:, q * 512:(q + 1) * 512]
                    if q % 2 == 0:
                        P.emit("act", lambda sc, dst=dst, pb=pb: sc.activation(out=dst, in_=PS[pb][:, 0:512], func=AF.Copy), [psr[pb]], [otile_reg[par]])
                    else:
                        P.emit("dve", lambda v, dst=dst, pb=pb: v.tensor_copy(dst, PS[pb][:, 0:512]), [psr[pb]], [otile_reg[par]])
                P.dma("sp", "out", out_d[tile * 128:(tile + 1) * 128, :], otile[par], reads=[otile_reg[par]])
        for blk in range(2):
            fin_block(blk)
        finish()
    return nc


def _tile_k(w):
    n = w.shape[1]
    return np.ascontiguousarray(w.reshape(KC, 128, n).transpose(1, 0, 2)).reshape(128, KC * n)


def _fm(v, nchunk):
    return np.ascontiguousarray(v.reshape(nchunk, 128).T)


def prep_shared(inp, s):
    w_in = inp["w_in"][0]
    o = {}
    RQ, RK, RV, RG = 0, 2048, 4096, 6144
    GQ, GK, GV, GGT = 8192, 9216, 10240, 12288
    GAF, GAB, MR, MG = 14336, 14352, 14368, 16416
    wr = np.empty((RH * 4, 128, KC * 256), np.float32)
    for h in range(RH):
        for i, base in enumerate((RQ, RK, RV, RG)):
            wr[h * 4 + i] = _tile_k(w_in[:, base + h * 256: base + (h + 1) * 256])
    o["w_ret"] = wr
    wg = np.empty((GH * 6, 128, KC * 256), np.float32)
    for h in range(GH):
        wg[h * 6 + 0] = _tile_k(w_in[:, GQ + h * 256: GQ + (h + 1) * 256])
        wg[h * 6 + 1] = _tile_k(w_in[:, GK + h * 256: GK + (h + 1) * 256])
        wg[h * 6 + 2] = _tile_k(w_in[:, GV + h * 512: GV + h * 512 + 256])
        wg[h * 6 + 3] = _tile_k(w_in[:, GV + h * 512 + 256: GV + (h + 1) * 512])
        wg[h * 6 + 4] = _tile_k(w_in[:, GGT + h * 512: GGT + h * 512 + 256])
        wg[h * 6 + 5] = _tile_k(w_in[:, GGT + h * 512 + 256: GGT + (h + 1) * 512])
    o["w_gla"] = wg
    lr = (w_in[:, GAF:GAF + 16], w_in[:, GAB:GAB + 16])
    if s == 1:
        lr = lr[::-1]
    o["w_lr"] = _tile_k(np.concatenate(lr, axis=1))
    wmg = np.empty((16, 128, KC * 256), np.float32)
    for j in range(16):
        wmg[j] = _tile_k(np.concatenate([w_in[:, MR + j * 128: MR + (j + 1) * 128], w_in[:, MG + j * 128: MG + (j + 1) * 128]], axis=1))
    o["w_mg"] = wmg
    for name, key in (("w_rp", "w_ret_proj"), ("w_gp", "w_gla_proj"), ("w_o", "w_out")):
        w = inp[key][0]
        o[name] = np.stack([_tile_k(w[:, j * 128:(j + 1) * 128]) for j in range(16)])
    w_up = inp["w_up"][0]
    o["w_up"] = np.stack([_tile_k(np.concatenate([w_up[:, cc * 128:(cc + 1) * 128], w_up[:, DFF + cc * 128: DFF + (cc + 1) * 128]], axis=1))
                          for cc in range(NCC)])
    o["w_dn"] = np.ascontiguousarray(inp["w_down"][0].reshape(NCC, 128, D))
    w_ada = inp["w_ada"][0]
    wa = np.empty((24, 128, KC * 512), np.float32)
    for g in range(24):
        blk = w_ada[:, g * 512:(g + 1) * 512].reshape(KC, 128, 4, 128).transpose(1, 2, 0, 3)
        wa[g] = np.ascontiguousarray(blk).reshape(128, KC * 512)
    o["wada"] = wa
    o["vecs"] = np.concatenate([_fm(inp["b_ada"][0], 96), _fm(inp["norm1_g"][0], 16), _fm(inp["norm2_g"][0], 16),
                                _fm(inp["final_g"], 16)], axis=1).astype(np.float32)
    cw = inp["conv_w"][0]
    if s == 1:
        cw = cw[::-1, ::-1]
    cwt = np.ascontiguousarray(cw.reshape(9, NCC, 128).transpose(2, 1, 0)).reshape(128, NCC * 9)
    o["convv"] = np.concatenate([cwt, _fm(inp["conv_b"][0], NCC)], axis=1).astype(np.float32)
    dl = inp["ret_decay_logit"][0]
    gu = inp["gla_gate_up"][0]
    gb = inp["gla_gate_bias"][0]
    order = (0, 1) if s == 0 else (1, 0)
    o["dlog"] = np.concatenate([dl[order[0]], dl[order[1]]])[None, :].astype(np.float32)
    o["gup"] = np.concatenate([np.concatenate([gu[d], gb[d][None, :]], axis=0) for d in order], axis=1).astype(np.float32)
    o["rng"] = inp["ret_norm_g"][0][None, :].astype(np.float32)
    o["gng"] = inp["gla_norm_g"][0][None, :].astype(np.float32)
    ident = np.eye(128, dtype=np.float32)
    idx = np.arange(128)
    mf = (idx[None, :] >= idx[:, None]).astype(np.float32)
    mb = (idx[None, :] <= idx[:, None]).astype(np.float32)
    io = np.stack([idx + 1.0, 128.0 - idx, 127.0 - idx, idx * 1.0]).astype(np.float32)
    iot = np.broadcast_to(io.reshape(1, 512), (128, 512))
    o["consts"] = np.concatenate([ident, mf, mb, iot], axis=1).astype(np.float32)
    pos = np.arange(L) if s == 0 else np.arange(L - 1, -1, -1)
    row = (pos // 64).astype(np.float32)
    col = (pos % 64).astype(np.float32)
    inv = (10000.0 ** (-np.arange(64, dtype=np.float32) / 64)).astype(np.float32)
    ang = np.concatenate([row[:, None] * inv, col[:, None] * inv], axis=-1)
    o["rope"] = np.concatenate([np.cos(ang).T, np.sin(ang).T], axis=1).astype(np.float32)
    return o


def make_in_maps(inp):
    shared = [prep_shared(inp, 0), prep_shared(inp, 1)]
    maps = []
    for core in range(8):
        b, s = core // 2, core % 2
        m = dict(shared[s])
        xb = inp["x"][b]
        cb = inp["ctx"][b]
        m["x"] = np.ascontiguousarray(xb if s == 0 else xb[::-1])
        m["ctx"] = np.ascontiguousarray(cb if s == 0 else cb[::-1])
        cvv = np.stack([inp["c"][b].reshape(KC, 128).T, inp["c_ctx"].reshape(KC, 128).T], axis=2)
        m["cv"] = np.ascontiguousarray(cvv).reshape(128, KC * 2).astype(np.float32)
        maps.append(m)
    return maps


_NC_CACHE = {}


def kernel(**inputs):
    inp = {k: np.asarray(v) for k, v in inputs.items()}
    if "nc" not in _NC_CACHE:
        _NC_CACHE["nc"] = build_program()
    nc = _NC_CACHE["nc"]
    maps = make_in_maps(inp)
    res = run_bass_kernel_spmd(nc, maps, core_ids=list(range(8)))
    out = np.empty((4, L, D), np.float32)
    for core in range(8):
        b, s = core // 2, core % 2
        o = res.results[core]["out"]
        if s == 0:
            out[b, 0:1024] = o
        else:
            out[b, 1024:2048] = o[::-1]
    return out
```

```python
import numpy as np
from contextlib import ExitStack
import concourse.bass as bass
import concourse.mybir as mybir
from concourse.bass_utils import run_bass_kernel_spmd

F32 = mybir.dt.float32
BF16 = mybir.dt.bfloat16
AF = mybir.ActivationFunctionType
ALU = mybir.AluOpType
AX = mybir.AxisListType

D = 2048
KC = 16
L = 2048
CTX = 256
NT = 9
TOK = NT * 128
TOKA = 2304
RH, GH = 8, 4
DFF = 5504
NCC = 43
EPS = 1e-6
EPS_RO = 256.0 * EPS
STOP_AFTER = None
DBG_OF_HEAD = None
NO_INTERLEAVE = False
EXP1 = False
EXP2 = True


class _Stop(Exception):
    pass


class Region:
    __slots__ = ("name", "w", "r", "excl")

    def __init__(self, name, excl=False):
        self.name = name
        self.w = {}
        self.r = {}
        self.excl = excl


class Prog:
    ENG = ("pe", "act", "dve", "pool", "sp")

    def __init__(self):
        self.ops = {e: [] for e in self.ENG}
        self.cnt = {}
        self.known = {e: {} for e in self.ENG}

    def _need(self, eng, reads, writes, is_dma=False):
        need = {}

        def add(k, v):
            if v > need.get(k, 0):
                need[k] = v
        for rg in reads:
            for k, v in rg.w.items():
                if k == eng and eng == "pe":
                    continue
                add(k, v)
            if rg.excl:
                for k, v in rg.r.items():
                    if k != eng or is_dma:
                        add(k, v)
        for rg in writes:
            for k, v in rg.w.items():
                if k != eng or is_dma:
                    add(k, v)
            for k, v in rg.r.items():
                if k != eng or is_dma:
                    add(k, v)
        out = []
        kn = self.known[eng]
        for k, v in need.items():
            if kn.get(k, 0) < v:
                kn[k] = v
                out.append((k, v))
        return out

    def _record(self, key, val, reads, writes):
        for rg in reads:
            rg.r[key] = val
        for rg in writes:
            rg.w[key] = val

    def emit(self, eng, fn, reads=(), writes=(), inc=True, wait=True):
        w = self._need(eng, reads, writes) if wait else []
        key = None
        if inc:
            self.cnt[eng] = self.cnt.get(eng, 0) + 1
            key = eng
            self._record(eng, self.cnt[eng], reads, writes)
        self.ops[eng].append((w, fn, key, 1))

    def mm(self, out, pairs, reads, writes):
        n = len(pairs)
        for i, (l, r) in enumerate(pairs):
            def fn(t, l=l, r=r, st=(i == 0), sp=(i == n - 1)):
                return t.matmul(out, lhsT=l, rhs=r, start=st, stop=sp)
            self.emit("pe", fn, reads, writes, inc=(i == n - 1), wait=(i == 0))

    def transpose(self, out, in_, ident, reads, writes):
        self.emit("pe", lambda t: t.transpose(out, in_, ident), reads, writes)

    def dma(self, eng, chan, out, in_, reads=(), writes=()):
        w = self._need(eng, reads, writes, is_dma=True)
        self.cnt[chan] = self.cnt.get(chan, 0) + 16
        self._record(chan, self.cnt[chan], reads, writes)
        self.ops[eng].append((w, lambda e: e.dma_start(out=out, in_=in_), chan, 16))

    def barrier(self):
        for eng in self.ENG:
            kn = self.known[eng]
            waits = []
            for k, v in self.cnt.items():
                if k == eng and eng == "pe":
                    continue
                if kn.get(k, 0) < v:
                    kn[k] = v
                    waits.append((k, v))
            if waits:
                self.ops[eng].append((waits, None, None, 0))

    def replay(self, block, sems, final_waits):
        hmap = {"pe": block.tensor, "act": block.scalar, "dve": block.vector, "pool": block.gpsimd, "sp": block.sync}
        for eng in self.ENG:
            ops = self.ops[eng]
            extra = final_waits if eng == "sp" else ()

            def body(h, ops=ops, extra=extra):
                for waits, fn, key, amt in ops:
                    for k, v in waits:
                        h.wait_ge(sems[k], v)
                    if fn is None:
                        continue
                    ins = fn(h)
                    if key is not None:
                        ins.then_inc(sems[key], amt)
                for k, v in extra:
                    h.wait_ge(sems[k], v)
            hmap[eng](body)


class Arena:
    def __init__(self, ap_f32, nwords):
        self.ap = ap_f32
        self.n = nwords
        self.off = 0

    def reset(self, off=0):
        self.off = off

    def f32(self, words):
        assert self.off + words <= self.n, (self.off, words, self.n)
        v = self.ap[:, self.off:self.off + words]
        self.off += words
        return v

    def bf16(self, elems):
        assert elems % 2 == 0
        return self.f32(elems // 2).bitcast(BF16)


def _r3(ap, a, b):
    return ap.rearrange("p (a b) -> p a b", a=a, b=b)


def build_program(dbg=None):
    nc = bass.Bass("TRN2", target_bir_lowering=False)
    P = Prog()
    dt_in = {}

    def din(name, shape, dt=F32):
        dt_in[name] = nc.dram_tensor(name, list(shape), dt, kind="ExternalInput").ap()
        return dt_in[name]

    x_d = din("x", [L, D])
    ctx_d = din("ctx", [CTX, D])
    cv_d = din("cv", [128, KC * 2])
    wada_d = din("wada", [24, 128, KC * 512])
    vec_d = din("vecs", [128, 96 + 48])
    conv_d = din("convv", [128, NCC * 10])
    cst_d = din("consts", [128, 128 * 3 + 512])
    rope_d = din("rope", [128, 2 * L])
    dl_d = din("dlog", [1, 16])
    gu_d = din("gup", [17, 2 * 1024])
    rg_d = din("rng", [1, 2048])
    gg_d = din("gng", [1, 2048])
    wr_d = din("w_ret", [RH * 4, 128, KC * 256])
    wg_d = din("w_gla", [GH * 6, 128, KC * 256])
    wlr_d = din("w_lr", [128, KC * 32])
    wmg_d = din("w_mg", [16, 128, KC * 256])
    wrp_d = din("w_rp", [16, 128, KC * 128])
    wgp_d = din("w_gp", [16, 128, KC * 128])
    wo_d = din("w_o", [16, 128, KC * 128])
    wup_d = din("w_up", [NCC, 128, KC * 256])
    wdn_d = din("w_dn", [NCC, 128, D])
    out_d = nc.dram_tensor("out", [1024, D], F32, kind="ExternalOutput").ap()
    R_d = nc.dram_tensor("r_scr", [TOK, 4096], BF16).ap()
    ST_d = nc.dram_tensor("st_scr", [12 * 2, 128, 1024], F32).ap()
    dbg_d = {}
    if dbg:
        for name, shape in dbg.items():
            dbg_d[name] = nc.dram_tensor("dbg_" + name, list(shape), F32, kind="ExternalOutput").ap()

    with ExitStack() as es:
        def sb(name, shape, dt=F32):
            return es.enter_context(nc.sbuf_tensor(name, list(shape), dt))
        HT_t = sb("HT", [128, 18432], F32)
        BIG_t = sb("BIG", [128, 18432], F32)
        WB_t = sb("WB", [128, 8192], F32)
        GA_t = sb("GA", [128, 3328], F32)
        MISC_t = sb("MISC", [128, 2560], F32)
        ident_f = sb("ident_f", [128, 128])
        cmask = sb("cmask", [128, 256 + 512])
        ident_b = sb("ident_b", [128, 128], BF16)
        mf_b = sb("mf_b", [128, 128], BF16)
        mb_b = sb("mb_b", [128, 128], BF16)
        ones_b = sb("ones_b", [128, 128], BF16)
        modT = sb("modT", [128, 192])
        vecs = sb("vecs_s", [128, 144])
        A1 = sb("A1", [128, 48])
        convv = sb("convv_s", [128, NCC * 10])
        lg = sb("lg", [128, 64])
        cvs = sb("cvs", [128, 64])
        PS = [es.enter_context(nc.psum_tensor("ps%d" % i, [128, 512], F32)) for i in range(8)]
        psr = [Region("ps%d" % i, excl=True) for i in range(8)]
        PSb = [p[:, :].bitcast(BF16) for p in PS]

        HT = HT_t[:, :]
        BIG = BIG_t[:, :]
        hT = HT.bitcast(BF16)
        hTo3 = _r3(hT[:, 0:18432], KC, TOK)
        hTx3 = _r3(hT[:, 18432:36864], KC, TOK)

        def hcol(k, c0, n):
            if c0 < TOK:
                assert c0 + n <= TOK
                return hTo3[:, k, c0:c0 + n]
            return hTx3[:, k, c0 - TOK:c0 - TOK + n]
        hT_reg = [Region("hT%d" % t) for t in range(18)]
        consts_reg = Region("consts")
        mod_reg = Region("mod")

        P.dma("sp", "ld", ident_f[:, :], cst_d[:, 0:128], writes=[consts_reg])
        P.dma("sp", "ld", cmask[:, :], cst_d[:, 128:896], writes=[consts_reg])
        P.dma("sp", "ld", vecs[:, :], vec_d[:, :], writes=[consts_reg])
        P.dma("sp", "ld", convv[:, :], conv_d[:, :], writes=[consts_reg])
        P.dma("sp", "ld", cvs[:, 0:32], cv_d[:, :], writes=[consts_reg])
        P.dma("sp", "ld", lg[:, 0:16], dl_d[0:1, :].partition_broadcast(128), writes=[consts_reg])
        c2 = Region("c2")
        P.emit("dve", lambda v: v.tensor_copy(ident_b[:, :], ident_f[:, :]), [consts_reg], [c2])
        P.emit("dve", lambda v: v.tensor_copy(mf_b[:, :], cmask[:, 0:128]), [consts_reg], [c2])
        P.emit("dve", lambda v: v.tensor_copy(mb_b[:, :], cmask[:, 128:256]), [consts_reg], [c2])
        P.emit("dve", lambda v: v.memset(ones_b[:, :], 1.0), [], [c2])
        epsc = lg[:, 48:56]
        P.emit("dve", lambda v: v.memset(epsc[:, 0:1], EPS), [], [c2])
        P.emit("dve", lambda v: v.memset(epsc[:, 1:2], EPS_RO), [], [c2])
        P.emit("dve", lambda v: v.memset(epsc[:, 2:3], 1.0), [], [c2])
        iotas = _r3(cmask[:, 256:768], 4, 128)
        bada = vecs[:, 0:96]
        n1g = vecs[:, 96:112]
        n2g = vecs[:, 112:128]
        fgv = vecs[:, 128:144]
        ONE = epsc[:, 2:3]

        def rstd_act(out, in_, scale, eps_ap, reads, writes):
            P.emit("act", lambda s: s.activation(out=out, in_=in_, func=AF.Ln, scale=scale, bias=eps_ap), reads + [c2], writes)
            P.emit("act", lambda s: s.activation(out=out, in_=out, func=AF.Exp, scale=-0.5), writes, writes)

        cs_b = MISC_t[:, 0:16].bitcast(BF16)
        cs_reg = Region("cs")
        P.emit("act", lambda s: s.activation(out=cs_b, in_=cvs[:, 0:32], func=AF.Silu), [consts_reg], [cs_reg])
        cs3 = _r3(cs_b, KC, 2)
        wb_reg = [Region("wb%d" % i) for i in range(4)]
        WBb = WB_t[:, :].bitcast(BF16)
        modT3 = _r3(modT[:, :], 96, 2)
        mod_reg2 = Region("mod2")

        def ada_group(g):
            bi = g % 2
            wv = WBb[:, bi * 8192:(bi + 1) * 8192]
            wregs = wb_reg[bi * 2:bi * 2 + 2]
            P.dma("pool", "w", wv, wada_d[g], writes=wregs)
            w4 = wv.rearrange("p (j k c) -> p j k c", j=4, k=KC, c=128)
            pb = g % 2
            for jj in range(4):
                P.mm(PS[pb][:, jj * 2:jj * 2 + 2],
                     [(w4[:, jj, k, :], cs3[:, k, :]) for k in range(KC)],
                     wregs + [cs_reg], [psr[pb]])
            P.emit("dve", lambda v: v.tensor_tensor(
                out=modT3[:, g * 4:g * 4 + 4, :], in0=_r3(PS[pb][:, 0:8], 4, 2),
                in1=bada[:, g * 4:g * 4 + 4].unsqueeze(2).to_broadcast([128, 4, 2]), op=ALU.add),
                [psr[pb], consts_reg], [mod_reg if g < 8 else mod_reg2])
        for g in range(8):
            ada_group(g)
        sh1 = modT3[:, 0:16, 0]
        sc1 = modT3[:, 16:32, 0]
        g1 = modT3[:, 32:48, 0]
        sh2 = modT3[:, 48:64, 0]
        sc2 = modT3[:, 64:80, 0]
        g2 = modT3[:, 80:96, 0]
        csh1 = modT3[:, 0:16, 1]
        csc1 = modT3[:, 16:32, 1]
        a_reg = Region("A")
        P.emit("dve", lambda v: v.scalar_tensor_tensor(out=A1[:, 0:16], in0=sc1, scalar=1.0, in1=n1g, op0=ALU.add, op1=ALU.mult),
               [mod_reg, consts_reg], [a_reg])
        P.emit("dve", lambda v: v.scalar_tensor_tensor(out=A1[:, 16:32], in0=csc1, scalar=1.0, in1=n1g, op0=ALU.add, op1=ALU.mult),
               [mod_reg, consts_reg], [a_reg])
        A2 = A1[:, 32:48]
        a2_reg = Region("A2")

        bigA = Arena(BIG, 18432)
        xt = [bigA.f32(2048) for _ in range(2)]
        xs = [bigA.f32(2048) for _ in range(2)]
        xt_reg = [Region("xt%d" % i) for i in range(2)]
        xs_reg = [Region("xs%d" % i) for i in range(2)]
        junk = bigA.f32(2048)
        junk_reg = Region("junk")
        st_small = MISC_t[:, 64:128]
        st_reg = [Region("st%d" % i) for i in range(2)]
        for t in range(18):
            bi = t % 2
            src = x_d[t * 128:(t + 1) * 128, :] if t < 16 else ctx_d[(t - 16) * 128:(t - 15) * 128, :]
            P.dma("sp", "ld", xt[bi], src, writes=[xt_reg[bi]])
            ssq = st_small[:, bi * 4:bi * 4 + 1]
            rstd = st_small[:, bi * 4 + 1:bi * 4 + 2]
            P.emit("act", lambda s, ssq=ssq: s.activation(out=ssq, in_=epsc[:, 0:1], func=AF.Copy, scale=0.0), [c2], [st_reg[bi]])
            P.emit("act", lambda s, bi=bi, ssq=ssq: s.activation(out=junk, in_=xt[bi], func=AF.Square, accum_out=ssq),
                   [xt_reg[bi]], [junk_reg, st_reg[bi]])
            rstd_act(rstd, ssq, 1.0 / D, epsc[:, 0:1], [st_reg[bi]], [st_reg[bi]])
            P.emit("dve", lambda v, bi=bi, rstd=rstd: v.tensor_tensor(out=xs[bi], in0=xt[bi], in1=rstd.to_broadcast([128, 2048]), op=ALU.mult),
                   [xt_reg[bi], st_reg[bi]], [xs_reg[bi]])
            Acol = A1[:, 0:16] if t < 16 else A1[:, 16:32]
            Bcol = sh1 if t < 16 else csh1
            for q in range(4):
                pb = 2 + (t * 4 + q) % 4
                for kk in range(4):
                    k = q * 4 + kk
                    P.transpose(PS[pb][:, kk * 128:(kk + 1) * 128], xs[bi][:, k * 128:(k + 1) * 128], ident_f[:, :],
                                [xs_reg[bi], consts_reg], [psr[pb]])
                for kk in range(4):
                    k = q * 4 + kk
                    dst = hcol(k, t * 128, 128)
                    if kk % 2 == 0:
                        P.emit("act", lambda s, pb=pb, kk=kk, k=k, dst=dst, Acol=Acol, Bcol=Bcol: s.activation(
                            out=dst, in_=PS[pb][:, kk * 128:(kk + 1) * 128], func=AF.Identity,
                            scale=Acol[:, k:k + 1], bias=Bcol[:, k:k + 1]), [psr[pb], a_reg, mod_reg], [hT_reg[t]])
                    else:
                        P.emit("dve", lambda v, pb=pb, kk=kk, k=k, dst=dst, Acol=Acol, Bcol=Bcol: v.tensor_scalar(
                            out=dst, in0=PS[pb][:, kk * 128:(kk + 1) * 128], scalar1=Acol[:, k:k + 1], scalar2=Bcol[:, k:k + 1],
                            op0=ALU.mult, op1=ALU.add), [psr[pb], a_reg, mod_reg], [hT_reg[t]])
            if t < 16:
                ada_group(8 + t)
        P.emit("dve", lambda v: v.scalar_tensor_tensor(out=A2, in0=sc2, scalar=1.0, in1=n2g, op0=ALU.add, op1=ALU.mult),
               [mod_reg2, consts_reg], [a2_reg])

        def finish():
            final_waits = [("out", P.cnt.get("out", 0))]
            sem_keys = sorted(P.cnt.keys())
            sems = {k: es.enter_context(nc.semaphore("s_" + k)) for k in sem_keys}
            with nc.Block() as block:
                P.replay(block, sems, final_waits)

        def dump_f32(name, ap, n):
            P.barrier()
            stage = BIG[:, 0:n] if ap.dtype != F32 else None
            if stage is not None:
                rg = Region("dump")
                P.emit("dve", lambda v: v.tensor_copy(stage, ap), [], [rg])
                P.dma("sp", "out", dbg_d[name][:, :], stage, reads=[rg])
            else:
                P.dma("sp", "out", dbg_d[name][:, :], ap)
            P.barrier()

        if STOP_AFTER == "B":
            P.barrier()
            dq = bigA.f32(2304)
            dq_reg = Region("dq")
            for k in range(KC):
                for half, src3 in enumerate((hTo3, hTx3)):
                    P.emit("dve", lambda v, k=k, src3=src3, half=half: v.tensor_copy(dq[:, half * TOK:(half + 1) * TOK], src3[:, k, :]),
                           hT_reg, [dq_reg])
                P.dma("sp", "out", dbg_d["hT"][:, k * TOKA:(k + 1) * TOKA], dq, reads=[dq_reg])
            P.dma("sp", "out", dbg_d["modT"][:, :], modT[:, :], reads=[mod_reg])
            finish()
            return nc

        wb_i = [0]

        def load_w(src, nbf=4096, nslots=4):
            i = wb_i[0] % nslots
            wb_i[0] += 1
            v = WBb[:, i * 4096:i * 4096 + nbf]
            P.dma("pool", "w", v, src, writes=[wb_reg[i]])
            return v, wb_reg[i]
        ip_rot = [0]

        def ipb():
            b = ip_rot[0] % 3
            ip_rot[0] += 1
            return b

        def rope_evac(pb0, pb1, n, dst0, dst1, cosv, sinv, t1, t2, treg, tabreg, dreg):
            p0 = PS[pb0][:, 0:n]
            p1 = PS[pb1][:, 0:n]
            P.emit("dve", lambda v: v.tensor_tensor(out=t1, in0=p0, in1=cosv, op=ALU.mult), [psr[pb0], tabreg], [treg])
            P.emit("dve", lambda v: v.tensor_tensor(out=t2, in0=p1, in1=sinv, op=ALU.mult), [psr[pb1], tabreg], [treg])
            P.emit("dve", lambda v: v.tensor_tensor(out=dst0, in0=t1, in1=t2, op=ALU.subtract), [treg], [dreg])
            P.emit("dve", lambda v: v.tensor_tensor(out=t1, in0=p1, in1=cosv, op=ALU.mult), [psr[pb1], tabreg, treg], [treg])
            P.emit("dve", lambda v: v.tensor_tensor(out=t2, in0=p0, in1=sinv, op=ALU.mult), [psr[pb0], tabreg, treg], [treg])
            P.emit("dve", lambda v: v.tensor_tensor(out=dst1, in0=t1, in1=t2, op=ALU.add), [treg], [dreg])

        GAb = GA_t[:, :].bitcast(BF16)
        gaT3 = _r3(GAb[:, 0:4608], 2, TOKA)
        guT = GAb[:, 4608:6656]
        ga_reg = Region("gaT")
        gu_reg = Region("guT")

        def gla_z(h, tile, dr, zb):
            for c in range(2):
                col = dr * 1024 + h * 256 + c * 128
                P.mm(PS[zb][:, dr * 256 + c * 128:dr * 256 + (c + 1) * 128],
                     [(guT[0:17, col:col + 128], gaT3[0:17, dr, tile * 128:(tile + 1) * 128])], [gu_reg, ga_reg], [psr[zb]])

        def gla_rest(T, dr, need_q, zb):
            E, Lb, ek, eb, enb, nb, sdec, dreg, elreg, nreg = T["E"], T["L"], T["ek"], T["eb"], T["enb"], T["nb"], T["sdec"], T["dreg"], T["elreg"], T["nreg"]
            P.emit("act", lambda s: s.activation(out=E, in_=PS[zb][:, dr * 256:(dr + 1) * 256], func=AF.Exp, scale=-1.0), [psr[zb]], [elreg])
            P.emit("act", lambda s: s.activation(out=E, in_=E, func=AF.Ln, bias=ONE), [elreg, c2], [elreg])
            E3 = _r3(E, 2, 128)
            L3 = _r3(Lb, 2, 128)
            for c in range(2):
                if dr == 0:
                    P.emit("dve", lambda v, c=c: v.tensor_tensor_scan(L3[:, c, :], E3[:, c, :], E3[:, c, :], 0.0, ALU.add, ALU.bypass),
                           [elreg], [elreg])
                else:
                    P.emit("dve", lambda v, c=c: v.tensor_tensor_scan(L3[:, c, ::-1], E3[:, c, ::-1], E3[:, c, ::-1], 0.0, ALU.add, ALU.bypass),
                           [elreg], [elreg])
            li = 127 if dr == 0 else 0
            Llast = L3[:, :, li:li + 1]
            P.emit("act", lambda s: s.activation(out=nb.unsqueeze(2), in_=Llast, func=AF.Copy, scale=-1.0 / 16), [elreg], [nreg])
            ek3 = _r3(ek, 2, 128)
            for c in range(2):
                P.emit("act", lambda s, c=c: s.activation(out=ek3[:, c, :], in_=L3[:, c, :], func=AF.Exp, scale=1.0 / 16, bias=nb[:, c:c + 1]),
                       [elreg, nreg], [dreg])
            P.emit("act", lambda s: s.activation(out=sdec.unsqueeze(2), in_=Llast, func=AF.Exp, scale=-1.0 / 16), [elreg], [nreg])
            if need_q:
                P.emit("act", lambda s: s.activation(out=eb, in_=Lb, func=AF.Exp, scale=-1.0 / 16), [elreg], [dreg])
                P.emit("act", lambda s: s.activation(out=enb, in_=Lb, func=AF.Exp, scale=1.0 / 16), [elreg], [dreg])
            return ek3, [sdec[:, 0:1], sdec[:, 1:2]]

        def k_mult(T, kT_tile3, kreg, ek3, dcy_regs, keng="dve"):
            khT3 = _r3(T["khT"], 2, 128)
            P.emit(keng, lambda e: e.tensor_tensor(out=khT3, in0=kT_tile3, in1=ek3, op=ALU.mult), [kreg] + dcy_regs, [T["khreg"]])

        def k_tr(T, cb, off=768):
            khT3 = _r3(T["khT"], 2, 128)
            for c in range(2):
                P.transpose(PSb[cb][:, off + c * 128:off + (c + 1) * 128], khT3[:, c, :], ident_b[:, :], [T["khreg"], c2], [psr[cb]])

        def k_copy(T, cb, off=768):
            kh = T["kh"]
            P.emit("act", lambda s: s.activation(out=kh, in_=PSb[cb][:, off:off + 256], func=AF.Copy), [psr[cb]], [T["khreg"]])

        def supdate(T, v_tile, vreg, sdecs, dcy_regs, S32, S16, sreg, dv, same_scalar):
            kh, kreg2 = T["kh"], T["khreg"]
            S3 = _r3(S32, 2, 512)
            if same_scalar and dv == 256:
                for c in range(2):
                    P.mm(PS[7][:, c * 256:(c + 1) * 256], [(kh[:, c * 128:(c + 1) * 128], v_tile)], [kreg2, vreg], [psr[7]])
                P.emit("dve", lambda v: v.scalar_tensor_tensor(
                    out=S3[:, :, 0:256], in0=S3[:, :, 0:256], scalar=sdecs[0], in1=_r3(PS[7][:, 0:512], 2, 256), op0=ALU.mult, op1=ALU.add),
                    [psr[7], sreg] + dcy_regs, [sreg])
            else:
                for c in range(2):
                    P.mm(PS[7][:, 0:dv], [(kh[:, c * 128:(c + 1) * 128], v_tile)], [kreg2, vreg], [psr[7]])
                    P.emit("dve", lambda v, c=c: v.scalar_tensor_tensor(
                        out=S3[:, c, 0:dv], in0=S3[:, c, 0:dv], scalar=sdecs[c], in1=PS[7][:, 0:dv], op0=ALU.mult, op1=ALU.add),
                        [psr[7], sreg] + dcy_regs, [sreg])
            if S16 is not None:
                S163 = _r3(S16[0], 2, 512)
                P.emit("act", lambda s: s.activation(out=S163[:, :, 0:dv], in_=S3[:, :, 0:dv], func=AF.Copy), [sreg], [S16[1]])

        lgam_reg = Region("lgam")
        P.emit("act", lambda s: s.activation(out=lg[:, 16:32], in_=lg[:, 0:16], func=AF.Exp, scale=-1.0), [consts_reg], [lgam_reg])
        P.emit("act", lambda s: s.activation(out=lg[:, 16:32], in_=lg[:, 16:32], func=AF.Ln, bias=ONE), [lgam_reg, c2], [lgam_reg])
        P.emit("act", lambda s: s.activation(out=lg[:, 16:32], in_=lg[:, 16:32], func=AF.Copy, scale=-1.0), [lgam_reg], [lgam_reg])
        P.emit("act", lambda s: s.activation(out=lg[:, 32:48], in_=lg[:, 16:32], func=AF.Exp, scale=128.0), [lgam_reg], [lgam_reg])

        def ret_tables(rtab3, rtreg, h, dirs_kinds):
            for dr, kind in dirs_kinds:
                col = 16 + dr * 8 + h
                if kind == 2:
                    io = iotas[:, 2 if dr == 0 else 3, :]
                else:
                    io = iotas[:, 0 if dr == 0 else 1, :]
                if kind == 1:
                    P.emit("act", lambda s, col=col: s.activation(out=lg[:, 56:57], in_=lg[:, col:col + 1], func=AF.Copy, scale=-1.0),
                           [lgam_reg], [rtreg])
                    P.emit("act", lambda s, dr=dr, kind=kind, io=io: s.activation(out=rtab3[:, dr * 3 + kind, :], in_=io, func=AF.Exp, scale=lg[:, 56:57]),
                           [rtreg, consts_reg], [rtreg])
                else:
                    P.emit("act", lambda s, dr=dr, kind=kind, io=io, col=col: s.activation(out=rtab3[:, dr * 3 + kind, :], in_=io, func=AF.Exp, scale=lg[:, col:col + 1]),
                           [lgam_reg, consts_reg], [rtreg])

        P.barrier()
        bigA.reset(0)
        c0_kT = [bigA.bf16(2 * TOK) for _ in range(2)]
        c0_v = [bigA.bf16(9 * 512) for _ in range(2)]
        c0k_reg = [Region("c0k%d" % i) for i in range(2)]
        c0v_reg = [Region("c0v%d" % i) for i in range(2)]
        ropeO = bigA.f32(2 * 896)
        ropeO_reg = Region("ropeO")
        P.dma("sp", "ld", ropeO[:, 0:896], rope_d[:, TOK:L], writes=[ropeO_reg])
        P.dma("sp", "ld", ropeO[:, 896:1792], rope_d[:, L + TOK:2 * L], writes=[ropeO_reg])

        def mk_big(A, tag, shared):
            T = {}
            T["E"] = shared[0]
            T["L"] = shared[1]
            T["elreg"] = shared[2]
            T["ek"] = A.f32(256)
            T["eb"] = A.f32(256)
            T["enb"] = A.f32(256)
            T["dreg"] = Region("dcy" + tag)
            return T

        def mk_small(A, A4, tag):
            T = {}
            T["nb"] = A4.f32(2)
            T["sdec"] = A4.f32(2)
            T["nreg"] = Region("nb" + tag)
            T["khT"] = A.bf16(256)
            T["kh"] = A.bf16(256)
            T["khreg"] = Region("kh" + tag)
            T["qt"] = A.bf16(256)
            T["kt"] = A.bf16(256)
            T["PT"] = A.bf16(128)
            T["qreg"] = Region("qt" + tag)
            T["preg"] = Region("pt" + tag)
            return T

        def mk_temps2(Abig, Asmall, A4, tag, shared):
            out = []
            for dr in range(2):
                big = mk_big(Abig, "%s_%d" % (tag, dr), shared)
                row = []
                for par in range(2):
                    Asm = Asmall[dr * 2 + par] if isinstance(Asmall, list) else Asmall
                    T = dict(big)
                    T.update(mk_small(Asm, A4, "%s_%d_%d" % (tag, dr, par)))
                    row.append(T)
                out.append(row)
            return out
        sh0 = (bigA.f32(256), bigA.f32(256), Region("el0"))
        T0 = mk_temps2(bigA, bigA, bigA, "c0", sh0)
        c0_S32 = [bigA.f32(1024) for _ in range(2)]
        c0_Sreg = [Region("c0S%d" % i) for i in range(2)]
        c0_rtab = bigA.f32(6 * 128)
        c0_rtab3 = _r3(c0_rtab, 6, 128)
        c0_rtreg = Region("c0rt")
        c0_t1 = bigA.f32(128)
        c0_t2 = bigA.f32(128)
        c0_treg = Region("c0t")
        st_dreg = [[Region("std%d_%d" % (i, j)) for j in range(2)] for i in range(12)]

        P.emit("dve", lambda v: v.memset(GAb[0:32, 0:4608], 1.0), [], [ga_reg])
        if EXP2:
            P.barrier()
        P.dma("pool", "w", guT[0:17, :], gu_d[:, :], writes=[gu_reg])
        wl, wlreg = load_w(wlr_d[:, :], nbf=512)
        wl3 = _r3(wl, KC, 32)
        if EXP1:
            P.barrier()
        for dr in range(2):
            for blk in range(6):
                pb = ipb()
                c0 = blk * 384
                P.mm(PS[pb][0:16, 0:384], [(wl3[:, k, dr * 16:(dr + 1) * 16], hcol(k, c0, 384)) for k in range(KC)],
                     [wlreg] + hT_reg[blk * 3:blk * 3 + 3], [psr[pb]])
                P.emit("act", lambda s, pb=pb, dr=dr, c0=c0: s.activation(out=gaT3[0:16, dr, c0:c0 + 384], in_=PS[pb][0:16, 0:384], func=AF.Copy),
                       [psr[pb]], [ga_reg])

        step_i = [0]

        def c0_inproj(hd):
            is_ret = hd < 8
            h = hd if is_ret else hd - 8
            dv = 256 if is_ret else 512
            s = hd % 2
            if is_ret:
                Wk, wkreg = load_w(wr_d[h * 4 + 1])
                Wv = [load_w(wr_d[h * 4 + 2])]
            else:
                Wk, wkreg = load_w(wg_d[h * 6 + 1])
                Wv = [load_w(wg_d[h * 6 + 2]), load_w(wg_d[h * 6 + 3])]
            Wk3 = _r3(Wk, KC, 256)
            kT3 = _r3(c0_kT[s], 2, TOK)
            v3 = _r3(c0_v[s], 9, 512)
            for blk in range(3):
                if blk > 0:
                    yield
                c0 = TOK + blk * 384
                pbs = [ipb(), ipb()]
                for c in range(2):
                    P.mm(PS[pbs[c]][:, 0:384], [(Wk3[:, k, c * 128:(c + 1) * 128], hcol(k, c0, 384)) for k in range(KC)],
                         [wkreg] + hT_reg[9 + blk * 3:12 + blk * 3], [psr[pbs[c]]])
                for tt in range(3):
                    lt = blk * 3 + tt
                    tile = 9 + lt
                    sl = slice(tt * 128, (tt + 1) * 128)
                    if is_ret and tile < 16:
                        rc = slice(lt * 128, (lt + 1) * 128)
                        cosv = ropeO[:, 0:896][:, rc]
                        sinv = ropeO[:, 896:1792][:, rc]
                        p0 = PS[pbs[0]][:, sl]
                        p1 = PS[pbs[1]][:, sl]
                        d0 = kT3[:, 0, lt * 128:(lt + 1) * 128]
                        d1 = kT3[:, 1, lt * 128:(lt + 1) * 128]
                        P.emit("dve", lambda v, p0=p0, cosv=cosv: v.tensor_tensor(out=c0_t1, in0=p0, in1=cosv, op=ALU.mult), [psr[pbs[0]], ropeO_reg], [c0_treg])
                        P.emit("dve", lambda v, p1=p1, sinv=sinv: v.tensor_tensor(out=c0_t2, in0=p1, in1=sinv, op=ALU.mult), [psr[pbs[1]], ropeO_reg], [c0_treg])
                        P.emit("dve", lambda v, d0=d0: v.tensor_tensor(out=d0, in0=c0_t1, in1=c0_t2, op=ALU.subtract), [c0_treg], [c0k_reg[s]])
                        P.emit("dve", lambda v, p1=p1, cosv=cosv: v.tensor_tensor(out=c0_t1, in0=p1, in1=cosv, op=ALU.mult), [psr[pbs[1]], ropeO_reg, c0_treg], [c0_treg])
                        P.emit("dve", lambda v, p0=p0, sinv=sinv: v.tensor_tensor(out=c0_t2, in0=p0, in1=sinv, op=ALU.mult), [psr[pbs[0]], ropeO_reg, c0_treg], [c0_treg])
                        P.emit("dve", lambda v, d1=d1: v.tensor_tensor(out=d1, in0=c0_t1, in1=c0_t2, op=ALU.add), [c0_treg], [c0k_reg[s]])
                    else:
                        for c in range(2):
                            P.emit("act", lambda sc, c=c, sl=sl, lt=lt, pbc=pbs[c]: sc.activation(
                                out=kT3[:, c, lt * 128:(lt + 1) * 128], in_=PS[pbc][:, sl], func=AF.Copy), [psr[pbs[c]]], [c0k_reg[s]])
            for lt in range(9):
                yield
                tile = 9 + lt
                pb = ipb()
                for i, (Wvv, wvreg) in enumerate(Wv):
                    Wv3 = _r3(Wvv, KC, 256)
                    P.mm(PS[pb][:, i * 256:(i + 1) * 256], [(hcol(k, tile * 128, 128), Wv3[:, k, :]) for k in range(KC)],
                         [wvreg, hT_reg[tile]], [psr[pb]])
                P.emit("act", lambda sc, lt=lt, pb=pb: sc.activation(out=v3[:, lt, 0:dv], in_=PS[pb][:, 0:dv], func=AF.Copy), [psr[pb]], [c0v_reg[s]])
            yield

        def c0_scan(hd, filler):
            is_ret = hd < 8
            h = hd if is_ret else hd - 8
            dv = 256 if is_ret else 512
            s = hd % 2
            kT3 = _r3(c0_kT[s], 2, TOK)
            v3 = _r3(c0_v[s], 9, 512)
            if is_ret:
                ret_tables(c0_rtab3, c0_rtreg, h, [(0, 2), (1, 2)])
            for dr in range(2):
                P.emit("act", lambda sc, S=c0_S32[dr]: sc.memzero(S), [], [c0_Sreg[dr]])
            orders = [[16, 17], [17, 16, 15, 14, 13, 12, 11, 10, 9]]
            ZB = [5, 5]
            info = {}

            def live(i):
                return [dr for dr in range(2) if 0 <= i < len(orders[dr])]

            def stA(i):
                if not is_ret:
                    for dr in live(i):
                        gla_z(h, orders[dr][i], dr, ZB[dr])

            def stB(i):
                for dr in live(i):
                    tile = orders[dr][i]
                    lt = tile - 9
                    T = T0[dr][i % 2]
                    if is_ret:
                        ek3 = c0_rtab3[:, dr * 3 + 2, :].unsqueeze(1).to_broadcast([128, 2, 128])
                        col = 32 + dr * 8 + h
                        sdecs = [lg[:, col:col + 1], lg[:, col:col + 1]]
                        dregs = [c0_rtreg, lgam_reg]
                    else:
                        ek3, sdecs = gla_rest(T, dr, False, ZB[dr])
                        dregs = [T["dreg"], T["nreg"]]
                    k_mult(T, kT3[:, :, lt * 128:(lt + 1) * 128], c0k_reg[s], ek3, dregs)
                    info[(i, dr)] = (lt, sdecs, dregs)

            def stC(i):
                for dr in live(i):
                    k_tr(T0[dr][i % 2], 3, 512 + dr * 256)

            def stD(i):
                for dr in live(i):
                    k_copy(T0[dr][i % 2], 3, 512 + dr * 256)

            def stE(i):
                for dr in live(i):
                    lt, sdecs, dregs = info[(i, dr)]
                    supdate(T0[dr][i % 2], v3[:, lt, 0:dv], c0v_reg[s], sdecs, dregs, c0_S32[dr], None, c0_Sreg[dr], dv, is_ret)
            stA(0)
            stB(0)
            for i in range(9):
                stA(i + 1)
                stB(i + 1)
                stC(i)
                filler()
                stD(i)
                stE(i)
                filler()
            for dr in range(2):
                P.dma("sp", "st", ST_d[hd * 2 + dr][:, :], c0_S32[dr], reads=[c0_Sreg[dr]], writes=[st_dreg[hd][dr]])

        gens0 = [None]

        def filler0():
            g = gens0[0]
            if g is None:
                return
            try:
                next(g)
            except StopIteration:
                gens0[0] = None

        def drain0():
            while gens0[0] is not None:
                filler0()
        gens0[0] = c0_inproj(0)
        drain0()
        for hd_ in range(12):
            gens0[0] = c0_inproj(hd_ + 1) if hd_ + 1 < 12 else None
            c0_scan(hd_, filler0)
            drain0()

        if STOP_AFTER == "C0":
            P.barrier()
            for i in range(24):
                P.dma("sp", "out", dbg_d["st"][i * 128:(i + 1) * 128, :], ST_d[i][:, :])
            rgd = Region("dmpg")
            P.emit("dve", lambda v: v.tensor_copy(BIG[0:32, 0:4608], GAb[0:32, 0:4608]), [], [rgd])
            P.dma("sp", "out", dbg_d["ga"][:, :], BIG[0:32, 0:4608], reads=[rgd])
            finish()
            return nc

        P.barrier()
        bigA.reset(0)
        sets = []
        for s in range(2):
            st = {"qT": bigA.bf16(2 * TOK), "kT": bigA.bf16(2 * TOK), "v": bigA.bf16(9 * 512), "gs": bigA.bf16(9 * 512),
                  "qreg": Region("q%d" % s), "kreg": Region("k%d" % s), "vreg": Region("v%d" % s), "greg": Region("g%d" % s)}
            sets.append(st)
        o_f = bigA.f32(9 * 512)
        o_f3 = _r3(o_f, 9, 512)
        of_reg = Region("o_f")
        cA = Arena(HT[:, 9216:18432], 9216)
        mA = Arena(MISC_t[:, 128:2560], 2432)
        ropeN = cA.f32(2 * TOK)
        ropeN_reg = Region("ropeN")
        P.dma("sp", "ld", ropeN[:, 0:TOK], rope_d[:, 0:TOK], writes=[ropeN_reg])
        P.dma("sp", "ld", ropeN[:, TOK:2 * TOK], rope_d[:, L:L + TOK], writes=[ropeN_reg])
        shC = (cA.f32(256), cA.f32(256), Region("elC"))
        gaHole = [Arena(GA_t[:, 576:1152], 576), Arena(GA_t[:, 1728:2304], 576)]
        TC = mk_temps2(cA, [cA, gaHole[0], cA, gaHole[1]], mA, "c", shC)
        rtab = cA.f32(6 * 128)
        rtab3 = _r3(rtab, 6, 128)
        rtreg = Region("rt")
        S32 = [WB_t[:, 6144 + i * 1024:6144 + (i + 1) * 1024] for i in range(2)]
        Sreg = [Region("S32_%d" % i) for i in range(2)]
        S16 = [(cA.bf16(1024), Region("S16_%d" % i)) for i in range(2)]
        rp_t1 = cA.f32(384)
        rp_t2 = cA.f32(384)
        rp_reg = Region("rp")
        otot = mA.f32(512)
        ntmp = mA.f32(512)
        ro_reg = Region("ro")
        rb = [mA.bf16(512) for _ in range(2)]
        rb_reg = [Region("rb%d" % i) for i in range(2)]
        stats = mA.f32(16)
        silt = mA.f32(512)
        sil_reg = Region("sil")
        grow = [cA.f32(512) for _ in range(2)]
        grow_reg = [Region("grow%d" % i) for i in range(2)]
        R_reg = Region("R_d")

        def in_proj(hd, s):
            is_ret = hd < 8
            h = hd if is_ret else hd - 8
            dv = 256 if is_ret else 512
            st = sets[s]
            qT3 = _r3(st["qT"], 2, TOK)
            kT3 = _r3(st["kT"], 2, TOK)
            v3 = _r3(st["v"], 9, 512)
            gs3 = _r3(st["gs"], 9, 512)
            base = wr_d if is_ret else wg_d
            nsub = 4 if is_ret else 6
            gsrc = (rg_d if is_ret else gg_d)[0:1, h * dv:(h + 1) * dv]
            P.dma("sp", "ld", grow[s][:, 0:dv], gsrc.partition_broadcast(128), writes=[grow_reg[s]])
            for qi, (dst3, dreg) in enumerate(((qT3, st["qreg"]), (kT3, st["kreg"]))):
                W, wreg = load_w(base[h * nsub + qi], nslots=3)
                W3 = _r3(W, KC, 256)
                for blk in range(3):
                    c0 = blk * 384
                    pbs = [ipb(), ipb()]
                    for c in range(2):
                        P.mm(PS[pbs[c]][:, 0:384], [(W3[:, k, c * 128:(c + 1) * 128], hcol(k, c0, 384)) for k in range(KC)],
                             [wreg] + hT_reg[blk * 3:blk * 3 + 3], [psr[pbs[c]]])
                    if is_ret:
                        rope_evac(pbs[0], pbs[1], 384, dst3[:, 0, c0:c0 + 384], dst3[:, 1, c0:c0 + 384],
                                  ropeN[:, c0:c0 + 384], ropeN[:, TOK + c0:TOK + c0 + 384], rp_t1, rp_t2, rp_reg, ropeN_reg, dreg)
                    else:
                        for c in range(2):
                            P.emit("act", lambda sc, c=c, c0=c0, pbc=pbs[c], dst3=dst3: sc.activation(
                                out=dst3[:, c, c0:c0 + 384], in_=PS[pbc][:, 0:384], func=AF.Copy), [psr[pbs[c]]], [dreg])
                    yield
            vsubs = [2] if is_ret else [2, 3]
            Wv = [load_w(base[h * nsub + i], nslots=3) for i in vsubs]
            for tile in range(9):
                pb = ipb()
                for i, (Wvv, wvreg) in enumerate(Wv):
                    Wv3 = _r3(Wvv, KC, 256)
                    P.mm(PS[pb][:, i * 256:(i + 1) * 256], [(hcol(k, tile * 128, 128), Wv3[:, k, :]) for k in range(KC)],
                         [wvreg, hT_reg[tile]], [psr[pb]])
                P.emit("act", lambda sc, tile=tile, pb=pb: sc.activation(out=v3[:, tile, 0:dv], in_=PS[pb][:, 0:dv], func=AF.Copy),
                       [psr[pb]], [st["vreg"]])
                yield
            gsubs = [3] if is_ret else [4, 5]
            Wg = [load_w(base[h * nsub + i], nslots=3) for i in gsubs]
            for tile in range(9):
                pb = ipb()
                for i, (Wgg, wgreg) in enumerate(Wg):
                    Wg3 = _r3(Wgg, KC, 256)
                    P.mm(PS[pb][:, i * 256:(i + 1) * 256], [(hcol(k, tile * 128, 128), Wg3[:, k, :]) for k in range(KC)],
                         [wgreg, hT_reg[tile]], [psr[pb]])
                P.emit("act", lambda sc, pb=pb: sc.activation(out=silt[:, 0:dv], in_=PS[pb][:, 0:dv], func=AF.Silu), [psr[pb]], [sil_reg])
                P.emit("pool", lambda e, tile=tile: e.tensor_tensor(out=gs3[:, tile, 0:dv], in0=silt[:, 0:dv], in1=grow[s][:, 0:dv], op=ALU.mult),
                       [sil_reg, grow_reg[s]], [st["greg"]])
                yield

        def scan(hd, s, filler):
            is_ret = hd < 8
            h = hd if is_ret else hd - 8
            dv = 256 if is_ret else 512
            st = sets[s]
            qT3 = _r3(st["qT"], 2, TOK)
            kT3 = _r3(st["kT"], 2, TOK)
            v3 = _r3(st["v"], 9, 512)
            gs3 = _r3(st["gs"], 9, 512)
            for dr in range(2):
                P.dma("sp", "st", S32[dr], ST_d[hd * 2 + dr][:, :], reads=[st_dreg[hd][dr]], writes=[Sreg[dr]])
                S3 = _r3(S32[dr], 2, 512)
                S163 = _r3(S16[dr][0], 2, 512)
                P.emit("act", lambda sc, S3=S3, S163=S163: sc.activation(out=S163[:, :, 0:dv], in_=S3[:, :, 0:dv], func=AF.Copy),
                       [Sreg[dr]], [S16[dr][1]])
            if is_ret:
                ret_tables(rtab3, rtreg, h, [(0, 0), (0, 1), (0, 2), (1, 0), (1, 1), (1, 2)])
            ZB = [5, 5]
            OB = [4, 6]
            info = {}

            def tile_of(t, dr):
                return t if dr == 0 else 8 - t

            def stA(t):
                if t < 9 and not is_ret:
                    for dr in range(2):
                        gla_z(h, tile_of(t, dr), dr, ZB[dr])

            def stB(t):
                if t >= 9:
                    return
                for dr in range(2):
                    tile = tile_of(t, dr)
                    T = TC[dr][t % 2]
                    tsl = slice(tile * 128, (tile + 1) * 128)
                    if is_ret:
                        eb3 = rtab3[:, dr * 3 + 0, :].unsqueeze(1).to_broadcast([128, 2, 128])
                        enb3 = rtab3[:, dr * 3 + 1, :].unsqueeze(1).to_broadcast([128, 2, 128])
                        ek3 = rtab3[:, dr * 3 + 2, :].unsqueeze(1).to_broadcast([128, 2, 128])
                        col = 32 + dr * 8 + h
                        sdecs = [lg[:, col:col + 1], lg[:, col:col + 1]]
                        dregs = [rtreg, lgam_reg]
                    else:
                        ek3, sdecs = gla_rest(T, dr, True, ZB[dr])
                        eb3 = _r3(T["eb"], 2, 128)
                        enb3 = _r3(T["enb"], 2, 128)
                        dregs = [T["dreg"], T["nreg"]]
                    qt3 = _r3(T["qt"], 2, 128)
                    kt3 = _r3(T["kt"], 2, 128)
                    P.emit("dve", lambda v, qt3=qt3, eb3=eb3, tsl=tsl: v.tensor_tensor(out=qt3, in0=qT3[:, :, tsl], in1=eb3, op=ALU.mult),
                           [st["qreg"]] + dregs, [T["qreg"]])
                    P.emit("dve", lambda v, kt3=kt3, enb3=enb3, tsl=tsl: v.tensor_tensor(out=kt3, in0=kT3[:, :, tsl], in1=enb3, op=ALU.mult),
                           [st["kreg"]] + dregs, [T["qreg"]])
                    if t < 8:
                        k_mult(T, kT3[:, :, tsl], st["kreg"], ek3, dregs, keng="pool")
                    info[(t, dr)] = (tile, sdecs, dregs, qt3, kt3)

            def stC(t):
                for dr in range(2):
                    tile, sdecs, dregs, qt3, kt3 = info[(t, dr)]
                    T = TC[dr][t % 2]
                    P.mm(PS[3][:, dr * 128:(dr + 1) * 128], [(kt3[:, c, :], qt3[:, c, :]) for c in range(2)], [T["qreg"]], [psr[3]])
                    if t < 8:
                        k_tr(T, 3, 512 + dr * 256)

            def stD(t):
                for dr in range(2):
                    T = TC[dr][t % 2]
                    mask = mf_b if dr == 0 else mb_b
                    PT = T["PT"]
                    P.emit("dve", lambda v, PT=PT, mask=mask, dr=dr: v.tensor_tensor(out=PT, in0=PS[3][:, dr * 128:(dr + 1) * 128], in1=mask[:, :], op=ALU.mult),
                           [psr[3], c2], [T["preg"]])
                    if t < 8:
                        k_copy(T, 3, 512 + dr * 256)

            def stE(t):
                for dr in range(2):
                    tile, sdecs, dregs, qt3, kt3 = info[(t, dr)]
                    T = TC[dr][t % 2]
                    PT = T["PT"]
                    ob = OB[dr]
                    S163 = _r3(S16[dr][0], 2, 512)
                    P.mm(PS[ob][:, 0:dv], [(PT, v3[:, tile, 0:dv])] + [(qt3[:, c, :], S163[:, c, 0:dv]) for c in range(2)],
                         [T["preg"], T["qreg"], st["vreg"], S16[dr][1]], [psr[ob]])
                    do_readout = (t > 4) if dr == 0 else (t >= 4)
                    if not do_readout:
                        P.emit("act", lambda sc, tile=tile, ob=ob: sc.activation(out=o_f3[:, tile, 0:dv], in_=PS[ob][:, 0:dv], func=AF.Copy),
                               [psr[ob]], [of_reg])
                    else:
                        readout(ob, is_ret, h, dv, tile, gs3, st)
                    if t < 8:
                        supdate(T, v3[:, tile, 0:dv], st["vreg"], sdecs, dregs, S32[dr], S16[dr], Sreg[dr], dv, is_ret)
            stA(0)
            stB(0)
            for t in range(9):
                stA(t + 1)
                stB(t + 1)
                stC(t)
                filler()
                stD(t)
                filler()
                stE(t)
                filler()
            if hd == DBG_OF_HEAD:
                P.barrier()
                P.dma("sp", "out", dbg_d["of"][:, :], o_f)
                finish()
                raise _Stop()

        ro_i = [0]

        def readout(ob, is_ret, h, dv, tile, gs3, st):
            P.emit("dve", lambda v: v.tensor_tensor(out=otot[:, 0:dv], in0=o_f3[:, tile, 0:dv], in1=PS[ob][:, 0:dv], op=ALU.add),
                   [psr[ob], of_reg], [ro_reg])
            mean = stats[:, 0:1]
            var = stats[:, 1:2]
            rs = stats[:, 2:3]
            nmr = stats[:, 3:4]
            i = ro_i[0] % 2
            ro_i[0] += 1
            if is_ret:
                P.emit("dve", lambda v: v.bn_stats(stats[:, 8:14], otot[:, 0:dv]), [ro_reg], [ro_reg])
                P.emit("dve", lambda v: v.bn_aggr(stats[:, 0:2], stats[:, 8:14]), [ro_reg], [ro_reg])
                rstd_act(rs, var, 1.0, epsc[:, 1:2], [ro_reg], [ro_reg])
                P.emit("dve", lambda v: v.scalar_tensor_tensor(out=nmr, in0=mean, scalar=-1.0, in1=rs, op0=ALU.mult, op1=ALU.mult), [ro_reg], [ro_reg])
                P.emit("act", lambda sc: sc.activation(out=ntmp[:, 0:dv], in_=otot[:, 0:dv], func=AF.Identity, scale=rs, bias=nmr), [ro_reg], [ro_reg])
                P.emit("dve", lambda v, i=i: v.tensor_tensor(out=rb[i][:, 0:dv], in0=ntmp[:, 0:dv], in1=gs3[:, tile, 0:dv], op=ALU.mult),
                       [ro_reg, st["greg"]], [rb_reg[i]])
                col0 = h * 256
            else:
                P.emit("act", lambda sc: sc.activation(out=var, in_=epsc[:, 0:1], func=AF.Copy, scale=0.0), [ro_reg, c2], [ro_reg])
                P.emit("act", lambda sc: sc.activation(out=ntmp[:, 0:dv], in_=otot[:, 0:dv], func=AF.Square, accum_out=var), [ro_reg], [ro_reg])
                rstd_act(rs, var, 1.0 / dv, epsc[:, 1:2], [ro_reg], [ro_reg])
                P.emit("dve", lambda v, i=i: v.scalar_tensor_tensor(out=rb[i][:, 0:dv], in0=otot[:, 0:dv], scalar=rs, in1=gs3[:, tile, 0:dv],
                                                                   op0=ALU.mult, op1=ALU.mult), [ro_reg, st["greg"]], [rb_reg[i]])
                col0 = 2048 + h * 512
            P.dma("sp", "rst", R_d[tile * 128:(tile + 1) * 128, col0:col0 + dv], rb[i][:, 0:dv], reads=[rb_reg[i]], writes=[R_reg])

        gens = [None]

        def filler(n=1):
            g = gens[0]
            if g is None:
                return
            for _ in range(n):
                try:
                    next(g)
                except StopIteration:
                    gens[0] = None
                    return

        def drain():
            while gens[0] is not None:
                filler()
        NHEADS = 12
        gens[0] = in_proj(0, 0)
        drain()
        try:
            for hd in range(NHEADS):
                gens[0] = in_proj(hd + 1, (hd + 1) % 2) if hd + 1 < NHEADS else None
                if NO_INTERLEAVE:
                    drain()
                scan(hd, hd % 2, filler)
                drain()
        except _Stop:
            return nc

        if STOP_AFTER == "C":
            P.barrier()
            rgd = Region("dmpR")
            stg = BIG.bitcast(BF16)[:, 0:4096]
            for tile in range(9):
                P.dma("sp", "ld", stg, R_d[tile * 128:(tile + 1) * 128, :], writes=[rgd])
                P.emit("dve", lambda v: v.tensor_copy(HT[:, 0:4096], stg), [rgd], [rgd])
                P.dma("sp", "out", dbg_d["R"][tile * 128:(tile + 1) * 128, :], HT[:, 0:4096], reads=[rgd], writes=[rgd])
            finish()
            return nc

        P.barrier()
        rT3 = _r3(BIG.bitcast(BF16), 32, TOK)
        rT_reg = Region("rT")
        hxb = HT[:, 9216:18432]
        rtile = [hxb[:, i * 2048:(i + 1) * 2048].bitcast(BF16) for i in range(2)]
        rtile_reg = [Region("rtile%d" % i) for i in range(2)]
        for tile in range(9):
            i = tile % 2
            P.dma("sp", "ld", rtile[i], R_d[tile * 128:(tile + 1) * 128, :], reads=[R_reg], writes=[rtile_reg[i]])
            for q in range(4):
                pb = (tile * 4 + q) % 8
                for j in range(8):
                    kc = q * 8 + j
                    P.transpose(PSb[pb][:, j * 128:(j + 1) * 128], rtile[i][:, kc * 128:(kc + 1) * 128], ident_b[:, :],
                                [rtile_reg[i], c2], [psr[pb]])
                dst = rT3[:, q * 8:(q + 1) * 8, tile * 128:(tile + 1) * 128]
                srcv = _r3(PSb[pb][:, 0:1024], 8, 128)
                if q % 2 == 0:
                    P.emit("act", lambda sc, dst=dst, srcv=srcv: sc.activation(out=dst, in_=srcv, func=AF.Copy), [psr[pb]], [rT_reg])
                else:
                    P.emit("dve", lambda v, dst=dst, srcv=srcv: v.tensor_copy(dst, srcv), [psr[pb]], [rT_reg])
        P.barrier()
        mT3 = _r3(hxb.bitcast(BF16), KC, TOK)
        mT_reg = Region("mT")
        gA = Arena(GA_t[:, :], 3328)
        dt1 = [gA.f32(384) for _ in range(2)]
        dt2 = [gA.f32(384) for _ in range(2)]
        dt_reg = [Region("dt%d" % i) for i in range(2)]

        wslot_v = [WBb[:, i * 4096:(i + 1) * 4096] for i in range(4)] + [MISC_t[:, 128:2176].bitcast(BF16)]
        wslot_r = wb_reg + [Region("wb4")]
        w5_i = [0]

        def load2(srcs):
            i = w5_i[0] % 5
            w5_i[0] += 1
            off = 0
            views = []
            for src, n in srcs:
                v = wslot_v[i][:, off:off + n]
                P.dma("pool", "w", v, src, writes=[wslot_r[i]])
                views.append(v)
                off += n
            return views, wslot_r[i]

        def d1_block(j, blk, wrp3, wgp3, wmg3, regA, regB):
            c0 = blk * 384
            par = (j * 3 + blk) % 2
            pbs = [par * 4 + i for i in range(4)]
            P.mm(PS[pbs[0]][:, 0:384], [(wrp3[:, k, :], rT3[:, k, c0:c0 + 384]) for k in range(KC)], [regA, rT_reg], [psr[pbs[0]]])
            P.mm(PS[pbs[1]][:, 0:384], [(wgp3[:, k, :], rT3[:, 16 + k, c0:c0 + 384]) for k in range(KC)], [regA, rT_reg], [psr[pbs[1]]])
            P.mm(PS[pbs[2]][:, 0:384], [(wmg3[:, k, 0:128], hTo3[:, k, c0:c0 + 384]) for k in range(KC)],
                 [regB] + hT_reg[blk * 3:blk * 3 + 3], [psr[pbs[2]]])
            P.mm(PS[pbs[3]][:, 0:384], [(wmg3[:, k, 128:256], hTo3[:, k, c0:c0 + 384]) for k in range(KC)],
                 [regB] + hT_reg[blk * 3:blk * 3 + 3], [psr[pbs[3]]])
            t1, t2, treg = dt1[par], dt2[par], dt_reg[par]
            P.emit("act", lambda sc: sc.activation(out=t1, in_=PS[pbs[2]][:, 0:384], func=AF.Sigmoid), [psr[pbs[2]]], [treg])
            P.emit("act", lambda sc: sc.activation(out=t2, in_=PS[pbs[3]][:, 0:384], func=AF.Sigmoid), [psr[pbs[3]]], [treg])
            P.emit("dve", lambda v: v.tensor_tensor(out=t1, in0=PS[pbs[0]][:, 0:384], in1=t1, op=ALU.mult), [psr[pbs[0]], treg], [treg])
            P.emit("dve", lambda v: v.tensor_tensor(out=t2, in0=PS[pbs[1]][:, 0:384], in1=t2, op=ALU.mult), [psr[pbs[1]], treg], [treg])
            P.emit("pool", lambda e: e.tensor_tensor(out=mT3[:, j, c0:c0 + 384], in0=t1, in1=t2, op=ALU.add), [treg], [mT_reg])

        for j in range(16):
            (wrp, wgp), regA = load2([(wrp_d[j], 2048), (wgp_d[j], 2048)])
            (wmg,), regB = load2([(wmg_d[j], 4096)])
            for blk in range(3):
                d1_block(j, blk, _r3(wrp, KC, 128), _r3(wgp, KC, 128), _r3(wmg, KC, 256), regA, regB)

        if STOP_AFTER == "D1":
            P.barrier()
            for k in range(KC):
                rgd = Region("dmp")
                P.emit("dve", lambda v, k=k: v.tensor_copy(BIG[:, 0:TOK], mT3[:, k, :]), [], [rgd])
                P.dma("sp", "out", dbg_d["mT"][:, k * TOK:(k + 1) * TOK], BIG[:, 0:TOK], reads=[rgd])
                P.barrier()
            finish()
            return nc

        P.barrier()
        x1T3 = _r3(BIG, KC, TOK)
        x1_reg = [Region("x1_%d" % k) for k in range(KC)]
        gA.reset(0)
        xtile = gA.f32(2048)
        xtile_reg = Region("xtile")
        for tile in range(9):
            P.dma("sp", "ld", xtile, x_d[tile * 128:(tile + 1) * 128, :], writes=[xtile_reg])
            for q in range(4):
                pb = (tile * 4 + q) % 8
                for kk in range(4):
                    k = q * 4 + kk
                    P.transpose(PS[pb][:, kk * 128:(kk + 1) * 128], xtile[:, k * 128:(k + 1) * 128], ident_f[:, :],
                                [xtile_reg, consts_reg], [psr[pb]])
                dst = x1T3[:, q * 4:(q + 1) * 4, tile * 128:(tile + 1) * 128]
                srcv = _r3(PS[pb][:, 0:512], 4, 128)
                if q % 2 == 0:
                    P.emit("act", lambda sc, dst=dst, srcv=srcv: sc.activation(out=dst, in_=srcv, func=AF.Copy), [psr[pb]], x1_reg[q * 4:(q + 1) * 4])
                else:
                    P.emit("dve", lambda v, dst=dst, srcv=srcv: v.tensor_copy(dst, srcv), [psr[pb]], x1_reg[q * 4:(q + 1) * 4])

        def d2_block(j, blk, wo3, rg):
            c0 = blk * 384
            pb = (j * 3 + blk) % 8
            P.mm(PS[pb][:, 0:384], [(wo3[:, k, :], mT3[:, k, c0:c0 + 384]) for k in range(KC)], [rg, mT_reg], [psr[pb]])
            P.emit("dve", lambda v: v.scalar_tensor_tensor(out=x1T3[:, j, c0:c0 + 384], in0=PS[pb][:, 0:384], scalar=g1[:, j:j + 1],
                                                          in1=x1T3[:, j, c0:c0 + 384], op0=ALU.mult, op1=ALU.add),
                   [psr[pb], x1_reg[j], mod_reg2], [x1_reg[j]])
        for j in range(16):
            (wo,), rg = load2([(wo_d[j], 2048)])
            for blk in range(3):
                d2_block(j, blk, _r3(wo, KC, 128), rg)

        P.barrier()
        h2T3 = _r3(hxb.bitcast(BF16), KC, TOK)
        h2_reg = Region("h2T")
        hA = Arena(HT[:, 0:9216], 9216)
        sqb = hA.bf16(16 * 512)
        sq_reg = Region("sq")
        rbc = hA.f32(512)
        rbc_reg = Region("rbc")
        tmpk = [hA.f32(512) for _ in range(2)]
        tmpk_reg = [Region("tmpk%d" % i) for i in range(2)]

        def ssq_bcast(c0, n, pb):
            sq3 = _r3(sqb[:, 0:16 * n], KC, n)
            P.emit("act", lambda sc: sc.activation(out=sq3, in_=x1T3[:, :, c0:c0 + n], func=AF.Square), x1_reg, [sq_reg])
            P.mm(PS[pb][:, 0:n], [(ones_b[:, :], sq3[:, k, :]) for k in range(KC)], [sq_reg, c2], [psr[pb]])
            rstd_act(rbc[:, 0:n], PS[pb][:, 0:n], 1.0 / D, epsc[:, 0:1], [psr[pb]], [rbc_reg])

        def d3_chunk(k, c0, n):
            i = k % 2
            P.emit("dve", lambda v: v.scalar_tensor_tensor(out=tmpk[i][:, 0:n], in0=x1T3[:, k, c0:c0 + n], scalar=A2[:, k:k + 1], in1=rbc[:, 0:n],
                                                          op0=ALU.mult, op1=ALU.mult), [x1_reg[k], a2_reg, rbc_reg], [tmpk_reg[i]])
            P.emit("act", lambda sc: sc.activation(out=h2T3[:, k, c0:c0 + n], in_=tmpk[i][:, 0:n], func=AF.Identity, bias=sh2[:, k:k + 1]),
                   [tmpk_reg[i], mod_reg2], [h2_reg])
        for blk in range(3):
            ssq_bcast(blk * 384, 384, blk)
            for k in range(KC):
                d3_chunk(k, blk * 384, 384)

        if STOP_AFTER == "D":
            P.barrier()
            for k in range(KC):
                P.dma("sp", "out", dbg_d["x1T"][:, k * TOK:(k + 1) * TOK], x1T3[:, k, :])
            for k in range(KC):
                rgd = Region("dmp")
                P.emit("dve", lambda v, k=k: v.tensor_copy(hA.ap[:, 0:TOK], h2T3[:, k, :]), [], [rgd])
                P.dma("sp", "out", dbg_d["h2T"][:, k * TOK:(k + 1) * TOK], hA.ap[:, 0:TOK], reads=[rgd])
                P.barrier()
            finish()
            return nc

        P.barrier()
        hA.reset(0)
        aTp = [hA.f32(20 * 66) for _ in range(2)]
        aTp3 = [_r3(a, 20, 66) for a in aTp]
        aT_reg = [Region("aT%d" % i) for i in range(2)]
        acc = [hA.f32(1024) for _ in range(2)]
        acc_reg = [Region("acc%d" % i) for i in range(2)]
        gact = [hA.f32(1024) for _ in range(2)]
        gact_reg = [Region("gact%d" % i) for i in range(2)]
        gvT = [hA.bf16(4 * 1024), GA_t[:, 0:2048].bitcast(BF16)]
        gv_reg = [Region("gv%d" % i) for i in range(2)]
        for i in range(2):
            P.emit("act", lambda sc, i=i: sc.memzero(aTp[i]), [], [aT_reg[i]])
        wdnv = WBb[:, 8192:16384]
        wdn3 = _r3(wdnv, 4, 2048)
        CW = convv[:, 0:NCC * 9]
        CB = convv[:, NCC * 9:NCC * 10]

        wu_of = {}

        def ffn_a(cc):
            g, ci = divmod(cc, 4)
            par = cc % 2
            slot = (0, 1, 4)[cc % 3]
            wu = wslot_v[slot]
            P.dma("pool", "w", wu, wup_d[cc], writes=[wslot_r[slot]])
            wu3 = _r3(wu, KC, 256)
            a3 = aTp3[par]
            for blk in range(3):
                pb = blk
                c0 = blk * 384
                P.mm(PS[pb][:, 0:384], [(wu3[:, k, 0:128], h2T3[:, k, c0:c0 + 384]) for k in range(KC)], [wslot_r[slot], h2_reg], [psr[pb]])
                P.emit("act", lambda sc, pb=pb, blk=blk: sc.activation(out=a3[:, 1 + blk * 6:7 + blk * 6, 1:65], in_=_r3(PS[pb][:, 0:384], 6, 64), func=AF.Copy),
                       [psr[pb]], [aT_reg[par]])
            acc3 = _r3(acc[par], 16, 64)
            for tap in range(9):
                i, jx = divmod(tap, 3)
                view = a3[:, i:i + 16, jx:jx + 64]
                w = CW[:, cc * 9 + tap:cc * 9 + tap + 1]
                if tap == 0:
                    P.emit("dve", lambda v, view=view, w=w: v.tensor_tensor(out=acc3, in0=view, in1=w.unsqueeze(2).to_broadcast([128, 16, 64]), op=ALU.mult),
                           [aT_reg[par], consts_reg], [acc_reg[par]])
                else:
                    P.emit("dve", lambda v, view=view, w=w: v.scalar_tensor_tensor(out=acc3, in0=view, scalar=w, in1=acc3, op0=ALU.mult, op1=ALU.add),
                           [aT_reg[par], consts_reg, acc_reg[par]], [acc_reg[par]])
            P.emit("act", lambda sc: sc.activation(out=gact[par], in_=acc[par], func=AF.Gelu_apprx_tanh, bias=CB[:, cc:cc + 1]),
                   [acc_reg[par], consts_reg], [gact_reg[par]])
            wu_of[cc] = (wu3, slot)

        def ffn_v(cc):
            g, ci = divmod(cc, 4)
            par = cc % 2
            wu3, slot = wu_of[cc]
            if ci == 0:
                for i in range(min(4, NCC - g * 4)):
                    P.dma("pool", "w", wdnv[:, i * 2048:(i + 1) * 2048], wdn_d[g * 4 + i], writes=[wb_reg[2], wb_reg[3]])
            gv3 = _r3(gvT[g % 2], 4, 1024)
            for tb in range(2):
                pb = 3 + tb
                P.mm(PS[pb][:, 0:512], [(wu3[:, k, 128:256], h2T3[:, k, tb * 512:(tb + 1) * 512]) for k in range(KC)],
                     [wslot_r[slot], h2_reg], [psr[pb]])
                P.emit("dve", lambda v, pb=pb, tb=tb: v.tensor_tensor(out=gv3[:, ci, tb * 512:(tb + 1) * 512], in0=PS[pb][:, 0:512],
                                                                    in1=gact[par][:, tb * 512:(tb + 1) * 512], op=ALU.mult),
                       [psr[pb], gact_reg[par]], [gv_reg[g % 2]])
            if ci == 3 or cc == NCC - 1:
                ncg = ci + 1
                for j in range(16):
                    for tb in range(2):
                        pb = 5 + (j * 2 + tb) % 3
                        P.mm(PS[pb][:, 0:512], [(wdn3[:, i, j * 128:(j + 1) * 128], gv3[:, i, tb * 512:(tb + 1) * 512]) for i in range(ncg)],
                             [wb_reg[2], wb_reg[3], gv_reg[g % 2]], [psr[pb]])
                        P.emit("dve", lambda v, pb=pb, j=j, tb=tb: v.scalar_tensor_tensor(
                            out=x1T3[:, j, tb * 512:(tb + 1) * 512], in0=PS[pb][:, 0:512], scalar=g2[:, j:j + 1],
                            in1=x1T3[:, j, tb * 512:(tb + 1) * 512], op0=ALU.mult, op1=ALU.add), [psr[pb], x1_reg[j], mod_reg2], [x1_reg[j]])
        ffn_a(0)
        for cc in range(NCC):
            if cc + 1 < NCC:
                ffn_a(cc + 1)
            ffn_v(cc)

        P.barrier()
        hA.reset(0)
        sqb = hA.bf16(16 * 512)
        rbc = hA.f32(512)
        otile = [GA_t[:, 0:2048], hA.f32(2048)]
        otile_reg = [Region("otile%d" % i) for i in range(2)]
        oT3 = _r3(hxb[:, 0:8192], KC, 512)
        oT_reg = Region("oT")

        def fin_block(blk):
            c0 = blk * 512
            ssq_bcast(c0, 512, blk)
            for k in range(KC):
                P.emit("dve", lambda v, k=k: v.scalar_tensor_tensor(out=oT3[:, k, :], in0=x1T3[:, k, c0:c0 + 512], scalar=fgv[:, k:k + 1], in1=rbc[:, 0:512],
                                                                   op0=ALU.mult, op1=ALU.mult), [x1_reg[k], consts_reg, rbc_reg], [oT_reg])
            for tt in range(4):
                tile = blk * 4 + tt
                par = tile % 2
                for q in range(4):
                    pb = 2 + (tile * 4 + q) % 6
                    for kk in range(4):
                        P.transpose(PS[pb][:, kk * 128:(kk + 1) * 128], oT3[:, q * 4 + kk, tt * 128:(tt + 1) * 128], ident_f[:, :],
                                    [oT_reg, consts_reg], [psr[pb]])
                    dst = otile[par][:, q * 512:(q + 1) * 512]
                    if q % 2 == 0:
                        P.emit("act", lambda sc, dst=dst, pb=pb: sc.activation(out=dst, in_=PS[pb][:, 0:512], func=AF.Copy), [psr[pb]], [otile_reg[par]])
                    else:
                        P.emit("dve", lambda v, dst=dst, pb=pb: v.tensor_copy(dst, PS[pb][:, 0:512]), [psr[pb]], [otile_reg[par]])
                P.dma("sp", "out", out_d[tile * 128:(tile + 1) * 128, :], otile[par], reads=[otile_reg[par]])
        for blk in range(2):
            fin_block(blk)
        finish()
    return nc


def _tile_k(w):
    n = w.shape[1]
    return np.ascontiguousarray(w.reshape(KC, 128, n).transpose(1, 0, 2)).reshape(128, KC * n)


def _fm(v, nchunk):
    return np.ascontiguousarray(v.reshape(nchunk, 128).T)


def prep_shared(inp, s):
    w_in = inp["w_in"][0]
    o = {}
    RQ, RK, RV, RG = 0, 2048, 4096, 6144
    GQ, GK, GV, GGT = 8192, 9216, 10240, 12288
    GAF, GAB, MR, MG = 14336, 14352, 14368, 16416
    wr = np.empty((RH * 4, 128, KC * 256), np.float32)
    for h in range(RH):
        for i, base in enumerate((RQ, RK, RV, RG)):
            wr[h * 4 + i] = _tile_k(w_in[:, base + h * 256: base + (h + 1) * 256])
    o["w_ret"] = wr
    wg = np.empty((GH * 6, 128, KC * 256), np.float32)
    for h in range(GH):
        wg[h * 6 + 0] = _tile_k(w_in[:, GQ + h * 256: GQ + (h + 1) * 256])
        wg[h * 6 + 1] = _tile_k(w_in[:, GK + h * 256: GK + (h + 1) * 256])
        wg[h * 6 + 2] = _tile_k(w_in[:, GV + h * 512: GV + h * 512 + 256])
        wg[h * 6 + 3] = _tile_k(w_in[:, GV + h * 512 + 256: GV + (h + 1) * 512])
        wg[h * 6 + 4] = _tile_k(w_in[:, GGT + h * 512: GGT + h * 512 + 256])
        wg[h * 6 + 5] = _tile_k(w_in[:, GGT + h * 512 + 256: GGT + (h + 1) * 512])
    o["w_gla"] = wg
    lr = (w_in[:, GAF:GAF + 16], w_in[:, GAB:GAB + 16])
    if s == 1:
        lr = lr[::-1]
    o["w_lr"] = _tile_k(np.concatenate(lr, axis=1))
    wmg = np.empty((16, 128, KC * 256), np.float32)
    for j in range(16):
        wmg[j] = _tile_k(np.concatenate([w_in[:, MR + j * 128: MR + (j + 1) * 128], w_in[:, MG + j * 128: MG + (j + 1) * 128]], axis=1))
    o["w_mg"] = wmg
    for name, key in (("w_rp", "w_ret_proj"), ("w_gp", "w_gla_proj"), ("w_o", "w_out")):
        w = inp[key][0]
        o[name] = np.stack([_tile_k(w[:, j * 128:(j + 1) * 128]) for j in range(16)])
    w_up = inp["w_up"][0]
    o["w_up"] = np.stack([_tile_k(np.concatenate([w_up[:, cc * 128:(cc + 1) * 128], w_up[:, DFF + cc * 128: DFF + (cc + 1) * 128]], axis=1))
                          for cc in range(NCC)])
    o["w_dn"] = np.ascontiguousarray(inp["w_down"][0].reshape(NCC, 128, D))
    w_ada = inp["w_ada"][0]
    wa = np.empty((24, 128, KC * 512), np.float32)
    for g in range(24):
        blk = w_ada[:, g * 512:(g + 1) * 512].reshape(KC, 128, 4, 128).transpose(1, 2, 0, 3)
        wa[g] = np.ascontiguousarray(blk).reshape(128, KC * 512)
    o["wada"] = wa
    o["vecs"] = np.concatenate([_fm(inp["b_ada"][0], 96), _fm(inp["norm1_g"][0], 16), _fm(inp["norm2_g"][0], 16),
                                _fm(inp["final_g"], 16)], axis=1).astype(np.float32)
    cw = inp["conv_w"][0]
    if s == 1:
        cw = cw[::-1, ::-1]
    cwt = np.ascontiguousarray(cw.reshape(9, NCC, 128).transpose(2, 1, 0)).reshape(128, NCC * 9)
    o["convv"] = np.concatenate([cwt, _fm(inp["conv_b"][0], NCC)], axis=1).astype(np.float32)
    dl = inp["ret_decay_logit"][0]
    gu = inp["gla_gate_up"][0]
    gb = inp["gla_gate_bias"][0]
    order = (0, 1) if s == 0 else (1, 0)
    o["dlog"] = np.concatenate([dl[order[0]], dl[order[1]]])[None, :].astype(np.float32)
    o["gup"] = np.concatenate([np.concatenate([gu[d], gb[d][None, :]], axis=0) for d in order], axis=1).astype(np.float32)
    o["rng"] = inp["ret_norm_g"][0][None, :].astype(np.float32)
    o["gng"] = inp["gla_norm_g"][0][None, :].astype(np.float32)
    ident = np.eye(128, dtype=np.float32)
    idx = np.arange(128)
    mf = (idx[None, :] >= idx[:, None]).astype(np.float32)
    mb = (idx[None, :] <= idx[:, None]).astype(np.float32)
    io = np.stack([idx + 1.0, 128.0 - idx, 127.0 - idx, idx * 1.0]).astype(np.float32)
    iot = np.broadcast_to(io.reshape(1, 512), (128, 512))
    o["consts"] = np.concatenate([ident, mf, mb, iot], axis=1).astype(np.float32)
    pos = np.arange(L) if s == 0 else np.arange(L - 1, -1, -1)
    row = (pos // 64).astype(np.float32)
    col = (pos % 64).astype(np.float32)
    inv = (10000.0 ** (-np.arange(64, dtype=np.float32) / 64)).astype(np.float32)
    ang = np.concatenate([row[:, None] * inv, col[:, None] * inv], axis=-1)
    o["rope"] = np.concatenate([np.cos(ang).T, np.sin(ang).T], axis=1).astype(np.float32)
    return o


def make_in_maps(inp):
    shared = [prep_shared(inp, 0), prep_shared(inp, 1)]
    maps = []
    for core in range(8):
        b, s = core // 2, core % 2
        m = dict(shared[s])
        xb = inp["x"][b]
        cb = inp["ctx"][b]
        m["x"] = np.ascontiguousarray(xb if s == 0 else xb[::-1])
        m["ctx"] = np.ascontiguousarray(cb if s == 0 else cb[::-1])
        cvv = np.stack([inp["c"][b].reshape(KC, 128).T, inp["c_ctx"].reshape(KC, 128).T], axis=2)
        m["cv"] = np.ascontiguousarray(cvv).reshape(128, KC * 2).astype(np.float32)
        maps.append(m)
    return maps


_NC_CACHE = {}


def kernel(**inputs):
    inp = {k: np.asarray(v) for k, v in inputs.items()}
    if "nc" not in _NC_CACHE:
        _NC_CACHE["nc"] = build_program()
    nc = _NC_CACHE["nc"]
    maps = make_in_maps(inp)
    res = run_bass_kernel_spmd(nc, maps, core_ids=list(range(8)))
    out = np.empty((4, L, D), np.float32)
    for core in range(8):
        b, s = core // 2, core % 2
        o = res.results[core]["out"]
        if s == 0:
            out[b, 0:1024] = o
        else:
            out[b, 1024:2048] = o[::-1]
    return out
```
